# Optimizing a Trainium2 kernel written in Bass

```python
import math
import jax, jax.numpy as jnp
from jax import lax
import numpy as np

D_MODEL = 1024
BATCH = 8
SEQ = 2048
DEPTH = 1
DEC_BATCH = 128
DEC_SEQ = 4
PAST_LEN = 16384
PAGE_SIZE = 128

RWKV_HEADS = 8
RWKV_HEAD_DIM = 64
RWKV_WIDTH = RWKV_HEADS * RWKV_HEAD_DIM
D_DECAY_LORA = 64
D_AAA_LORA = 64
D_GATE_LORA = 128
LNX_EPS = 64e-5
HGRN_HEADS = 4
HGRN_EXPAND = 128
HGRN_HEAD_DIM = 128
HGRN_FDIM = HGRN_HEADS * HGRN_EXPAND
HGRN_WIDTH = HGRN_HEADS * HGRN_HEAD_DIM
HGRN_CHUNK = 16
MIX_WIDTH = RWKV_WIDTH + HGRN_WIDTH
SHIFT_WIDTH = 3 * RWKV_WIDTH + D_DECAY_LORA + D_AAA_LORA + D_GATE_LORA
IN_WIDTH = SHIFT_WIDTH + 2 * HGRN_FDIM + 2 * HGRN_WIDTH
D_FF = 4 * D_MODEL
NORM_EPS = 1e-6

kernel_name = 'rwkv7_hgrn2_parallel_hybrid_step'


def _rmsnorm(x, gain):
    x32 = x.astype(jnp.float32)
    y = x32 * lax.rsqrt(jnp.mean(x32 * x32, axis=-1, keepdims=True) + NORM_EPS)
    return (y * gain.astype(jnp.float32)).astype(x.dtype)


def _wkv7_scan(r, decay, k, v, kk, a, s0):
    def step(s, inp):
        r_t, w_t, k_t, v_t, kk_t, a_t = inp
        sa = jnp.einsum('bhvk,bhk->bhv', s, -kk_t)
        s = (s * w_t[:, :, None, :] + sa[..., None] * (kk_t * a_t)[:, :, None, :]
             + v_t[..., None] * k_t[:, :, None, :])
        o = jnp.einsum('bhvk,bhk->bhv', s, r_t)
        return s, o
    xs = tuple(jnp.swapaxes(t, 0, 1) for t in (r, decay, k, v, kk, a))
    s_final, o = lax.scan(step, s0, xs)
    return jnp.swapaxes(o, 0, 1), s_final


def _hgrn2_chunked(q, k, log_f, v, s0):
    B, T, H, F = q.shape
    I = v.shape[-1]
    C = math.gcd(T, HGRN_CHUNK)
    n = T // C

    def to_chunks(t):
        return t.reshape(B, n, C, H, t.shape[-1]).transpose(1, 0, 3, 2, 4)

    causal = jnp.tril(jnp.ones((C, C), dtype=bool))

    def step(s, inp):
        qc, kc, gc, vc = inp
        b = jnp.cumsum(gc, axis=2)
        o_inter = jnp.einsum('bhtf,bhfi->bhti', qc * jnp.exp(b), s)
        diff = b[:, :, :, None, :] - b[:, :, None, :, :]
        dec = jnp.where(causal[:, :, None], jnp.exp(jnp.minimum(diff, 0.0)), 0.0)
        att = jnp.einsum('bhtf,bhtsf,bhsf->bhts', qc, dec, kc)
        o_intra = jnp.einsum('bhts,bhsi->bhti', att, vc)
        b_last = b[:, :, -1:, :]
        s = (jnp.exp(b_last[:, :, 0, :])[..., None] * s
             + jnp.einsum('bhsf,bhsi->bhfi', kc * jnp.exp(b_last - b), vc))
        return s, o_inter + o_intra

    s_final, o = lax.scan(step, s0, tuple(to_chunks(t) for t in (q, k, log_f, v)))
    o = o.transpose(1, 0, 3, 2, 4).reshape(B, T, H, I)
    return o, s_final


def _token_mixers(h, shift_prev, wkv_prev, hgrn_prev, l, w):
    B, T, _ = h.shape
    dt = h.dtype
    f32 = jnp.float32
    p = h @ w['w_in'][l]
    p_rw, p_hg = p[..., :SHIFT_WIDTH], p[..., SHIFT_WIDTH:]
    p_prev = jnp.concatenate([shift_prev[:, None, :].astype(dt), p_rw[:, :-1]], axis=1)
    x_rw = p_rw + (p_prev - p_rw) * w['mu_shift'][l]
    new_shift = p_rw[:, -1]
    o1 = RWKV_WIDTH
    o2 = 2 * RWKV_WIDTH
    o3 = 3 * RWKV_WIDTH
    o4 = o3 + D_DECAY_LORA
    o5 = o4 + D_AAA_LORA
    r, k, v, wd, ad, gd = jnp.split(x_rw, [o1, o2, o3, o4, o5], axis=-1)
    r, k, v = r.astype(f32), k.astype(f32), v.astype(f32)
    w_log = -jax.nn.softplus(-(w['w0'][l] + jnp.tanh(wd) @ w['w_decay_up'][l]).astype(f32)) - 0.5
    decay = jnp.exp(-jnp.exp(w_log))
    a = jax.nn.sigmoid((w['a0'][l] + ad @ w['w_aaa_up'][l]).astype(f32))
    g = (jax.nn.sigmoid(gd) @ w['w_gate_up'][l]).astype(f32)

    def heads(t):
        return t.reshape(B, T, RWKV_HEADS, RWKV_HEAD_DIM)

    kk = heads(k * w['k_k'][l])
    kk = kk / jnp.maximum(jnp.sqrt(jnp.sum(kk * kk, axis=-1, keepdims=True)), 1e-12)
    k = k * (1.0 + (a - 1.0) * w['k_a'][l])
    rh, kh, vh, ah, dh = heads(r), heads(k), heads(v), heads(a), heads(decay)
    o, wkv_new = _wkv7_scan(rh, dh, kh, vh, kk, ah, wkv_prev.astype(f32))
    mean = jnp.mean(o, axis=-1, keepdims=True)
    var = jnp.mean(jnp.square(o - mean), axis=-1, keepdims=True)
    o = ((o - mean) * lax.rsqrt(var + LNX_EPS)).reshape(B, T, RWKV_WIDTH) * w['lnx_w'][l] + w['lnx_b'][l]
    bonus = jnp.sum(rh * kh * w['r_k'][l], axis=-1, keepdims=True) * vh
    y_rw = (o + bonus.reshape(B, T, RWKV_WIDTH)) * g

    q, f_raw, i_in, og = jnp.split(p_hg, [HGRN_FDIM, 2 * HGRN_FDIM, 2 * HGRN_FDIM + HGRN_WIDTH], axis=-1)
    lb = jnp.cumsum(jax.nn.softmax(w['hgrn_lb'].astype(f32), axis=0), axis=0)[l]
    f = lb + (1.0 - lb) * jax.nn.sigmoid(f_raw.astype(f32))
    qh = jax.nn.silu(q.astype(f32)).reshape(B, T, HGRN_HEADS, HGRN_EXPAND)
    kf = (1.0 - f).reshape(B, T, HGRN_HEADS, HGRN_EXPAND)
    gf = jnp.log(f).reshape(B, T, HGRN_HEADS, HGRN_EXPAND)
    ih = i_in.astype(f32).reshape(B, T, HGRN_HEADS, HGRN_HEAD_DIM)
    o_h, hgrn_new = _hgrn2_chunked(qh, kf, gf, ih, hgrn_prev.astype(f32))
    o_h = o_h * lax.rsqrt(jnp.mean(o_h * o_h, axis=-1, keepdims=True) + NORM_EPS) * w['hgrn_gnorm'][l]
    y_hg = o_h.reshape(B, T, HGRN_WIDTH) * jax.nn.silu(og.astype(f32))

    y = jnp.concatenate([y_rw, y_hg], axis=-1).astype(dt) @ w['w_out'][l]
    return y, new_shift.astype(shift_prev.dtype), wkv_new.astype(wkv_prev.dtype), hgrn_new.astype(hgrn_prev.dtype)


def _trunk(x, c, shift0, wkv0, hgrn0, w):
    shifts, wkvs, hgrns = [], [], []
    for l in range(DEPTH):
        mod = jax.nn.silu(c) @ w['w_ada'][l] + w['b_ada'][l]
        sh1, sc1, gt1, sh2, sc2, gt2 = jnp.split(mod, 6, axis=-1)
        h = _rmsnorm(x, w['norm1'][l]) * (1.0 + sc1[:, None]) + sh1[:, None]
        y, s_sh, s_wkv, s_hg = _token_mixers(h, shift0[l], wkv0[l], hgrn0[l], l, w)
        x = x + gt1[:, None] * y
        h = _rmsnorm(x, w['norm2'][l]) * (1.0 + sc2[:, None]) + sh2[:, None]
        x = x + gt2[:, None] * (jnp.square(jax.nn.relu(h @ w['w_up'][l])) @ w['w_down'][l])
        shifts.append(s_sh)
        wkvs.append(s_wkv)
        hgrns.append(s_hg)
    return _rmsnorm(x, w['norm_f']), jnp.stack(shifts), jnp.stack(wkvs), jnp.stack(hgrns)


def setup_inputs(seed: int = 0) -> dict:
    key = jax.random.key(seed)
    ks = iter(jax.random.split(key, 32))

    def nrm(shape, scale):
        return jax.random.normal(next(ks), shape, jnp.float32) * scale

    L = DEPTH
    return {
        'x_prompt': nrm((BATCH, SEQ, D_MODEL), 1.0),
        'x_sample': nrm((DEC_BATCH, DEC_SEQ, D_MODEL), 1.0),
        'c_prompt': nrm((BATCH, D_MODEL), 1.0),
        'c_sample': nrm((DEC_BATCH, D_MODEL), 1.0),
        'state_shift': nrm((L, DEC_BATCH, SHIFT_WIDTH), 1.0),
        'state_wkv': nrm((L, DEC_BATCH, RWKV_HEADS, RWKV_HEAD_DIM, RWKV_HEAD_DIM), 0.3),
        'state_hgrn': nrm((L, DEC_BATCH, HGRN_HEADS, HGRN_EXPAND, HGRN_HEAD_DIM), 0.3),
        'norm1': 1.0 + nrm((L, D_MODEL), 0.05),
        'norm2': 1.0 + nrm((L, D_MODEL), 0.05),
        'norm_f': 1.0 + nrm((D_MODEL,), 0.05),
        'w_ada': nrm((L, D_MODEL, 6 * D_MODEL), D_MODEL ** -0.5),
        'b_ada': nrm((L, 6 * D_MODEL), 0.01),
        'w_in': nrm((L, D_MODEL, IN_WIDTH), D_MODEL ** -0.5),
        'mu_shift': jax.random.uniform(next(ks), (L, SHIFT_WIDTH), jnp.float32),
        'w0': jnp.linspace(-6.5, -0.5, RWKV_WIDTH, dtype=jnp.float32)[None, :] + nrm((L, RWKV_WIDTH), 0.1),
        'w_decay_up': nrm((L, D_DECAY_LORA, RWKV_WIDTH), 0.5 * D_DECAY_LORA ** -0.5),
        'a0': nrm((L, RWKV_WIDTH), 0.1),
        'w_aaa_up': nrm((L, D_AAA_LORA, RWKV_WIDTH), 0.5 * D_AAA_LORA ** -0.5),
        'w_gate_up': nrm((L, D_GATE_LORA, RWKV_WIDTH), D_GATE_LORA ** -0.5),
        'k_k': 0.85 + nrm((L, RWKV_WIDTH), 0.05),
        'k_a': 1.0 + nrm((L, RWKV_WIDTH), 0.05),
        'r_k': nrm((L, RWKV_HEADS, RWKV_HEAD_DIM), 0.1),
        'lnx_w': 1.0 + nrm((L, RWKV_WIDTH), 0.05),
        'lnx_b': nrm((L, RWKV_WIDTH), 0.01),
        'hgrn_lb': nrm((L + 1, HGRN_FDIM), 0.1),
        'hgrn_gnorm': 1.0 + nrm((L, HGRN_HEAD_DIM), 0.05),
        'w_out': nrm((L, MIX_WIDTH, D_MODEL), MIX_WIDTH ** -0.5),
        'w_up': nrm((L, D_MODEL, D_FF), D_MODEL ** -0.5),
        'w_down': nrm((L, D_FF, D_MODEL), D_FF ** -0.5),
    }


def reference(x_prompt, x_sample, c_prompt, c_sample, state_shift, state_wkv, state_hgrn,
              norm1, norm2, norm_f, w_ada, b_ada, w_in, mu_shift, w0, w_decay_up, a0, w_aaa_up,
              w_gate_up, k_k, k_a, r_k, lnx_w, lnx_b, hgrn_lb, hgrn_gnorm, w_out, w_up, w_down):
    w = {'norm1': norm1, 'norm2': norm2, 'norm_f': norm_f, 'w_ada': w_ada, 'b_ada': b_ada,
         'w_in': w_in, 'mu_shift': mu_shift, 'w0': w0, 'w_decay_up': w_decay_up, 'a0': a0,
         'w_aaa_up': w_aaa_up, 'w_gate_up': w_gate_up, 'k_k': k_k, 'k_a': k_a, 'r_k': r_k,
         'lnx_w': lnx_w, 'lnx_b': lnx_b, 'hgrn_lb': hgrn_lb, 'hgrn_gnorm': hgrn_gnorm,
         'w_out': w_out, 'w_up': w_up, 'w_down': w_down}
    bp = x_prompt.shape[0]
    dt = x_prompt.dtype
    shift0 = jnp.zeros((DEPTH, bp, SHIFT_WIDTH), dt)
    wkv0 = jnp.zeros((DEPTH, bp, RWKV_HEADS, RWKV_HEAD_DIM, RWKV_HEAD_DIM), dt)
    hgrn0 = jnp.zeros((DEPTH, bp, HGRN_HEADS, HGRN_EXPAND, HGRN_HEAD_DIM), dt)
    y_prompt, shift_p, wkv_p, hgrn_p = _trunk(x_prompt, c_prompt, shift0, wkv0, hgrn0, w)
    y_sample, shift_s, wkv_s, hgrn_s = _trunk(x_sample, c_sample, state_shift, state_wkv, state_hgrn, w)
    return (y_prompt, y_sample, shift_p, wkv_p, hgrn_p, shift_s, wkv_s, hgrn_s)
```

```python
import contextlib
import numpy as np
import concourse.bass as bass
import concourse.mybir as mybir
from concourse.bass_utils import run_bass_kernel_spmd

F32 = mybir.dt.float32
BF16 = mybir.dt.bfloat16
AF = mybir.ActivationFunctionType
ALU = mybir.AluOpType

D = 1024
NCORES = 8
SEQ = 2048
SB = 16
ST = 4
DFF = 4096
INW = 3840
SHW = 1792
WDEC = 0.6065306597126334
NORM_EPS = 1e-6
LNX_EPS = 64e-5

CH_R, CH_K, CH_V, CH_WA, CH_G, CH_Q, CH_F, CH_I, CH_OG = 0, 4, 8, 12, 13, 14, 18, 22, 26
INPROJ_ORDER = [12, 13, 0, 4, 8, 14, 18, 22, 26, 1, 5, 9, 15, 19, 23, 27,
                2, 6, 10, 16, 20, 24, 28, 3, 7, 11, 17, 21, 25, 29]

VC = {}
_off = 0
for _n, _c in [("norm1", 8), ("norm2", 8), ("b_ada", 48), ("mu", 14), ("w0", 4), ("a0", 4), ("k_k", 4),
               ("k_a", 4), ("r_k", 4), ("lnx_w", 4), ("lnx_b", 4), ("lb0", 4), ("lb1", 4), ("gnorm", 1)]:
    VC[_n] = _off
    _off += _c
NVC = _off


class Buf:
    def __init__(self, name, t):
        self.name = name
        self.w = None
        self.r = {}
        self.aliases = []
        self.set_t(t)

    def set_t(self, t):
        self.t = t
        if t is None:
            return
        if hasattr(t, "offset") and hasattr(t, "ap"):
            self.th = t.tensor
            self.pstride = int(t.ap[0][0])
            self.base = int(t.offset)
        else:
            self.th = t.tensor if hasattr(t, "tensor") else t
            self.pstride = int(np.prod(list(t.shape)[1:]))
            self.base = 0

    def __getitem__(self, idx):
        return View(self, self.t[idx])

    def ap(self, p0, npart, off, dims):
        return View(self, bass.AP(self.th, self.base + p0 * self.pstride + off,
                                  [[self.pstride, npart]] + [list(d) for d in dims]))


class View:
    def __init__(self, buf, ap):
        self.buf = buf
        self.ap = ap

    def re(self, pat, **kw):
        return View(self.buf, self.ap.rearrange(pat, **kw))

    def bc(self, shape):
        return View(self.buf, self.ap.to_broadcast(shape))

    def __getitem__(self, idx):
        return View(self.buf, self.ap[idx])


class Eng:
    def __init__(self, name, sem):
        self.name = name
        self.sem = sem
        self.count = 0
        self.waited = {}
        self.ops = []


class DSem:
    def __init__(self, sem):
        self.sem = sem
        self.total = 0


class Sched:
    def __init__(self, nc, stack):
        self.nc = nc
        self.E = {}
        for n in ["pe", "act", "dve", "pool", "sp"]:
            self.E[n] = Eng(n, stack.enter_context(nc.semaphore("s_" + n)))
        self.dsems = [DSem(stack.enter_context(nc.semaphore("d%d" % i))) for i in range(32)]
        self.drr = 0
        self.nops = 0
        self.npe = 0
        self.marks = []

    def _collect(self, E, reads, writes, extra=()):
        deps = {}

        def need(s, v):
            if v > deps.get(id(s), (s, 0))[1]:
                deps[id(s)] = (s, v)

        for b0 in reads:
            for b in [b0] + b0.aliases:
                if b.w is not None:
                    need(*b.w)
            if getattr(b0, "excl", False):
                for s, v in b0.r.values():
                    if s is not E.sem:
                        need(s, v)
        for b0 in writes:
            for b in [b0] + b0.aliases:
                if b.w is not None:
                    need(*b.w)
                for s, v in b.r.values():
                    need(s, v)
        for s, v in extra:
            need(s, v)
        waits = []
        for s, v in deps.values():
            if E.name == "pe" and s is E.sem:
                continue
            if E.waited.get(id(s), 0) < v:
                E.waited[id(s)] = v
                waits.append((s, v))
        return waits

    @staticmethod
    def _update(tok, reads, writes):
        s, v = tok
        for b in reads:
            if b.r.get(id(s), (s, 0))[1] < v:
                b.r[id(s)] = (s, v)
        for b in writes:
            b.w = tok
            b.r = {}

    def emit(self, en, fn, reads=(), writes=(), signal=True):
        E = self.E[en]
        reads = [v.buf if isinstance(v, View) else v for v in reads]
        writes = [v.buf if isinstance(v, View) else v for v in writes]
        waits = self._collect(E, reads, writes)
        if signal:
            E.count += 1
            tokv = E.count
        else:
            tokv = E.count + 1
        E.ops.append((waits, fn, ("sig", E.sem) if signal else None))
        self._update((E.sem, tokv), reads, writes)
        self.nops += 1

    def dma(self, en, out, in_, reads=(), writes=(), **kw):
        E = self.E[en]
        reads = [v.buf if isinstance(v, View) else v for v in reads]
        writes = [v.buf if isinstance(v, View) else v for v in writes]
        ds = self.dsems[self.drr % len(self.dsems)]
        self.drr += 1
        waits = self._collect(E, reads, writes, extra=[(ds.sem, ds.total)] if ds.total else [])
        ds.total += 16
        E.ops.append((waits, lambda h: h.dma_start(out=out, in_=in_, **kw), ("dma", ds.sem)))
        self._update((ds.sem, ds.total), reads, writes)
        self.nops += 1

    def barrier(self):
        for E in self.E.values():
            waits = []
            for O in self.E.values():
                if O is E or O.count == 0:
                    continue
                if E.waited.get(id(O.sem), 0) < O.count:
                    E.waited[id(O.sem)] = O.count
                    waits.append((O.sem, O.count))
            for ds in self.dsems:
                if ds.total and E.waited.get(id(ds.sem), 0) < ds.total:
                    E.waited[id(ds.sem)] = ds.total
                    waits.append((ds.sem, ds.total))
            if waits:
                E.ops.append((waits, None, None))

    def flush(self):
        nc = self.nc
        hmap = {"pe": "tensor", "act": "scalar", "dve": "vector", "pool": "gpsimd", "sp": "sync"}
        with nc.Block() as block:
            for n, E in self.E.items():
                ops = E.ops

                def body(h, ops=ops):
                    for waits, fn, sig in ops:
                        for s, v in waits:
                            h.wait_ge(s, v)
                        if fn is None:
                            continue
                        ins = fn(h)
                        if sig is not None:
                            if sig[0] == "sig":
                                ins.then_inc(sig[1], 1)
                            else:
                                ins.then_inc(sig[1], 16)

                getattr(block, hmap[n])(body)
        for E in self.E.values():
            E.ops = []

    def mark(self, name):
        self.marks.append((name, self.npe))

    def mm(self, out, lhsT, rhs, start=True, stop=True, signal=None):
        self.npe += 1
        if signal is None:
            signal = stop
        self.emit("pe", lambda e: e.matmul(out.ap, lhsT.ap, rhs.ap, start=start, stop=stop),
                  reads=[lhsT, rhs], writes=[out], signal=signal)

    def act(self, out, in_, func, bias=None, scale=None, accum=None, eng="act"):
        kw = {}
        rd = [in_]
        wr = [out]
        if bias is not None:
            if isinstance(bias, View):
                kw["bias"] = bias.ap
                rd.append(bias)
            else:
                kw["bias"] = bias
        if scale is not None:
            if isinstance(scale, View):
                kw["scale"] = scale.ap
                rd.append(scale)
            else:
                kw["scale"] = scale
        if accum is not None:
            kw["accum_out"] = accum.ap
            wr.append(accum)
        self.emit(eng, lambda e: e.activation(out=out.ap, in_=in_.ap, func=func, **kw), reads=rd, writes=wr)

    def tt(self, out, in0, in1, op, eng="dve"):
        self.emit(eng, lambda e: e.tensor_tensor(out=out.ap, in0=in0.ap, in1=in1.ap, op=op),
                  reads=[in0, in1], writes=[out])

    def ts(self, out, in0, s1, op0, s2=None, op1=None, eng="dve"):
        rd = [in0]
        a1 = s1
        a2 = s2
        if isinstance(s1, View):
            rd.append(s1)
            a1 = s1.ap
        if isinstance(s2, View):
            rd.append(s2)
            a2 = s2.ap
        if op1 is None:
            self.emit(eng, lambda e: e.tensor_scalar(out=out.ap, in0=in0.ap, scalar1=a1, scalar2=None, op0=op0),
                      reads=rd, writes=[out])
        else:
            self.emit(eng, lambda e: e.tensor_scalar(out=out.ap, in0=in0.ap, scalar1=a1, scalar2=a2, op0=op0, op1=op1),
                      reads=rd, writes=[out])

    def stt(self, out, in0, s, in1, op0, op1):
        rd = [in0, in1]
        a = s
        if isinstance(s, View):
            rd.append(s)
            a = s.ap
        self.emit("dve", lambda e: e.scalar_tensor_tensor(out=out.ap, in0=in0.ap, scalar=a, in1=in1.ap, op0=op0, op1=op1),
                  reads=rd, writes=[out])

    def copy(self, out, in_, eng="dve"):
        if eng == "act":
            self.emit("act", lambda e: e.activation(out=out.ap, in_=in_.ap, func=AF.Identity), reads=[in_], writes=[out])
        else:
            self.emit(eng, lambda e: e.tensor_copy(out=out.ap, in_=in_.ap), reads=[in_], writes=[out])

    def memset(self, out, val, eng="pool"):
        self.emit(eng, lambda e: e.memset(out.ap, val), reads=[], writes=[out])

    def recip(self, out, in_):
        self.emit("dve", lambda e: e.reciprocal(out=out.ap, in_=in_.ap), reads=[in_], writes=[out])

    def scan(self, out, d0, d1, init, op0, op1):
        self.emit("dve", lambda e: e.tensor_tensor_scan(out=out.ap, data0=d0.ap, data1=d1.ap, initial=init, op0=op0, op1=op1),
                  reads=[d0, d1], writes=[out])


NCST = 1296
C_ID, C_IDS, C_BONE, C_BMEAN, C_AMEAN = 0, 128, 192, 320, 448
C_M1, C_MS = 576, 704
C_M1S, C_MSS = 768, 776
C_ONES = 780
C_M2H = 912
C_BD = 1040
C_N0, C_N0T, C_N1 = 1104, 1168, 1232


def make_consts():
    c = np.zeros((128, NCST), np.float32)
    c[:, C_ID:C_ID + 128] = np.eye(128)
    c[0:64, C_IDS:C_IDS + 64] = np.eye(64)
    c[64:128, C_IDS:C_IDS + 64] = np.eye(64)
    blk = np.kron(np.eye(2), np.ones((64, 64)))
    c[:, C_BONE:C_BONE + 128] = blk
    c[:, C_BMEAN:C_BMEAN + 128] = blk / 64.0
    c[:, C_AMEAN:C_AMEAN + 128] = 1.0 / 128.0
    s = np.arange(128)[:, None] % 64
    t = np.arange(64)[None, :]
    c[:, C_M1:C_M1 + 64] = (s < t)
    c[:, C_M1 + 64:C_M1 + 128] = (s <= t)
    c[:, C_MS:C_MS + 64] = (t < s)
    s4 = np.arange(128)[:, None] % 32
    t4 = np.arange(4)[None, :]
    ok = (s4 < 4)
    c[:, C_M1S:C_M1S + 4] = (s4 < t4) & ok
    c[:, C_M1S + 4:C_M1S + 8] = (s4 <= t4) & ok
    c[:, C_MSS:C_MSS + 4] = (t4 < s4) & ok
    c[:, C_ONES:C_ONES + 128] = 1.0
    r = np.arange(128)[:, None] % 64
    q = np.arange(64)[None, :]
    same16 = (r // 16) == (q // 16)
    same32 = (r // 32) == (q // 32)
    c[:, C_M2H:C_M2H + 64] = (r < q) & same16
    c[:, C_M2H + 64:C_M2H + 128] = (r <= q)
    c[:, C_BD:C_BD + 64] = (q < r) & same16
    c[:, C_N0:C_N0 + 64] = (q < r) & same32 & ~same16
    c[:, C_N0T:C_N0T + 64] = (r < q) & same32 & ~same16
    c[:, C_N1:C_N1 + 64] = (r >= 32) & (q < 32)
    return c


class PsBank:
    def __init__(self, buf, col):
        self.buf = buf
        self.col = col

    def ap(self, p0, npart, off, dims):
        return View(self.buf, bass.AP(self.buf.th, p0 * 4096 + self.col + off, [[4096, npart]] + [list(d) for d in dims]))

    def full(self, npart=128, n=512):
        return self.ap(0, npart, 0, [(1, n)])

    def done(self):
        self.owner.ps_open.discard(self.idx)


class Builder:
    def __init__(self, stage=99, debug=False, lite=False):
        self.stage = stage
        self.debug = debug
        self.lite = lite
        self.dbg_col = 0

    def dram_in(self, name, shape, dt=F32):
        return self.nc.dram_tensor(name, list(shape), dt, kind="ExternalInput").ap()

    def dram_out(self, name, shape, dt=F32):
        return self.nc.dram_tensor(name, list(shape), dt, kind="ExternalOutput").ap()

    def sb(self, stack, name, shape, dt=F32):
        t = stack.enter_context(self.nc.sbuf_tensor("sb_" + name, list(shape), dt))
        return Buf(name, t)

    def dbg(self, view, ncols, npart=128):
        if not self.debug:
            return
        c0 = self.dbg_col
        self.dbg_col += ncols
        assert self.dbg_col <= 8192
        self.S.dma("sp", self.O["dbg"][0:npart, c0:c0 + ncols], view.ap, reads=[view])
        return c0

    def build(self):
        nc = bass.Bass("TRN2", target_bir_lowering=False)
        self.nc = nc
        I = {}
        I["xp"] = self.dram_in("xp", [SEQ, D])
        I["xs"] = self.dram_in("xs", [SB * ST, D])
        I["cT"] = self.dram_in("cT", [128, 8, 17])
        I["sst"] = self.dram_in("sst", [128, 14, SB])
        I["swkv"] = self.dram_in("swkv", [SB, 8, 64, 64])
        I["shg"] = self.dram_in("shg", [SB, 4, 128, 128])
        if self.lite:
            for n in ["w_ada", "w_in", "w_out", "w_up", "w_down"]:
                I[n] = self.dram_in(n, [8, 8])
        else:
            I["w_ada"] = self.dram_in("w_ada", [D, 6 * D])
            I["w_in"] = self.dram_in("w_in", [D, INW])
            I["w_out"] = self.dram_in("w_out", [D, D])
            I["w_up"] = self.dram_in("w_up", [D, DFF])
            I["w_down"] = self.dram_in("w_down", [DFF, D])
        I["w_dec"] = self.dram_in("w_dec", [64, 512])
        I["w_aaa"] = self.dram_in("w_aaa", [64, 512])
        I["w_gate"] = self.dram_in("w_gate", [128, 512])
        I["vecF"] = self.dram_in("vecF", [128, NVC])
        I["normf"] = self.dram_in("normf", [128, D])
        I["bgt"] = self.dram_in("bgt", [1, 2 * D])
        I["cst"] = self.dram_in("cst", [128, NCST])
        O = {}
        O["yp"] = self.dram_out("yp", [SEQ, D])
        O["ys"] = self.dram_out("ys", [SB * ST, D])
        O["shp"] = self.dram_out("shp", [14, 128])
        O["wkvp"] = self.dram_out("wkvp", [8, 64, 64])
        O["hgp"] = self.dram_out("hgp", [4, 128, 128])
        O["shs"] = self.dram_out("shs", [SB, 14, 128])
        O["wkvs"] = self.dram_out("wkvs", [SB, 8, 64, 64])
        O["hgs"] = self.dram_out("hgs", [SB, 4, 128, 128])
        if self.debug:
            O["dbg"] = self.dram_out("dbg", [128, 8192])
        self.I, self.O = I, O

        with contextlib.ExitStack() as stack:
            S = Sched(nc, stack)
            self.S = S
            self.setup_persistent(stack)
            if self.stage < 0.1:
                S.barrier()
                S.flush()
                return nc
            with contextlib.ExitStack() as st2:
                self.run_phase(st2, sample=True)
                S.barrier()
                S.flush()
            if self.stage >= 2:
                with contextlib.ExitStack() as st3:
                    self.run_phase(st3, sample=False)
                    S.barrier()
                    S.flush()
        return nc

    def ps(self, hold=False):
        for _ in range(8):
            i = self.ps_rr % 8
            self.ps_rr += 1
            if i not in self.ps_open:
                if hold:
                    self.ps_open.add(i)
                pb = PsBank(self.psum[i], i * 512)
                pb.owner = self
                pb.idx = i
                return pb
        raise RuntimeError("no free PSUM bank")

    def setup_persistent(self, stack):
        S, I = self.S, self.I
        sb = lambda n, s, d=F32: self.sb(stack, n, s, d)
        pst = stack.enter_context(self.nc.psum_tensor("psall", [128, 4096], F32))
        self.psum = []
        for i in range(8):
            b = Buf("ps%d" % i, None)
            b.t = pst
            b.th = pst.tensor if hasattr(pst, "tensor") else pst
            b.pstride = 4096
            b.base = 0
            b.excl = True
            self.psum.append(b)
        self.ps_rr = 0
        self.ps_open = set()
        self.vecF = sb("vecF", [128, NVC])
        self.cstf = sb("cstf", [128, 576])
        self.cstf_full = None
        self.cstb = sb("cstb", [128, NCST], BF16)
        self.normf = sb("normf", [128, D])
        self.bgt = sb("bgt", [4, 512], BF16)
        self.scT = sb("scT", [128, 8, 17], BF16)
        self.cTf = sb("cTf", [128, 8, 17])
        self.modT = sb("modT", [128, 32, 17])
        self.G1 = sb("G1", [128, 8, 17])
        self.G2 = sb("G2", [128, 8, 17])
        self.omm = sb("omm", [128, 14])
        self.epsc = sb("epsc", [128, 4])
        self.kk2 = sb("kk2", [128, 4])
        self.omka = sb("omka", [128, 4])
        self.lbv = sb("lbv", [128, 4])
        self.oml = sb("oml", [128, 4])
        self.wdec = sb("wdec", [128, 512], BF16)
        self.wgate = sb("wgate", [128, 512], BF16)
        self.ring = [sb("ring%d" % i, [128, 4096], BF16) for i in range(3)]
        self.tiles = []
        self.tile_issued = 0
        self.tile_got = 0
        S.dma("sp", self.vecF[:, :].ap, I["vecF"], writes=[self.vecF])
        S.dma("sp", self.cstf[:, :].ap, I["cst"][:, 0:576], writes=[self.cstf])
        S.dma("pool", self.cstb[:, :].ap, I["cst"], writes=[self.cstb])
        S.dma("sp", self.normf[:, :].ap, I["normf"], writes=[self.normf])
        S.dma("sp", self.cTf[:, :, :].ap, I["cT"], writes=[self.cTf])
        S.dma("pool", self.bgt[:, :].ap, I["bgt"].rearrange("o (g n) -> (o g) n", g=4), writes=[self.bgt])
        S.dma("pool", self.wdec[0:64, :].ap, I["w_dec"], writes=[self.wdec])
        S.dma("pool", self.wdec[64:128, :].ap, I["w_aaa"], writes=[self.wdec])
        S.dma("pool", self.wgate[:, :].ap, I["w_gate"], writes=[self.wgate])
        vf = self.vecF
        S.memset(self.epsc[:, 0:1], LNX_EPS)
        S.memset(self.epsc[:, 1:2], NORM_EPS)
        S.memset(self.epsc[:, 2:3], 1e-30)
        S.tt(self.kk2[:, :], vf[:, VC["k_k"]:VC["k_k"] + 4], vf[:, VC["k_k"]:VC["k_k"] + 4], ALU.mult)
        S.act(self.scT[:, :, :], self.cTf[:, :, :], AF.Silu)
        S.ts(self.omm[:, :], vf[:, VC["mu"]:VC["mu"] + 14], -1.0, ALU.mult, 1.0, ALU.add)
        S.ts(self.omka[:, :], vf[:, VC["k_a"]:VC["k_a"] + 4], -1.0, ALU.mult, 1.0, ALU.add)
        S.tt(self.oml[:, :], vf[:, VC["lb0"]:VC["lb0"] + 4], vf[:, VC["lb1"]:VC["lb1"] + 4], ALU.subtract)
        S.act(self.lbv[:, :], self.oml[:, :], AF.Sigmoid)
        S.ts(self.oml[:, :], self.lbv[:, :], -1.0, ALU.mult, 1.0, ALU.add)
        if self.lite:
            return
        fm_groups = [0, 1, 2, 3, 6, 7, 8, 9]
        for g in fm_groups:
            self.tiles.append(("cols", I["w_ada"], [g * 512 + j * 128 for j in range(4)]))
        for ph in range(2 if self.stage >= 2 else 1):
            for g in [4, 5, 10, 11]:
                self.tiles.append(("cols", I["w_ada"], [g * 512 + j * 128 for j in range(4)]))
            for blk in range(1 if ph == 0 else 4):
                for gi in range(0, 30, 4):
                    self.tiles.append(("cols", I["w_in"], [(gi + q_) * 128 for q_ in range(len(INPROJ_ORDER[gi:gi + 4]))]))
                for hf in range(2):
                    self.tiles.append(("cols", I["w_out"], [hf * 512 + j * 128 for j in range(4)]))
                for g in range(8):
                    self.tiles.append(("cols", I["w_up"], [g * 512 + j * 128 for j in range(4)]))
                    self.tiles.append(("rows", I["w_down"], g * 512))
        for gi, g in enumerate(fm_groups):
            wt = self.get_tile()
            ps = self.ps()
            for j in range(4):
                for k in range(8):
                    S.mm(ps.ap(0, 128, j * 32, [(1, 17)]), wt.ap(0, 128, k * 512 + j * 128, [(1, 128)]), self.scT[:, k, :],
                         start=(k == 0), stop=(k == 7), signal=(k == 7 and j == 3))
            for j in range(4):
                fc = g * 4 + j
                S.act(self.modT[:, gi * 4 + j, :], ps.ap(0, 128, j * 32, [(1, 17)]), AF.Identity,
                      bias=vf[:, VC["b_ada"] + fc:VC["b_ada"] + fc + 1], scale=1.0)
        for (G, nk, sc0) in [(self.G1, "norm1", 8), (self.G2, "norm2", 24)]:
            for k in range(8):
                S.ts(G[:, k, :], self.modT[:, sc0 + k, :], 1.0, ALU.add, vf[:, VC[nk] + k:VC[nk] + k + 1], ALU.mult)

    def issue_tile(self, idx):
        S = self.S
        kind, ap, arg = self.tiles[idx]
        wt = self.ring[idx % len(self.ring)]
        if kind == "cols":
            cols = arg
            j = 0
            while j < len(cols):
                n = 1
                while j + n < len(cols) and cols[j + n] == cols[j] + 128 * n:
                    n += 1
                src = ap[:, cols[j]:cols[j] + 128 * n].rearrange("(k p) c -> p k c", p=128)
                dst = wt.ap(0, 128, j * 128, [(512, 8), (1, 128 * n)])
                S.dma("pool", dst.ap, src, writes=[wt])
                j += n
        else:
            r0 = arg
            src = ap[r0:r0 + 512, :].rearrange("(c p) n -> p c n", p=128)
            dst = wt.ap(0, 128, 0, [(1024, 4), (1, 1024)])
            S.dma("pool", dst.ap, src, writes=[wt])

    def get_tile(self):
        i = self.tile_got
        self.tile_got += 1
        target = min(len(self.tiles), i + len(self.ring) - 1)
        while self.tile_issued < target:
            self.issue_tile(self.tile_issued)
            self.tile_issued += 1
        return self.ring[i % len(self.ring)]

    def run_phase(self, stack, sample):
        S, I, O = self.S, self.I, self.O
        vf = self.vecF
        sb = lambda n, s, d=F32: self.sb(stack, ("s_" if sample else "p_") + n, s, d)
        N = 64 if sample else 512
        C = 4 if sample else 64
        NCH = N // C
        HS = 64
        R = min(N, 128)
        NT = max(1, N // 128)
        nblk = 1 if sample else 4
        L = 1 if sample else 3
        idb = self.cstb
        idf = self.cstf

        def rows(fn):
            if not sample:
                fn(0, 128)
            else:
                fn(0, C)
                fn(HS, C)

        xbt = [sb("xb%d" % t_, [128, D]) for t_ in range(NT)]
        st8 = sb("st8", [128, 16])
        hT = sb("hT", [128, 8, N], BF16)
        mixT = Buf("mixT", hT.t)
        mixT.aliases = [hT]
        hT.aliases = [mixT]
        GT1 = sb("GT1", [128, D])
        GT2 = sb("GT2", [128, D])
        NH = N // 2
        NCHH = NCH // 2 if False else (N // 2) // (4 if sample else 64)
        TH = [[sb("T%d_%d" % (hf_, i), [128, N // 2]) for i in range(5)] for hf_ in range(2)]
        rkv = [sb("rkv%d" % i_, [128, 3, N]) for i_ in range(2)]
        wa = sb("wa", [128, N], BF16)
        sgd = sb("sgd", [128, N], BF16)
        notst = sb("notst", [128, N])
        self._tmp = [sb("tmpA", [128, 512]), sb("tmpB", [128, 512])]
        arena = stack.enter_context(self.nc.sbuf_tensor(("s_" if sample else "p_") + "arena", [128, max(28 * N, 2 * NT * D + 8 * N)], BF16))
        aoff = [0]

        def carve(name, nel_bf16, dt, pat=None, **kw):
            ap = arena[:, aoff[0]:aoff[0] + nel_bf16]
            aoff[0] += nel_bf16
            if dt == F32:
                ap = ap.bitcast(F32)
            if pat:
                ap = ap.rearrange(pat, **kw)
            return Buf(name, ap)

        AR = carve("AR", 8 * N, BF16, "p (a c t e) -> p a c t e", a=4, c=NCH, t=2)
        Bt = carve("Bt", 4 * N, BF16, "p (a n) -> p a n", a=4)
        Kt = carve("Kt", 4 * N, BF16, "p (a n) -> p a n", a=4)
        Vt = carve("Vt", 4 * N, BF16, "p (a n) -> p a n", a=4)
        gbf = carve("gbf", 4 * N, BF16, "p (a n) -> p a n", a=4)
        bonus = carve("bonus", 4 * N, BF16, "p (a n) -> p a n", a=4)
        aoff[0] = 0
        dacc = carve("dacc", 2 * NT * D, F32, "p (t d) -> p t d", t=NT)
        uT = [carve("uT%d" % i, 4 * N, BF16, "p (a n) -> p a n", a=4) for i in range(2)]
        mixer_bufs = [AR, Bt, Kt, Vt, gbf, bonus]
        mlp_bufs = [dacc] + uT
        for a in mixer_bufs:
            a.aliases = list(mlp_bufs)
        for a in mlp_bufs:
            a.aliases = list(mixer_bufs)
        gamR = sb("gamR", [128, 4, NCH])
        gamH = sb("gamH", [128, 4, NCH])
        sprev = sb("sprev", [128, 14, SB if sample else 1])
        arena2 = stack.enter_context(self.nc.sbuf_tensor(("s_" if sample else "p_") + "arena2", [128, max(NT * D, 8 * N)], BF16))
        xn = Buf("xn", arena2[:, 0:NT * D].rearrange("p (t d) -> p t d", t=NT))
        Qh = Buf("Qh", arena2[:, 0:4 * N].rearrange("p (a n) -> p a n", a=4))
        Kh = Buf("Kh", arena2[:, 4 * N:8 * N].rearrange("p (a n) -> p a n", a=4))
        screp = Buf("screp", arena2[:, 0:8 * R].rearrange("p (k r) -> p k r", k=8))
        xn.aliases = [Qh, Kh, screp]
        Qh.aliases = [xn, screp]
        Kh.aliases = [xn, screp]
        screp.aliases = [xn, Qh, Kh]
        Vh = sb("Vh", [128, 4, N], BF16)
        ogs = sb("ogs", [128, 4, N], BF16)
        qs = sb("qs", [128, N])
        NS = 3

        class TS:
            pass
        tsets = []
        for i_ in range(NS):
            t_ = TS()
            t_.AkRk = sb("AkRk%d" % i_, [128, 4, 2, C], BF16)
            t_.AbRb = sb("AbRb%d" % i_, [128, 4, 2, C], BF16)
            t_.Pm = [sb("Pm%d_%d" % (i_, q_), [128, 4, C], BF16) for q_ in range(2)]
            t_.PTm = [sb("PTm%d_%d" % (i_, q_), [128, 4, C], BF16) for q_ in range(2)]
            t_.Qm = sb("Qm%d" % i_, [128, 4, C], BF16)
            t_.TTm = [sb("TTm%d_%d" % (i_, q_), [128, 4, C], BF16) for q_ in range(2)]
            if not sample:
                hbn = [sb("HB%d_%d" % (i_, q_), [128, 4, 64], BF16) for q_ in range(3)]
                t_.HB = hbn + [t_.Pm[0], t_.Pm[1], t_.PTm[0], t_.PTm[1], t_.Qm]
            else:
                t_.HB = None
            t_.Btok = sb("Btok%d" % i_, [128, 4, 64], BF16)
            t_.Ktok = sb("Ktok%d" % i_, [128, 4, 64], BF16)
            t_.Vtok = sb("Vtok%d" % i_, [128, 4, 64], BF16)
            tsets.append(t_)
        OTc = sb("OTc", [128, 4, C])
        PT1 = sb("PT1", [128, 4, C])
        PT2 = sb("PT2", [128, 4, C])
        HO = sb("HO", [128, 4, C])
        HT1 = sb("HT1", [128, 4, C])
        if sample:
            psets = [(OTc, PT1, PT2), (sb("OTc2", [128, 4, C]), sb("PT12", [128, 4, C]), sb("PT22", [128, 4, C]))]
        else:
            def alias_view(name, base_buf, ap):
                b_ = Buf(name, ap)
                b_.aliases = [base_buf]
                base_buf.aliases = base_buf.aliases + [b_]
                return b_
            q3 = lambda lo: qs.t[:, lo:lo + 4 * C].rearrange("p (a b) -> p a b", a=4)
            w3 = wa.t[:, :].bitcast(F32).rearrange("p (a b) -> p a b", a=4)
            psets = [(OTc, PT1, PT2), (alias_view("OTc2", qs, q3(0)), alias_view("PT12", qs, q3(4 * C)), alias_view("PT22", wa, w3))]
        Wsb = sb("Wsb", [128, 4, 64], BF16)
        Usb = sb("Usb", [128, 4, 64], BF16)
        tH = sb("tH", [128, 4, 64])
        hKtok = sb("hKtok", [128, 4, 128], BF16)
        hVtok = sb("hVtok", [128, 4, 128], BF16)
        hAT = sb("hAT", [128, 4, C], BF16)
        tS = sb("tS", [128, 4, 128])
        if sample:
            Hst = sb("Hst", [128, 4, SB, 64])
            Hsb = sb("Hsb", [128, 4, SB, 64], BF16)
            Sld = sb("Sld", [128, 4, SB, 64])
            Shs = sb("Shs", [128, 4, SB, 128])
            Shb = sb("Shb", [128, 4, SB, 128], BF16)
        else:
            Hst = sb("Hst", [128, 4, 1, 64])
            Hsb = sb("Hsb", [128, 4, 1, 64], BF16)
            Sld = sb("Sld", [128, 4, 1, 64])
            Shs = sb("Shs", [128, 4, 1, 128])
            Shb = sb("Shb", [128, 4, 1, 128], BF16)

        S.memset(notst[:, :], 1.0)
        S.memset(notst.ap(0, 128, 0, [(C, NCH), (1, 1)]), 0.0)
        if sample:
            if self.stage < 0.12:
                return
            S.dma("sp", sprev[:, :, :].ap, I["sst"], writes=[sprev])
            if self.stage < 0.13:
                return
            for hp in range(4):
                S.dma("sp", Sld[:, hp, :, :].ap,
                      I["swkv"][:, 2 * hp:2 * hp + 2, :, :].rearrange("b h v k -> (h v) b k"), writes=[Sld])
            if self.stage < 0.14:
                return
            for h in range(4):
                S.dma("sp", Shs[:, h, :, :].ap, I["shg"][:, h, :, :].rearrange("b f i -> f b i"), writes=[Shs])
            if self.stage < 0.15:
                return
            S.copy(Shb[:, :, :, :], Shs[:, :, :, :], eng="pool")
            if self.stage < 0.16:
                return
            for hp in range(4):
                for b0 in range(0, SB, 8):
                    ps = self.ps()
                    for b in range(b0, b0 + 8):
                        for h in range(2):
                            S.mm(ps.ap(64 * h, 64, (b - b0) * 64, [(1, 64)]), Sld[64 * h:64 * h + 64, hp, b, :],
                                 idf[64 * h:64 * h + 64, C_IDS:C_IDS + 64], signal=(b == b0 + 7 and h == 1))
                    S.copy(Hst[:, hp, b0:b0 + 8, :], ps.ap(0, 128, 0, [(64, 8), (1, 64)]), eng="dve")
                    S.copy(Hsb[:, hp, b0:b0 + 8, :], ps.ap(0, 128, 0, [(64, 8), (1, 64)]), eng="act")
        else:
            S.memset(sprev[:, :, :], 0.0)
            S.memset(Hst[:, :, :, :], 0.0)
            S.memset(Hsb[:, :, :, :], 0.0)
            S.memset(Shs[:, :, :, :], 0.0)
            S.memset(Shb[:, :, :, :], 0.0)

        if self.stage < 0.2:
            return
        if sample:
            S.copy(screp.ap(0, 128, 0, [(R, 8), (4, 16), (1, 4)]), self.scT.ap(0, 128, 1, [(17, 8), (1, 16), (0, 4)]))
        else:
            S.copy(screp[:, :, :], self.scT.ap(0, 128, 0, [(17, 8), (0, 128)]))
        for gi, (GT, half) in enumerate([(GT1, 0), (GT1, 1), (GT2, 0), (GT2, 1)]):
            wt = self.get_tile()
            ps = self.ps()
            for k in range(8):
                S.mm(ps.full(R, 512), screp[:, k, :], wt.ap(0, 128, k * 512, [(1, 512)]), start=(k == 0), stop=False, signal=False)
            S.mm(ps.full(R, 512), idb.ap(0, 4, C_ID + gi, [(0, R)]), self.bgt[0:4, :], start=False, stop=True)
            S.copy(GT[0:R, half * 512:(half + 1) * 512], ps.full(R, 512), eng="act")

        def rmsnorm_T(G, sh0, dst):
            for t in range(NT):
                S.act(xn[0:R, t, :], xbt[t][0:R, :], AF.Square, accum=st8[0:R, t:t + 1])
            S.ts(st8[0:R, 4:4 + NT], st8[0:R, 0:NT], 1.0 / D, ALU.mult, NORM_EPS, ALU.add)
            S.act(st8[0:R, 4:4 + NT], st8[0:R, 4:4 + NT], AF.Sqrt)
            S.recip(st8[0:R, 8:8 + NT], st8[0:R, 4:4 + NT])
            for t in range(NT):
                S.ts(xn[0:R, t, :], xbt[t][0:R, :], st8[0:R, 8 + t:9 + t], ALU.mult)
            for k in range(8):
                ps = self.ps()
                for t in range(NT):
                    S.mm(ps.ap(0, 128, t * 128, [(1, R)]), xn[0:R, t, k * 128:(k + 1) * 128], idb[0:R, C_ID:C_ID + R],
                         signal=(t == NT - 1))
                if not sample:
                    S.act(dst[:, k, :], ps.full(128, N), AF.Identity, scale=G[:, k, 0:1], bias=self.modT[:, sh0 + k, 0:1])
                else:
                    S.tt(self._tmp[0].ap(0, 128, 0, [(4, 16), (1, 4)]), ps.ap(0, 128, 0, [(4, 16), (1, 4)]),
                         G.ap(0, 128, k * 17 + 1, [(1, 16), (0, 4)]), ALU.mult)
                    S.tt(dst.ap(0, 128, k * N, [(4, 16), (1, 4)]), self._tmp[0].ap(0, 128, 0, [(4, 16), (1, 4)]),
                         self.modT.ap(0, 128, (sh0 + k) * 17 + 1, [(1, 16), (0, 4)]), ALU.add)

        def tv(buf, off=0):
            return buf.ap(0, 128, off, [(C, NCH), (1, C)])

        def token_shift(c, ps, dst, dst_off):
            mu = vf[:, VC["mu"] + c:VC["mu"] + c + 1]
            p1 = dst.ap(0, 128, dst_off, [(1, N)])
            S.act(p1, ps.full(128, N), AF.Identity, scale=self.omm[:, c:c + 1])
            if not sample:
                S.stt(dst.ap(0, 128, dst_off, [(1, 1)]), sprev[:, c, 0:1], mu, dst.ap(0, 128, dst_off, [(1, 1)]), ALU.mult, ALU.add)
                S.stt(dst.ap(0, 128, dst_off + 1, [(1, N - 1)]), ps.ap(0, 128, 0, [(1, N - 1)]), mu,
                      dst.ap(0, 128, dst_off + 1, [(1, N - 1)]), ALU.mult, ALU.add)
                S.copy(sprev[:, c, 0:1], ps.ap(0, 128, N - 1, [(1, 1)]), eng="act")
            else:
                S.stt(dst.ap(0, 128, dst_off, [(4, 16), (1, 1)]), sprev.ap(0, 128, c * SB, [(1, 16), (1, 1)]), mu,
                      dst.ap(0, 128, dst_off, [(4, 16), (1, 1)]), ALU.mult, ALU.add)
                S.stt(dst.ap(0, 128, dst_off + 1, [(4, 16), (1, 3)]), ps.ap(0, 128, 0, [(4, 16), (1, 3)]), mu,
                      dst.ap(0, 128, dst_off + 1, [(4, 16), (1, 3)]), ALU.mult, ALU.add)
                S.copy(sprev.ap(0, 128, c * SB, [(1, 16), (1, 1)]), ps.ap(0, 128, 3, [(4, 16), (1, 1)]), eng="act")

        if sample:
            TG = [[sb("TG%d_%d" % (hf_, i), [128, N // 2]) for i in range(3)] for hf_ in range(2)]
        else:
            mkv = lambda b_: Buf(b_.name + "_v", b_.t[:, :, :].rearrange("p a b -> p (a b)"))
            TG = [[mkv(PT1), mkv(PT2), mkv(OTc)], [mkv(HO), mkv(HT1), mkv(tH)]]
            for (v_, o_) in zip(TG[0] + TG[1], [PT1, PT2, OTc, HO, HT1, tH]):
                v_.aliases = [o_]
                o_.aliases = [v_]

        def run_rr(gens):
            gens = list(gens)
            while gens:
                for g_ in list(gens):
                    try:
                        next(g_)
                    except StopIteration:
                        gens.remove(g_)

        def hv(buf, off, hf):
            return buf.ap(0, 128, off + hf * NH, [(1, NH)])

        def hvc(buf, off, hf):
            return buf.ap(0, 128, off + hf * NH, [(C, NCHH), (1, C)])

        def rwkv_prep(j, rk, hf):
            T = TH[hf]
            r = hv(rk, 0, hf)
            k = hv(rk, N, hf)
            v = hv(rk, 2 * N, hf)
            jc = slice(j * 128, (j + 1) * 128)
            col = lambda n: vf[:, VC[n] + j:VC[n] + j + 1]
            tvh = lambda b_: b_.ap(0, 128, 0, [(C, NCHH), (1, C)])
            pfull = lambda p_: p_.ap(0, 128, 0, [(1, NH)])
            ps1 = self.ps()
            S.mm(pfull(ps1), self.wdec[0:64, jc], hv(wa, 0, hf)[0:64])
            S.act(T[0][:, :], pfull(ps1), AF.Sigmoid, bias=col("w0"), scale=1.0)
            yield
            ps2 = self.ps()
            S.mm(pfull(ps2), self.wdec[64:128, jc], hv(wa, 0, hf)[64:128])
            S.act(T[1][:, :], pfull(ps2), AF.Sigmoid, bias=col("a0"), scale=1.0)
            yield
            ps3 = self.ps()
            S.mm(pfull(ps3), self.wgate[:, jc], hv(sgd, 0, hf))
            S.copy(hv(gbf, j * N, hf), pfull(ps3), eng="act")
            yield
            S.scan(T[2][:, :], hv(notst, 0, hf), T[0][:, :], 0.0, ALU.mult, ALU.add)
            yield
            S.tt(T[0][:, :], T[2][:, :], T[0][:, :], ALU.subtract)
            S.act(T[3][:, :], T[2][:, :], AF.Exp, scale=-WDEC)
            yield
            S.act(T[4][:, :], T[2][:, :], AF.Exp, scale=WDEC)
            S.act(T[0][:, :], T[0][:, :], AF.Exp, scale=-WDEC)
            yield
            S.copy(gamR.ap(0, 128, j * NCH + hf * NCHH, [(1, NCHH)]), T[3].ap(0, 128, C - 1, [(C, NCHH)]), eng="pool")
            S.stt(T[2][:, :], k, self.kk2[:, j:j + 1], k, ALU.mult, ALU.mult)
            yield
            ps4 = self.ps()
            S.mm(pfull(ps4), idf[:, C_BONE:C_BONE + 128], T[2][:, :])
            S.act(T[2][:, :], pfull(ps4), AF.Ln, bias=self.epsc[:, 2:3], scale=1.0)
            yield
            S.act(T[2][:, :], T[2][:, :], AF.Exp, scale=-0.5)
            yield
            S.stt(T[2][:, :], k, col("k_k"), T[2][:, :], ALU.mult, ALU.mult)
            yield
            S.stt(AR.ap(0, 128, j * 2 * N + hf * NCHH * 2 * C, [(2 * C, NCHH), (1, C)]), tvh(T[2]), -1.0, tvh(T[0]), ALU.mult, ALU.mult)
            S.tt(T[0][:, :], T[2][:, :], T[1][:, :], ALU.mult, eng="pool")
            yield
            S.tt(hv(Bt, j * N, hf), T[0][:, :], T[4][:, :], ALU.mult, eng="pool")
            S.ts(T[2][:, :], T[1][:, :], col("k_a"), ALU.mult, self.omka[:, j:j + 1], ALU.add)
            yield
            S.tt(T[2][:, :], k, T[2][:, :], ALU.mult)
            yield
            S.tt(hv(Kt, j * N, hf), T[2][:, :], T[4][:, :], ALU.mult, eng="pool")
            S.tt(AR.ap(0, 128, j * 2 * N + hf * NCHH * 2 * C + C, [(2 * C, NCHH), (1, C)]), hvc(rk, 0, hf), tvh(T[3]), ALU.mult)
            S.copy(hv(Vt, j * N, hf), v, eng="act")
            yield
            S.stt(T[1][:, :], r, col("r_k"), T[2][:, :], ALU.mult, ALU.mult)
            yield
            ps5 = self.ps()
            S.mm(pfull(ps5), idf[:, C_BONE:C_BONE + 128], T[1][:, :])
            S.tt(hv(bonus, j * N, hf), pfull(ps5), v, ALU.mult)

        def hgrn_prep(h, ps, hf):
            T = TG[hf]
            pfh = ps.ap(0, 128, hf * NH, [(1, NH)])
            S.act(T[0][:, :], pfh, AF.Sigmoid)
            yield
            S.ts(T[1][:, :], T[0][:, :], self.oml[:, h:h + 1], ALU.mult, self.lbv[:, h:h + 1], ALU.add)
            yield
            S.ts(T[0][:, :], T[1][:, :], -1.0, ALU.mult, 1.0, ALU.add, eng="pool")
            S.act(T[1][:, :], T[1][:, :], AF.Ln)
            yield
            S.scan(T[2][:, :], hv(notst, 0, hf), T[1][:, :], 0.0, ALU.mult, ALU.add)
            yield
            S.act(T[1][:, :], T[2][:, :], AF.Exp)
            S.act(T[2][:, :], T[2][:, :], AF.Exp, scale=-1.0)
            yield
            S.tt(hv(Qh, h * N, hf), hv(qs, 0, hf), T[1][:, :], ALU.mult)
            S.tt(hv(Kh, h * N, hf), T[0][:, :], T[2][:, :], ALU.mult, eng="pool")
            S.copy(gamH.ap(0, 128, h * NCH + hf * NCHH, [(1, NCHH)]), T[1].ap(0, 128, C - 1, [(C, NCHH)]), eng="pool")

        active_rw = []
        active_hg = []

        def post_inproj(c, ps):
            if c < 12:
                j = c % 4
                token_shift(c, ps, rkv[j % 2], (c // 4) * N)
                if c // 4 == 2:
                    active_rw.extend([rwkv_prep(j, rkv[j % 2], 0), rwkv_prep(j, rkv[j % 2], 1)])
            elif c == 12:
                token_shift(c, ps, self._tmp[0], 0)
                S.act(wa[0:64, :], self._tmp[0][0:64, 0:N], AF.Tanh)
                S.copy(wa[64:128, :], self._tmp[0][64:128, 0:N], eng="act")
            elif c == 13:
                token_shift(c, ps, self._tmp[1], 0)
                S.act(sgd[:, :], self._tmp[1][:, 0:N], AF.Sigmoid)
            elif c < 18:
                S.act(qs[:, :], ps.full(128, N), AF.Silu)
            elif c < 22:
                gs_ = [hgrn_prep(c - 18, ps, 0), hgrn_prep(c - 18, ps, 1)]
                for g_ in gs_:
                    next(g_)
                active_hg.extend(gs_)
            elif c < 26:
                S.copy(Vh[:, c - 22, :], ps.full(128, N), eng="act")
            else:
                S.act(ogs[:, c - 26, :], ps.full(128, N), AF.Silu)

        m1c = idb.ap(0, 128, C_M1S if sample else C_M1, [(0, 4), (1, 2 * C)])
        msc = idb.ap(0, 128, C_MSS if sample else C_BD, [(0, 4), (1, C)])
        m2c = m1c if sample else idb.ap(0, 128, C_M2H, [(0, 4), (1, 2 * C)])
        idcc = idb.ap(0, 128, C_IDS, [(0, 4), (1, C)]) if not sample else None

        def idview(r0, nr):
            if sample:
                return idb.ap(r0, nr, C_ID + r0, [(0, 4), (1, C)])
            return View(idcc.buf, idcc.ap[r0:r0 + nr])

        def getbanks(n):
            while 8 - len(self.ps_open) < n:
                yield
            return [self.ps(hold=True) for _ in range(n)]

        post_gens = {}

        def hmm(out_ps, lhs, rhs, w=C, kb=None):
            for hp in range(4):
                for h in range(2):
                    ob = HS * h
                    S.mm(out_ps.ap(ob, C, hp * w, [(1, w)]), lhs(ob, hp), rhs(ob, hp), signal=(hp == 3 and h == 1))

        def rwkv_inv_gen(c, ts):
            cs = slice(c * C, (c + 1) * C)
            AkRk, AbRb, Pm, PTm, Qm, TTm = ts.AkRk, ts.AbRb, ts.Pm, ts.PTm, ts.Qm, ts.TTm
            (psA,) = yield from getbanks(1)
            for hp in range(4):
                for h in range(2):
                    rb, ob = 64 * h, HS * h
                    arv = AR.ap(rb, 64, (hp * NCH + c) * 2 * C, [(1, 2 * C)])
                    S.mm(psA.ap(ob, C, hp * 2 * C, [(1, 2 * C)]), Kt[rb:rb + 64, hp, cs], arv, signal=(hp == 3 and h == 1))
            yield
            rows(lambda r0, nr: S.tt(AkRk.ap(r0, nr, 0, [(2 * C, 4), (1, 2 * C)]), psA.ap(r0, nr, 0, [(2 * C, 4), (1, 2 * C)]),
                                     View(m1c.buf, m1c.ap[r0:r0 + nr]), ALU.mult))
            psA.done()
            psB, psC = yield from getbanks(2)
            for hp in range(4):
                for h in range(2):
                    rb, ob = 64 * h, HS * h
                    last = (hp == 3 and h == 1)
                    arv = AR.ap(rb, 64, (hp * NCH + c) * 2 * C, [(1, 2 * C)])
                    S.mm(psB.ap(ob, C, hp * 2 * C, [(1, 2 * C)]), Bt[rb:rb + 64, hp, cs], arv, signal=last)
                    S.mm(psC.ap(ob, C, hp * C, [(1, C)]), AR.ap(rb, 64, (hp * NCH + c) * 2 * C, [(1, C)]),
                         Bt[rb:rb + 64, hp, cs], signal=last)
            yield

            def ev(r0, nr):
                S.tt(AbRb.ap(r0, nr, 0, [(2 * C, 4), (1, 2 * C)]), psB.ap(r0, nr, 0, [(2 * C, 4), (1, 2 * C)]),
                     View(m2c.buf, m2c.ap[r0:r0 + nr]), ALU.mult)
                S.tt(Pm[0].ap(r0, nr, 0, [(C, 4), (1, C)]), psC.ap(r0, nr, 0, [(C, 4), (1, C)]),
                     View(msc.buf, msc.ap[r0:r0 + nr]), ALU.mult)
                S.tt(TTm[0].ap(r0, nr, 0, [(C, 4), (1, C)]), AbRb.ap(r0, nr, 0, [(2 * C, 4), (1, C)]), idview(r0, nr), ALU.add, eng="pool")
            rows(ev)
            if not sample:
                mk = lambda off: idb.ap(0, 128, off, [(0, 4), (1, C)])
                HB = ts.HB
                S.tt(HB[0][:, :, :], psC.ap(0, 128, 0, [(C, 4), (1, C)]), mk(C_N0), ALU.mult)
                S.tt(HB[1][:, :, :], psB.ap(0, 128, 0, [(2 * C, 4), (1, C)]), mk(C_N0T), ALU.mult)
                S.tt(HB[2][:, :, :], psC.ap(0, 128, 0, [(C, 4), (1, C)]), mk(C_N1), ALU.mult)
            psB.done()
            psC.done()
            yield
            Pc = Pm[0]
            PTc = lambda ob, hp: AbRb.ap(ob, C, hp * 2 * C, [(1, C)])
            TTc = TTm[0]
            for l in range(1, L + 1):
                if l < L:
                    psP, psPT = yield from getbanks(2)
                else:
                    (psP,) = yield from getbanks(1)
                    psPT = None
                hmm(psP, PTc, lambda ob, hp, Pc=Pc: Pc.ap(ob, C, hp * C, [(1, C)]))
                if l < L:
                    hmm(psPT, lambda ob, hp, Pc=Pc: Pc.ap(ob, C, hp * C, [(1, C)]), PTc)
                yield
                Pn, PTn = Pm[l % 2], PTm[l % 2]

                def ev2(r0, nr):
                    S.tt(Qm.ap(r0, nr, 0, [(C, 4), (1, C)]), psP.ap(r0, nr, 0, [(C, 4), (1, C)]), idview(r0, nr), ALU.add)
                    if l < L:
                        S.copy(Pn.ap(r0, nr, 0, [(C, 4), (1, C)]), psP.ap(r0, nr, 0, [(C, 4), (1, C)]), eng="act")
                        S.copy(PTn.ap(r0, nr, 0, [(C, 4), (1, C)]), psPT.ap(r0, nr, 0, [(C, 4), (1, C)]), eng="dve")
                rows(ev2)
                psP.done()
                if psPT is not None:
                    psPT.done()
                (psT,) = yield from getbanks(1)
                hmm(psT, lambda ob, hp: Qm.ap(ob, C, hp * C, [(1, C)]), lambda ob, hp, TTc=TTc: TTc.ap(ob, C, hp * C, [(1, C)]))
                yield
                TTn = TTm[l % 2]
                rows(lambda r0, nr: S.copy(TTn.ap(r0, nr, 0, [(C, 4), (1, C)]), psT.ap(r0, nr, 0, [(C, 4), (1, C)]), eng="act"))
                psT.done()
                Pc = Pn
                PTc = (lambda PTn: (lambda ob, hp: PTn.ap(ob, C, hp * C, [(1, C)])))(PTn)
                TTc = TTn
            pss = []
            pbk, pv_ = yield from getbanks(2)
            for (src, dstb, pst_, co) in [(Bt, ts.Btok, pbk, 0), (Kt, ts.Ktok, pbk, 256), (Vt, ts.Vtok, pv_, 0)]:
                for hp in range(4):
                    for h in range(2):
                        rb, ob = 64 * h, HS * h
                        S.mm(pst_.ap(ob, C, co + hp * 64, [(1, 64)]), src[rb:rb + 64, hp, cs], idb[rb:rb + 64, C_IDS:C_IDS + 64],
                             signal=(hp == 3 and h == 1))
                pss.append((pst_, dstb, co))
            yield
            for pst_, dstb, co in pss:
                rows(lambda r0, nr: S.copy(dstb.ap(r0, nr, 0, [(64, 4), (1, 64)]), pst_.ap(r0, nr, co, [(64, 4), (1, 64)]), eng="act"))
            pbk.done()
            pv_.done()
            if not sample:
                HB = ts.HB
                full = lambda b_: b_.ap(0, 128, 0, [(C, 4), (1, C)])
                pfull = lambda p_: p_.ap(0, 128, 0, [(C, 4), (1, C)])
                bl = lambda b_: (lambda ob, hp: b_.ap(ob, C, hp * C, [(1, C)]))
                N0, N0T, N1, D0, X, D1, X2, D1T = HB
                D0T = TTc
                (p_,) = yield from getbanks(1)
                for hp in range(4):
                    for h in range(2):
                        rb = 64 * h
                        S.mm(p_.ap(rb, 64, hp * C, [(1, C)]), D0T.ap(rb, 64, hp * C, [(1, C)]), idb[rb:rb + 64, C_IDS:C_IDS + 64],
                             signal=(hp == 3 and h == 1))
                yield
                S.copy(full(D0), pfull(p_), eng="act")
                p_.done()
                p_, p2_ = yield from getbanks(2)
                hmm(p_, bl(N0T), bl(D0))
                hmm(p2_, bl(N0), bl(D0T))
                yield
                S.copy(full(X), pfull(p_), eng="act")
                S.copy(full(X2), pfull(p2_), eng="act")
                p_.done()
                p2_.done()
                p_, p2_ = yield from getbanks(2)
                hmm(p_, bl(D0T), bl(X))
                hmm(p2_, bl(D0), bl(X2))
                yield
                S.tt(full(D1), pfull(p_), full(D0), ALU.add)
                S.tt(full(D1T), pfull(p2_), full(D0T), ALU.add)
                p_.done()
                p2_.done()
                (p_,) = yield from getbanks(1)
                hmm(p_, bl(N1), bl(D1T))
                yield
                S.copy(full(X), pfull(p_), eng="act")
                p_.done()
                (p_,) = yield from getbanks(1)
                hmm(p_, bl(D1), bl(X))
                yield
                TTf = TTm[0] if TTc is TTm[1] else TTm[1]
                S.tt(full(TTf), pfull(p_), full(D1T), ALU.add)
                p_.done()
                TTc = TTf
            ts.TTfin = TTc

        def rwkv_chain_gen(c, ts):
            sidx = c if sample else 0
            cs = slice(c * C, (c + 1) * C)
            AkRk, AbRb, Btok, Ktok, Vtok, TTc = ts.AkRk, ts.AbRb, ts.Btok, ts.Ktok, ts.Vtok, ts.TTfin
            (psW,) = yield from getbanks(1)
            for hp in range(4):
                for h in range(2):
                    rb, ob = 64 * h, HS * h
                    S.mm(psW.ap(ob, C, hp * 64, [(1, 64)]), AR.ap(rb, 64, (hp * NCH + c) * 2 * C, [(1, C)]),
                         Hsb[rb:rb + 64, hp, sidx, :], start=True, stop=False, signal=False)
                    S.mm(psW.ap(ob, C, hp * 64, [(1, 64)]), AkRk.ap(ob, C, hp * 2 * C, [(1, C)]),
                         Vtok.ap(ob, C, hp * 64, [(1, 64)]), start=False, stop=True, signal=(hp == 3 and h == 1))
            yield
            rows(lambda r0, nr: S.copy(Wsb.ap(r0, nr, 0, [(64, 4), (1, 64)]), psW.ap(r0, nr, 0, [(64, 4), (1, 64)]), eng="act"))
            psW.done()
            (psU,) = yield from getbanks(1)
            hmm(psU, lambda ob, hp: TTc.ap(ob, C, hp * C, [(1, C)]), lambda ob, hp: Wsb.ap(ob, C, hp * 64, [(1, 64)]), w=64)
            yield
            rows(lambda r0, nr: S.copy(Usb.ap(r0, nr, 0, [(64, 4), (1, 64)]), psU.ap(r0, nr, 0, [(64, 4), (1, 64)]), eng="act"))
            psU.done()
            psO, psH = yield from getbanks(2)
            for hp in range(4):
                for h in range(2):
                    rb, ob = 64 * h, HS * h
                    last = (hp == 3 and h == 1)
                    S.mm(psO.ap(rb, 64, hp * C, [(1, C)]), Hsb[rb:rb + 64, hp, sidx, :],
                         AR.ap(rb, 64, (hp * NCH + c) * 2 * C + C, [(1, C)]), start=True, stop=False, signal=False)
                    S.mm(psO.ap(rb, 64, hp * C, [(1, C)]), Usb.ap(ob, C, hp * 64, [(1, 64)]),
                         AbRb.ap(ob, C, hp * 2 * C + C, [(1, C)]), start=False, stop=False, signal=False)
                    S.mm(psO.ap(rb, 64, hp * C, [(1, C)]), Vtok.ap(ob, C, hp * 64, [(1, 64)]),
                         AkRk.ap(ob, C, hp * 2 * C + C, [(1, C)]), start=False, stop=True, signal=last)
                    S.mm(psH.ap(rb, 64, hp * 64, [(1, 64)]), Btok.ap(ob, C, hp * 64, [(1, 64)]),
                         Usb.ap(ob, C, hp * 64, [(1, 64)]), start=True, stop=False, signal=False)
                    S.mm(psH.ap(rb, 64, hp * 64, [(1, 64)]), Ktok.ap(ob, C, hp * 64, [(1, 64)]),
                         Vtok.ap(ob, C, hp * 64, [(1, 64)]), start=False, stop=True, signal=last)
            yield
            hview = Hst.ap(0, 128, sidx * 64, [(Hst.pstride // 4, 4), (1, 64)])
            hbview = Hsb.ap(0, 128, sidx * 64, [(Hsb.pstride // 4, 4), (1, 64)])
            S.tt(tH[:, :, :], psH.ap(0, 128, 0, [(64, 4), (1, 64)]), hview, ALU.add)
            gv = gamR.ap(0, 128, c, [(NCH, 4), (0, 64)])
            S.tt(hbview, tH[:, :, :], gv, ALU.mult, eng="pool")
            S.tt(hview, tH[:, :, :], gv, ALU.mult, eng="pool")
            OTc, PT1, PT2 = psets[c % 2]
            S.copy(OTc[:, :, :], psO.ap(0, 128, 0, [(C, 4), (1, C)]), eng="act")
            psO.done()
            psH.done()
            post_gens[c] = rwkv_post_gen(c)

        def rwkv_post_gen(c):
            OTc, PT1, PT2 = psets[c % 2]
            flat = lambda b_: b_.ap(0, 128, 0, [(1, 4 * C)])
            (ps1,) = yield from getbanks(1)
            S.mm(ps1.ap(0, 128, 0, [(1, 4 * C)]), idf[:, C_BMEAN:C_BMEAN + 128], flat(OTc))
            yield
            S.tt(flat(PT1), flat(OTc), ps1.ap(0, 128, 0, [(1, 4 * C)]), ALU.subtract)
            ps1.done()
            S.tt(flat(PT2), flat(PT1), flat(PT1), ALU.mult, eng="pool")
            (ps2,) = yield from getbanks(1)
            S.mm(ps2.ap(0, 128, 0, [(1, 4 * C)]), idf[:, C_BMEAN:C_BMEAN + 128], flat(PT2))
            yield
            S.act(flat(PT2), ps2.ap(0, 128, 0, [(1, 4 * C)]), AF.Ln, bias=self.epsc[:, 0:1], scale=1.0)
            ps2.done()
            S.act(flat(PT2), flat(PT2), AF.Exp, scale=-0.5)
            S.tt(flat(PT1), flat(PT1), flat(PT2), ALU.mult, eng="pool")
            lw = vf.ap(0, 128, VC["lnx_w"], [(1, 4), (0, C)])
            lb_ = vf.ap(0, 128, VC["lnx_b"], [(1, 4), (0, C)])
            S.tt(PT1[:, :, :], PT1[:, :, :], lw, ALU.mult, eng="pool")
            S.tt(PT1[:, :, :], PT1[:, :, :], lb_, ALU.add, eng="pool")
            S.tt(PT1[:, :, :], PT1[:, :, :], bonus.ap(0, 128, c * C, [(N, 4), (1, C)]), ALU.add, eng="pool")
            S.tt(mixT.ap(0, 128, c * C, [(N, 4), (1, C)]), PT1[:, :, :], gbf.ap(0, 128, c * C, [(N, 4), (1, C)]), ALU.mult, eng="pool")

        def hgrn_gen():
            mi = idb.ap(0, C, (C_M1S + 4) if sample else (C_M1 + 64), [(0, 4), (1, C)])
            for c in range(NCH):
                sidx = c if sample else 0
                cs = slice(c * C, (c + 1) * C)
                psK, psV = yield from getbanks(2)
                for h in range(4):
                    S.mm(psK.ap(0, C, h * 128, [(1, 128)]), Kh[:, h, cs], idb[:, C_ID:C_ID + 128], signal=(h == 3))
                    S.mm(psV.ap(0, C, h * 128, [(1, 128)]), Vh[:, h, cs], idb[:, C_ID:C_ID + 128], signal=(h == 3))
                yield
                S.copy(hKtok[0:C, :, :], psK.ap(0, C, 0, [(128, 4), (1, 128)]), eng="act")
                S.copy(hVtok[0:C, :, :], psV.ap(0, C, 0, [(128, 4), (1, 128)]), eng="act")
                psK.done()
                psV.done()
                psA, psG = yield from getbanks(2)
                for h in range(4):
                    S.mm(psA.ap(0, C, h * C, [(1, C)]), Kh[:, h, cs], Qh[:, h, cs], signal=(h == 3))
                for h in range(4):
                    S.mm(psG.ap(0, 128, h * 128, [(1, 128)]), hKtok[0:C, h, :], hVtok[0:C, h, :], signal=(h == 3))
                yield
                S.tt(hAT[0:C, :, :], psA.ap(0, C, 0, [(C, 4), (1, C)]), mi, ALU.mult)
                psA.done()
                sview = Shs.ap(0, 128, sidx * 128, [(Shs.pstride // 4, 4), (1, 128)])
                sbview = Shb.ap(0, 128, sidx * 128, [(Shb.pstride // 4, 4), (1, 128)])
                S.tt(tS[:, :, :], psG.ap(0, 128, 0, [(128, 4), (1, 128)]), sview, ALU.add)
                psG.done()
                (psO,) = yield from getbanks(1)
                for h in range(4):
                    S.mm(psO.ap(0, 128, h * C, [(1, C)]), Shb[:, h, sidx, :], Qh[:, h, cs], start=(h == 0), stop=False, signal=False)
                for h in range(4):
                    S.mm(psO.ap(0, 128, h * C, [(1, C)]), hVtok[0:C, h, :], hAT[0:C, h, :], start=False, stop=True, signal=(h == 3))
                gv = gamH.ap(0, 128, c, [(NCH, 4), (0, 128)])
                S.tt(sbview, tS[:, :, :], gv, ALU.mult, eng="pool")
                S.tt(sview, tS[:, :, :], gv, ALU.mult, eng="pool")
                yield
                flat = lambda b_: b_.ap(0, 128, 0, [(1, 4 * C)])
                S.copy(HO[:, :, :], psO.ap(0, 128, 0, [(C, 4), (1, C)]), eng="act")
                psO.done()
                S.tt(flat(HT1), flat(HO), flat(HO), ALU.mult, eng="pool")
                (ps1,) = yield from getbanks(1)
                S.mm(ps1.ap(0, 128, 0, [(1, 4 * C)]), idf[:, C_AMEAN:C_AMEAN + 128], flat(HT1))
                yield
                S.act(flat(HT1), ps1.ap(0, 128, 0, [(1, 4 * C)]), AF.Ln, bias=self.epsc[:, 1:2], scale=1.0)
                ps1.done()
                S.act(flat(HT1), flat(HT1), AF.Exp, scale=-0.5)
                S.tt(flat(HT1), flat(HT1), flat(HO), ALU.mult, eng="pool")
                S.stt(mixT.ap(0, 128, 4 * N + c * C, [(N, 4), (1, C)]), HT1[:, :, :], vf[:, VC["gnorm"]:VC["gnorm"] + 1],
                      ogs.ap(0, 128, c * C, [(N, 4), (1, C)]), ALU.mult, ALU.mult)

        def run_units():
            free = list(range(NS))
            inv = {}
            inv_done = {}
            chain = None
            next_inv = 0
            next_chain = 0
            hg = hgrn_gen()
            hg_alive = True
            while next_chain < NCH or chain is not None or hg_alive or post_gens:
                while free and next_inv < NCH:
                    si = free.pop(0)
                    inv[next_inv] = (rwkv_inv_gen(next_inv, tsets[si]), si)
                    next_inv += 1
                if chain is None and next_chain in inv_done and (next_chain - 2) not in post_gens:
                    si = inv_done.pop(next_chain)
                    chain = (rwkv_chain_gen(next_chain, tsets[si]), next_chain, si)
                for _rep in range(2):
                    if chain is not None:
                        try:
                            next(chain[0])
                        except StopIteration:
                            free.append(chain[2])
                            next_chain += 1
                            chain = None
                for cc in sorted(list(inv.keys())):
                    g_, si = inv[cc]
                    try:
                        next(g_)
                    except StopIteration:
                        del inv[cc]
                        inv_done[cc] = si
                for cc in sorted(list(post_gens.keys())):
                    try:
                        next(post_gens[cc])
                    except StopIteration:
                        del post_gens[cc]
                if hg_alive:
                    try:
                        next(hg)
                    except StopIteration:
                        hg_alive = False

        if self.stage < 0.3:
            return
        for blk in range(nblk):
            xsrc = I["xs"] if sample else I["xp"][blk * 512:(blk + 1) * 512, :]
            ydst = O["ys"] if sample else O["yp"][blk * 512:(blk + 1) * 512, :]
            for t in range(NT):
                S.dma("sp", xbt[t][0:R, :].ap, xsrc[t * 128:t * 128 + R, :], writes=[xbt[t]])
            S.mark(('s' if sample else 'p') + str(blk) + ':norm1')
            rmsnorm_T(self.G1, 0, hT)
            S.mark(('s' if sample else 'p') + str(blk) + ':inproj')
            if self.stage < 0.35:
                return
            def inproj_stream():
                for gi in range(0, 30, 4):
                    chunks = INPROJ_ORDER[gi:gi + 4]
                    wt = self.get_tile()
                    for j, c in enumerate(chunks):
                        if c < 12 and c // 4 == 2:
                            while active_rw:
                                yield
                        if 14 <= c < 22:
                            while active_hg:
                                yield
                        ps = self.ps()
                        for k in range(8):
                            S.mm(ps.full(128, N), wt.ap(0, 128, k * 512 + j * 128, [(1, 128)]), hT[:, k, :], start=(k == 0), stop=(k == 7))
                        post_inproj(c, ps)
                        yield

            ip_ = inproj_stream()
            ip_alive = True
            while ip_alive or active_rw or active_hg:
                if ip_alive:
                    try:
                        next(ip_)
                    except StopIteration:
                        ip_alive = False
                for lst_ in (active_rw, active_hg):
                    for g_ in list(lst_):
                        try:
                            next(g_)
                        except StopIteration:
                            lst_.remove(g_)
            if self.stage < 0.4:
                return
            S.mark(('s' if sample else 'p') + str(blk) + ':units')
            run_units()
            S.mark(('s' if sample else 'p') + str(blk) + ':outproj')
            if sample and self.debug:
                dtmp = sb('dtmp', [128, 8 * N])
                S.copy(dtmp[:, :], mixT.ap(0, 128, 0, [(1, 8 * N)]))
                self.dbg(dtmp[:, :], 8 * N)
            if self.stage < 0.6:
                return
            for hf in range(2):
                wt = self.get_tile()
                for t in range(NT):
                    ps = self.ps()
                    for k in range(8):
                        S.mm(ps.full(R, 512), mixT[:, k, t * 128:t * 128 + R], wt.ap(0, 128, k * 512, [(1, 512)]),
                             start=(k == 0), stop=(k == 7))
                    tmp = self.tmp512(t)
                    S.tt(tmp[0:R, :], ps.full(R, 512), GT1[0:R, hf * 512:(hf + 1) * 512], ALU.mult)
                    S.tt(xbt[t][0:R, hf * 512:(hf + 1) * 512], xbt[t][0:R, hf * 512:(hf + 1) * 512], tmp[0:R, :], ALU.add, eng="pool")
            if sample and self.debug:
                self.dbg(xbt[0][0:R, :], 1024, R)
            if self.stage < 0.7:
                return
            S.mark(('s' if sample else 'p') + str(blk) + ':norm2')
            rmsnorm_T(self.G2, 16, hT)
            S.mark(('s' if sample else 'p') + str(blk) + ':mlp')
            if sample and self.debug:
                S.copy(dtmp[:, :], hT.ap(0, 128, 0, [(1, 8 * N)]))
                self.dbg(dtmp[:, :], 8 * N)
            for g in range(8):
                wu = self.get_tile()
                wd = self.get_tile()
                u = uT[g % 2]
                for j in range(4):
                    ps = self.ps()
                    for k in range(8):
                        S.mm(ps.full(128, N), wu.ap(0, 128, k * 512 + j * 128, [(1, 128)]), hT[:, k, :], start=(k == 0), stop=(k == 7))
                    tmp = self.tmp512(j)
                    S.act(tmp.ap(0, 128, 0, [(1, N)]), ps.full(128, N), AF.Relu)
                    S.tt(u[:, j, :], tmp.ap(0, 128, 0, [(1, N)]), tmp.ap(0, 128, 0, [(1, N)]), ALU.mult, eng="pool")
                for t in range(NT):
                    for hf in range(2):
                        ps = self.ps()
                        for j in range(4):
                            S.mm(ps.full(R, 512), u[:, j, t * 128:t * 128 + R], wd.ap(0, 128, j * 1024 + hf * 512, [(1, 512)]),
                                 start=(j == 0), stop=(j == 3))
                        dv = dacc[0:R, t, hf * 512:(hf + 1) * 512]
                        if g == 0:
                            S.copy(dv, ps.full(R, 512), eng="act")
                        else:
                            S.tt(dv, ps.full(R, 512), dv, ALU.add)
            if sample and self.debug:
                self.dbg(dacc[0:R, 0, :], 1024, R)
                self.dbg(GT1[0:R, :], 1024, R)
                self.dbg(GT2[0:R, :], 1024, R)
            S.mark(('s' if sample else 'p') + str(blk) + ':final')
            for t in range(NT):
                for hf in range(2):
                    tmp = self.tmp512(hf)
                    S.tt(tmp[0:R, :], dacc[0:R, t, hf * 512:(hf + 1) * 512], GT2[0:R, hf * 512:(hf + 1) * 512], ALU.mult, eng="pool")
                    S.tt(xbt[t][0:R, hf * 512:(hf + 1) * 512], xbt[t][0:R, hf * 512:(hf + 1) * 512], tmp[0:R, :], ALU.add)
            for t in range(NT):
                S.act(xn[0:R, t, :], xbt[t][0:R, :], AF.Square, accum=st8[0:R, t:t + 1])
            S.ts(st8[0:R, 4:4 + NT], st8[0:R, 0:NT], 1.0 / D, ALU.mult, NORM_EPS, ALU.add)
            S.act(st8[0:R, 4:4 + NT], st8[0:R, 4:4 + NT], AF.Sqrt)
            S.recip(st8[0:R, 8:8 + NT], st8[0:R, 4:4 + NT])
            for t in range(NT):
                S.stt(xbt[t][0:R, :], xbt[t][0:R, :], st8[0:R, 8 + t:9 + t], self.normf[0:R, :], ALU.mult, ALU.mult)
                S.dma("sp", ydst[t * 128:t * 128 + R, :], xbt[t][0:R, :].ap, reads=[xbt[t]])

        if self.stage < 0.9:
            return
        if sample:
            shsb = sb("shsb", [SB, 14, 128])
            for c0 in range(0, 14, 4):
                ps = self.ps()
                n = min(4, 14 - c0)
                for c in range(c0, c0 + n):
                    S.mm(ps.ap(0, SB, (c - c0) * 128, [(1, 128)]), sprev[:, c, :], idf[:, C_ID:C_ID + 128], signal=(c == c0 + n - 1))
                S.copy(shsb[0:SB, c0:c0 + n, :], ps.ap(0, SB, 0, [(128, n), (1, 128)]), eng="act")
            S.dma("sp", O["shs"], shsb[:, :, :].ap, reads=[shsb])
            for hp in range(4):
                for b0 in range(0, SB, 8):
                    ps = self.ps()
                    for b in range(b0, b0 + 8):
                        for h in range(2):
                            S.mm(ps.ap(64 * h, 64, (b - b0) * 64, [(1, 64)]), Hst[64 * h:64 * h + 64, hp, b, :],
                                 idf[64 * h:64 * h + 64, C_IDS:C_IDS + 64], signal=(b == b0 + 7 and h == 1))
                    S.copy(Sld[:, hp, b0:b0 + 8, :], ps.ap(0, 128, 0, [(64, 8), (1, 64)]), eng="act")
                S.dma("sp", O["wkvs"][:, 2 * hp:2 * hp + 2, :, :].rearrange("b h v k -> (h v) b k"), Sld[:, hp, :, :].ap, reads=[Sld])
            for h in range(4):
                S.dma("sp", O["hgs"][:, h, :, :].rearrange("b f i -> f b i"), Shs[:, h, :, :].ap, reads=[Shs])
        else:
            shpb = sb("shpb", [14, 128])
            ps = self.ps()
            S.mm(ps.ap(0, 14, 0, [(1, 128)]), sprev[:, :, 0], idf[:, C_ID:C_ID + 128])
            S.copy(shpb[:, :], ps.ap(0, 14, 0, [(1, 128)]), eng="act")
            S.dma("sp", O["shp"], shpb[:, :].ap, reads=[shpb])
            ps = self.ps()
            for hp in range(4):
                for h in range(2):
                    S.mm(ps.ap(64 * h, 64, hp * 64, [(1, 64)]), Hst[64 * h:64 * h + 64, hp, 0, :],
                         idf[64 * h:64 * h + 64, C_IDS:C_IDS + 64], signal=(hp == 3 and h == 1))
            S.copy(Sld[:, :, 0, :], ps.ap(0, 128, 0, [(64, 4), (1, 64)]), eng="act")
            S.dma("sp", O["wkvp"].rearrange("(hp h) v k -> (h v) hp k", h=2), Sld[:, :, 0, :].ap, reads=[Sld])
            S.dma("sp", O["hgp"].rearrange("h f i -> f h i"), Shs[:, :, 0, :].ap, reads=[Shs])

    def tmp512(self, i):
        return self._tmp[i % len(self._tmp)]


_NC_CACHE = {}


def _prep_inputs(inp):
    f = lambda a: np.ascontiguousarray(np.asarray(a, dtype=np.float32))
    chunks = lambda v: f(v).reshape(-1, 128).T
    vec = np.zeros((128, NVC), np.float32)

    def put(name, v):
        c = chunks(v)
        vec[:, VC[name]:VC[name] + c.shape[1]] = c

    put("norm1", inp["norm1"][0]); put("norm2", inp["norm2"][0]); put("b_ada", inp["b_ada"][0])
    put("mu", inp["mu_shift"][0]); put("w0", inp["w0"][0]); put("a0", inp["a0"][0]); put("k_k", inp["k_k"][0])
    put("k_a", inp["k_a"][0]); put("r_k", np.asarray(inp["r_k"][0]).reshape(-1)); put("lnx_w", inp["lnx_w"][0])
    put("lnx_b", inp["lnx_b"][0]); put("lb0", inp["hgrn_lb"][0]); put("lb1", inp["hgrn_lb"][1])
    put("gnorm", inp["hgrn_gnorm"][0])
    b_ada = f(inp["b_ada"][0])
    shared = {
        "w_ada": f(inp["w_ada"][0]),
        "w_in": np.ascontiguousarray(f(inp["w_in"][0])[:, np.concatenate([np.arange(c * 128, (c + 1) * 128) for c in INPROJ_ORDER])]),
        "w_out": f(inp["w_out"][0]),
        "w_up": f(inp["w_up"][0]), "w_down": f(inp["w_down"][0]), "w_dec": f(inp["w_decay_up"][0]),
        "w_aaa": f(inp["w_aaa_up"][0]), "w_gate": f(inp["w_gate_up"][0]), "vecF": vec,
        "normf": np.ascontiguousarray(np.broadcast_to(f(inp["norm_f"])[None, :], (128, D))),
        "bgt": np.ascontiguousarray(np.concatenate([b_ada[2 * D:3 * D], b_ada[5 * D:6 * D]])[None, :]),
        "cst": make_consts(),
    }
    xp, xs = f(inp["x_prompt"]), f(inp["x_sample"])
    cp, cs_ = f(inp["c_prompt"]), f(inp["c_sample"])
    sst, swkv, shg = f(inp["state_shift"][0]), f(inp["state_wkv"][0]), f(inp["state_hgrn"][0])
    maps = []
    for c in range(NCORES):
        bs = slice(c * SB, (c + 1) * SB)
        call = np.concatenate([cp[c:c + 1], cs_[bs]], axis=0)
        cT = np.ascontiguousarray(call.reshape(17, 8, 128).transpose(2, 1, 0))
        sstc = np.ascontiguousarray(sst[bs].reshape(SB, 14, 128).transpose(2, 1, 0))
        m = dict(shared)
        m.update({"xp": xp[c], "xs": np.ascontiguousarray(xs[bs].reshape(SB * ST, D)), "cT": cT, "sst": sstc,
                  "swkv": np.ascontiguousarray(swkv[bs]), "shg": np.ascontiguousarray(shg[bs])})
        maps.append(m)
    return maps


def _run(inp, stage=99, debug=False, lite=False):
    key = (stage, debug, lite)
    if key not in _NC_CACHE:
        _NC_CACHE[key] = Builder(stage=stage, debug=debug, lite=lite).build()
    nc = _NC_CACHE[key]
    maps = _prep_inputs(inp)
    if lite:
        for m in maps:
            for n in ["w_ada", "w_in", "w_out", "w_up", "w_down"]:
                m[n] = np.zeros((8, 8), np.float32)
    res = run_bass_kernel_spmd(nc, maps, core_ids=list(range(NCORES)))
    return res.results


def kernel(**inp):
    rs = _run(inp)
    g = lambda k: [np.asarray(r[k], dtype=np.float32) for r in rs]
    y_prompt = np.stack(g("yp"), 0)
    y_sample = np.concatenate(g("ys"), 0).reshape(NCORES * SB, ST, D)
    shift_p = np.stack([a.reshape(SHW) for a in g("shp")], 0)[None]
    wkv_p = np.stack(g("wkvp"), 0)[None]
    hgrn_p = np.stack(g("hgp"), 0)[None]
    shift_s = np.concatenate([a.reshape(SB, SHW) for a in g("shs")], 0)[None]
    wkv_s = np.concatenate(g("wkvs"), 0)[None]
    hgrn_s = np.concatenate(g("hgs"), 0)[None]
    return (y_prompt, y_sample, shift_p, wkv_p, hgrn_p, shift_s, wkv_s, hgrn_s)
```

```python
import contextlib
import numpy as np
import concourse.bass as bass
import concourse.mybir as mybir
from concourse.bass_utils import run_bass_kernel_spmd

F32 = mybir.dt.float32
BF16 = mybir.dt.bfloat16
AF = mybir.ActivationFunctionType
ALU = mybir.AluOpType

D = 1024
NCORES = 8
SEQ = 2048
SB = 16
ST = 4
DFF = 4096
INW = 3840
SHW = 1792
WDEC = 0.6065306597126334
NORM_EPS = 1e-6
LNX_EPS = 64e-5

CH_R, CH_K, CH_V, CH_WA, CH_G, CH_Q, CH_F, CH_I, CH_OG = 0, 4, 8, 12, 13, 14, 18, 22, 26
INPROJ_ORDER = [12, 13, 0, 4, 8, 14, 18, 22, 26, 1, 5, 9, 15, 19, 23, 27,
                2, 6, 10, 16, 20, 24, 28, 3, 7, 11, 17, 21, 25, 29]

VC = {}
_off = 0
for _n, _c in [("norm1", 8), ("norm2", 8), ("b_ada", 48), ("mu", 14), ("w0", 4), ("a0", 4), ("k_k", 4),
               ("k_a", 4), ("r_k", 4), ("lnx_w", 4), ("lnx_b", 4), ("lb0", 4), ("lb1", 4), ("gnorm", 1)]:
    VC[_n] = _off
    _off += _c
NVC = _off


class Buf:
    def __init__(self, name, t):
        self.name = name
        self.w = None
        self.r = {}
        self.aliases = []
        self.set_t(t)

    def set_t(self, t):
        self.t = t
        if t is None:
            return
        if hasattr(t, "offset") and hasattr(t, "ap"):
            self.th = t.tensor
            self.pstride = int(t.ap[0][0])
            self.base = int(t.offset)
        else:
            self.th = t.tensor if hasattr(t, "tensor") else t
            self.pstride = int(np.prod(list(t.shape)[1:]))
            self.base = 0

    def __getitem__(self, idx):
        return View(self, self.t[idx])

    def ap(self, p0, npart, off, dims):
        return View(self, bass.AP(self.th, self.base + p0 * self.pstride + off,
                                  [[self.pstride, npart]] + [list(d) for d in dims]))


class View:
    def __init__(self, buf, ap):
        self.buf = buf
        self.ap = ap

    def re(self, pat, **kw):
        return View(self.buf, self.ap.rearrange(pat, **kw))

    def bc(self, shape):
        return View(self.buf, self.ap.to_broadcast(shape))

    def __getitem__(self, idx):
        return View(self.buf, self.ap[idx])


class Eng:
    def __init__(self, name, sem):
        self.name = name
        self.sem = sem
        self.count = 0
        self.waited = {}
        self.ops = []


class DSem:
    def __init__(self, sem):
        self.sem = sem
        self.total = 0


class Sched:
    def __init__(self, nc, stack):
        self.nc = nc
        self.E = {}
        for n in ["pe", "act", "dve", "pool", "sp"]:
            self.E[n] = Eng(n, stack.enter_context(nc.semaphore("s_" + n)))
        self.dsems = [DSem(stack.enter_context(nc.semaphore("d%d" % i))) for i in range(32)]
        self.drr = 0
        self.nops = 0
        self.npe = 0
        self.marks = []

    def _collect(self, E, reads, writes, extra=()):
        deps = {}

        def need(s, v):
            if v > deps.get(id(s), (s, 0))[1]:
                deps[id(s)] = (s, v)

        for b0 in reads:
            for b in [b0] + b0.aliases:
                if b.w is not None:
                    need(*b.w)
            if getattr(b0, "excl", False):
                for s, v in b0.r.values():
                    if s is not E.sem:
                        need(s, v)
        for b0 in writes:
            for b in [b0] + b0.aliases:
                if b.w is not None:
                    need(*b.w)
                for s, v in b.r.values():
                    need(s, v)
        for s, v in extra:
            need(s, v)
        waits = []
        for s, v in deps.values():
            if E.name == "pe" and s is E.sem:
                continue
            if E.waited.get(id(s), 0) < v:
                E.waited[id(s)] = v
                waits.append((s, v))
        return waits

    @staticmethod
    def _update(tok, reads, writes):
        s, v = tok
        for b in reads:
            if b.r.get(id(s), (s, 0))[1] < v:
                b.r[id(s)] = (s, v)
        for b in writes:
            b.w = tok
            b.r = {}

    def emit(self, en, fn, reads=(), writes=(), signal=True):
        E = self.E[en]
        reads = [v.buf if isinstance(v, View) else v for v in reads]
        writes = [v.buf if isinstance(v, View) else v for v in writes]
        waits = self._collect(E, reads, writes)
        if signal:
            E.count += 1
            tokv = E.count
        else:
            tokv = E.count + 1
        E.ops.append((waits, fn, ("sig", E.sem) if signal else None))
        self._update((E.sem, tokv), reads, writes)
        self.nops += 1

    def dma(self, en, out, in_, reads=(), writes=(), **kw):
        E = self.E[en]
        reads = [v.buf if isinstance(v, View) else v for v in reads]
        writes = [v.buf if isinstance(v, View) else v for v in writes]
        ds = self.dsems[self.drr % len(self.dsems)]
        self.drr += 1
        waits = self._collect(E, reads, writes, extra=[(ds.sem, ds.total)] if ds.total else [])
        ds.total += 16
        E.ops.append((waits, lambda h: h.dma_start(out=out, in_=in_, **kw), ("dma", ds.sem)))
        self._update((ds.sem, ds.total), reads, writes)
        self.nops += 1

    def barrier(self):
        for E in self.E.values():
            waits = []
            for O in self.E.values():
                if O is E or O.count == 0:
                    continue
                if E.waited.get(id(O.sem), 0) < O.count:
                    E.waited[id(O.sem)] = O.count
                    waits.append((O.sem, O.count))
            for ds in self.dsems:
                if ds.total and E.waited.get(id(ds.sem), 0) < ds.total:
                    E.waited[id(ds.sem)] = ds.total
                    waits.append((ds.sem, ds.total))
            if waits:
                E.ops.append((waits, None, None))

    def flush(self):
        nc = self.nc
        hmap = {"pe": "tensor", "act": "scalar", "dve": "vector", "pool": "gpsimd", "sp": "sync"}
        with nc.Block() as block:
            for n, E in self.E.items():
                ops = E.ops

                def body(h, ops=ops):
                    for waits, fn, sig in ops:
                        for s, v in waits:
                            h.wait_ge(s, v)
                        if fn is None:
                            continue
                        ins = fn(h)
                        if sig is not None:
                            if sig[0] == "sig":
                                ins.then_inc(sig[1], 1)
                            else:
                                ins.then_inc(sig[1], 16)

                getattr(block, hmap[n])(body)
        for E in self.E.values():
            E.ops = []

    def mark(self, name):
        self.marks.append((name, self.npe))

    def mm(self, out, lhsT, rhs, start=True, stop=True, signal=None):
        self.npe += 1
        if signal is None:
            signal = stop
        self.emit("pe", lambda e: e.matmul(out.ap, lhsT.ap, rhs.ap, start=start, stop=stop),
                  reads=[lhsT, rhs], writes=[out], signal=signal)

    def act(self, out, in_, func, bias=None, scale=None, accum=None, eng="act"):
        kw = {}
        rd = [in_]
        wr = [out]
        if bias is not None:
            if isinstance(bias, View):
                kw["bias"] = bias.ap
                rd.append(bias)
            else:
                kw["bias"] = bias
        if scale is not None:
            if isinstance(scale, View):
                kw["scale"] = scale.ap
                rd.append(scale)
            else:
                kw["scale"] = scale
        if accum is not None:
            kw["accum_out"] = accum.ap
            wr.append(accum)
        self.emit(eng, lambda e: e.activation(out=out.ap, in_=in_.ap, func=func, **kw), reads=rd, writes=wr)

    def tt(self, out, in0, in1, op, eng="dve"):
        self.emit(eng, lambda e: e.tensor_tensor(out=out.ap, in0=in0.ap, in1=in1.ap, op=op),
                  reads=[in0, in1], writes=[out])

    def ts(self, out, in0, s1, op0, s2=None, op1=None, eng="dve"):
        rd = [in0]
        a1 = s1
        a2 = s2
        if isinstance(s1, View):
            rd.append(s1)
            a1 = s1.ap
        if isinstance(s2, View):
            rd.append(s2)
            a2 = s2.ap
        if op1 is None:
            self.emit(eng, lambda e: e.tensor_scalar(out=out.ap, in0=in0.ap, scalar1=a1, scalar2=None, op0=op0),
                      reads=rd, writes=[out])
        else:
            self.emit(eng, lambda e: e.tensor_scalar(out=out.ap, in0=in0.ap, scalar1=a1, scalar2=a2, op0=op0, op1=op1),
                      reads=rd, writes=[out])

    def stt(self, out, in0, s, in1, op0, op1):
        rd = [in0, in1]
        a = s
        if isinstance(s, View):
            rd.append(s)
            a = s.ap
        self.emit("dve", lambda e: e.scalar_tensor_tensor(out=out.ap, in0=in0.ap, scalar=a, in1=in1.ap, op0=op0, op1=op1),
                  reads=rd, writes=[out])

    def copy(self, out, in_, eng="dve"):
        if eng == "act":
            self.emit("act", lambda e: e.activation(out=out.ap, in_=in_.ap, func=AF.Identity), reads=[in_], writes=[out])
        else:
            self.emit(eng, lambda e: e.tensor_copy(out=out.ap, in_=in_.ap), reads=[in_], writes=[out])

    def memset(self, out, val, eng="pool"):
        self.emit(eng, lambda e: e.memset(out.ap, val), reads=[], writes=[out])

    def recip(self, out, in_):
        self.emit("dve", lambda e: e.reciprocal(out=out.ap, in_=in_.ap), reads=[in_], writes=[out])

    def scan(self, out, d0, d1, init, op0, op1):
        self.emit("dve", lambda e: e.tensor_tensor_scan(out=out.ap, data0=d0.ap, data1=d1.ap, initial=init, op0=op0, op1=op1),
                  reads=[d0, d1], writes=[out])


NCST = 1296
C_ID, C_IDS, C_BONE, C_BMEAN, C_AMEAN = 0, 128, 192, 320, 448
C_M1, C_MS = 576, 704
C_M1S, C_MSS = 768, 776
C_ONES = 780
C_M2H = 912
C_BD = 1040
C_N0, C_N0T, C_N1 = 1104, 1168, 1232


def make_consts():
    c = np.zeros((128, NCST), np.float32)
    c[:, C_ID:C_ID + 128] = np.eye(128)
    c[0:64, C_IDS:C_IDS + 64] = np.eye(64)
    c[64:128, C_IDS:C_IDS + 64] = np.eye(64)
    blk = np.kron(np.eye(2), np.ones((64, 64)))
    c[:, C_BONE:C_BONE + 128] = blk
    c[:, C_BMEAN:C_BMEAN + 128] = blk / 64.0
    c[:, C_AMEAN:C_AMEAN + 128] = 1.0 / 128.0
    s = np.arange(128)[:, None] % 64
    t = np.arange(64)[None, :]
    c[:, C_M1:C_M1 + 64] = (s < t)
    c[:, C_M1 + 64:C_M1 + 128] = (s <= t)
    c[:, C_MS:C_MS + 64] = (t < s)
    s4 = np.arange(128)[:, None] % 32
    t4 = np.arange(4)[None, :]
    ok = (s4 < 4)
    c[:, C_M1S:C_M1S + 4] = (s4 < t4) & ok
    c[:, C_M1S + 4:C_M1S + 8] = (s4 <= t4) & ok
    c[:, C_MSS:C_MSS + 4] = (t4 < s4) & ok
    c[:, C_ONES:C_ONES + 128] = 1.0
    r = np.arange(128)[:, None] % 64
    q = np.arange(64)[None, :]
    same16 = (r // 16) == (q // 16)
    same32 = (r // 32) == (q // 32)
    c[:, C_M2H:C_M2H + 64] = (r < q) & same16
    c[:, C_M2H + 64:C_M2H + 128] = (r <= q)
    c[:, C_BD:C_BD + 64] = (q < r) & same16
    c[:, C_N0:C_N0 + 64] = (q < r) & same32 & ~same16
    c[:, C_N0T:C_N0T + 64] = (r < q) & same32 & ~same16
    c[:, C_N1:C_N1 + 64] = (r >= 32) & (q < 32)
    return c


class PsBank:
    def __init__(self, buf, col):
        self.buf = buf
        self.col = col

    def ap(self, p0, npart, off, dims):
        return View(self.buf, bass.AP(self.buf.th, p0 * 4096 + self.col + off, [[4096, npart]] + [list(d) for d in dims]))

    def full(self, npart=128, n=512):
        return self.ap(0, npart, 0, [(1, n)])

    def done(self):
        self.owner.ps_open.discard(self.idx)


class Builder:
    def __init__(self, stage=99, debug=False, lite=False):
        self.stage = stage
        self.debug = debug
        self.lite = lite
        self.dbg_col = 0

    def dram_in(self, name, shape, dt=F32):
        return self.nc.dram_tensor(name, list(shape), dt, kind="ExternalInput").ap()

    def dram_out(self, name, shape, dt=F32):
        return self.nc.dram_tensor(name, list(shape), dt, kind="ExternalOutput").ap()

    def sb(self, stack, name, shape, dt=F32):
        t = stack.enter_context(self.nc.sbuf_tensor("sb_" + name, list(shape), dt))
        return Buf(name, t)

    def dbg(self, view, ncols, npart=128):
        if not self.debug:
            return
        c0 = self.dbg_col
        self.dbg_col += ncols
        assert self.dbg_col <= 8192
        self.S.dma("sp", self.O["dbg"][0:npart, c0:c0 + ncols], view.ap, reads=[view])
        return c0

    def build(self):
        nc = bass.Bass("TRN2", target_bir_lowering=False)
        self.nc = nc
        I = {}
        I["xp"] = self.dram_in("xp", [SEQ, D])
        I["xs"] = self.dram_in("xs", [SB * ST, D])
        I["cT"] = self.dram_in("cT", [128, 8, 17])
        I["sst"] = self.dram_in("sst", [128, 14, SB])
        I["swkv"] = self.dram_in("swkv", [SB, 8, 64, 64])
        I["shg"] = self.dram_in("shg", [SB, 4, 128, 128])
        if self.lite:
            for n in ["w_ada", "w_in", "w_out", "w_up", "w_down"]:
                I[n] = self.dram_in(n, [8, 8])
        else:
            I["w_ada"] = self.dram_in("w_ada", [D, 6 * D])
            I["w_in"] = self.dram_in("w_in", [D, INW])
            I["w_out"] = self.dram_in("w_out", [D, D])
            I["w_up"] = self.dram_in("w_up", [D, DFF])
            I["w_down"] = self.dram_in("w_down", [DFF, D])
        I["w_dec"] = self.dram_in("w_dec", [64, 512])
        I["w_aaa"] = self.dram_in("w_aaa", [64, 512])
        I["w_gate"] = self.dram_in("w_gate", [128, 512])
        I["vecF"] = self.dram_in("vecF", [128, NVC])
        I["normf"] = self.dram_in("normf", [128, D])
        I["bgt"] = self.dram_in("bgt", [1, 2 * D])
        I["cst"] = self.dram_in("cst", [128, NCST])
        O = {}
        O["yp"] = self.dram_out("yp", [SEQ, D])
        O["ys"] = self.dram_out("ys", [SB * ST, D])
        O["shp"] = self.dram_out("shp", [14, 128])
        O["wkvp"] = self.dram_out("wkvp", [8, 64, 64])
        O["hgp"] = self.dram_out("hgp", [4, 128, 128])
        O["shs"] = self.dram_out("shs", [SB, 14, 128])
        O["wkvs"] = self.dram_out("wkvs", [SB, 8, 64, 64])
        O["hgs"] = self.dram_out("hgs", [SB, 4, 128, 128])
        if self.debug:
            O["dbg"] = self.dram_out("dbg", [128, 8192])
        self.I, self.O = I, O

        with contextlib.ExitStack() as stack:
            S = Sched(nc, stack)
            self.S = S
            self.setup_persistent(stack)
            if self.stage < 0.1:
                S.barrier()
                S.flush()
                return nc
            with contextlib.ExitStack() as st2:
                self.run_phase(st2, sample=True)
                S.barrier()
                S.flush()
            if self.stage >= 2:
                with contextlib.ExitStack() as st3:
                    self.run_phase(st3, sample=False)
                    S.barrier()
                    S.flush()
        return nc

    def ps(self, hold=False):
        for _ in range(8):
            i = self.ps_rr % 8
            self.ps_rr += 1
            if i not in self.ps_open:
                if hold:
                    self.ps_open.add(i)
                pb = PsBank(self.psum[i], i * 512)
                pb.owner = self
                pb.idx = i
                return pb
        raise RuntimeError("no free PSUM bank")

    def setup_persistent(self, stack):
        S, I = self.S, self.I
        sb = lambda n, s, d=F32: self.sb(stack, n, s, d)
        pst = stack.enter_context(self.nc.psum_tensor("psall", [128, 4096], F32))
        self.psum = []
        for i in range(8):
            b = Buf("ps%d" % i, None)
            b.t = pst
            b.th = pst.tensor if hasattr(pst, "tensor") else pst
            b.pstride = 4096
            b.base = 0
            b.excl = True
            self.psum.append(b)
        self.ps_rr = 0
        self.ps_open = set()
        self.vecF = sb("vecF", [128, NVC])
        self.cstf = sb("cstf", [128, 576])
        self.cstf_full = None
        self.cstb = sb("cstb", [128, NCST], BF16)
        self.normf = sb("normf", [128, D])
        self.bgt = sb("bgt", [4, 512], BF16)
        self.scT = sb("scT", [128, 8, 17], BF16)
        self.cTf = sb("cTf", [128, 8, 17])
        self.modT = sb("modT", [128, 32, 17])
        self.G1 = sb("G1", [128, 8, 17])
        self.G2 = sb("G2", [128, 8, 17])
        self.omm = sb("omm", [128, 14])
        self.epsc = sb("epsc", [128, 4])
        self.kk2 = sb("kk2", [128, 4])
        self.omka = sb("omka", [128, 4])
        self.lbv = sb("lbv", [128, 4])
        self.oml = sb("oml", [128, 4])
        self.wdec = sb("wdec", [128, 512], BF16)
        self.wgate = sb("wgate", [128, 512], BF16)
        self.ring = [sb("ring%d" % i, [128, 4096], BF16) for i in range(3)]
        self.tiles = []
        self.tile_issued = 0
        self.tile_got = 0
        S.dma("sp", self.vecF[:, :].ap, I["vecF"], writes=[self.vecF])
        S.dma("sp", self.cstf[:, :].ap, I["cst"][:, 0:576], writes=[self.cstf])
        S.dma("pool", self.cstb[:, :].ap, I["cst"], writes=[self.cstb])
        S.dma("sp", self.normf[:, :].ap, I["normf"], writes=[self.normf])
        S.dma("sp", self.cTf[:, :, :].ap, I["cT"], writes=[self.cTf])
        S.dma("pool", self.bgt[:, :].ap, I["bgt"].rearrange("o (g n) -> (o g) n", g=4), writes=[self.bgt])
        S.dma("pool", self.wdec[0:64, :].ap, I["w_dec"], writes=[self.wdec])
        S.dma("pool", self.wdec[64:128, :].ap, I["w_aaa"], writes=[self.wdec])
        S.dma("pool", self.wgate[:, :].ap, I["w_gate"], writes=[self.wgate])
        vf = self.vecF
        S.memset(self.epsc[:, 0:1], LNX_EPS)
        S.memset(self.epsc[:, 1:2], NORM_EPS)
        S.memset(self.epsc[:, 2:3], 1e-30)
        S.tt(self.kk2[:, :], vf[:, VC["k_k"]:VC["k_k"] + 4], vf[:, VC["k_k"]:VC["k_k"] + 4], ALU.mult)
        S.act(self.scT[:, :, :], self.cTf[:, :, :], AF.Silu)
        S.ts(self.omm[:, :], vf[:, VC["mu"]:VC["mu"] + 14], -1.0, ALU.mult, 1.0, ALU.add)
        S.ts(self.omka[:, :], vf[:, VC["k_a"]:VC["k_a"] + 4], -1.0, ALU.mult, 1.0, ALU.add)
        S.tt(self.oml[:, :], vf[:, VC["lb0"]:VC["lb0"] + 4], vf[:, VC["lb1"]:VC["lb1"] + 4], ALU.subtract)
        S.act(self.lbv[:, :], self.oml[:, :], AF.Sigmoid)
        S.ts(self.oml[:, :], self.lbv[:, :], -1.0, ALU.mult, 1.0, ALU.add)
        if self.lite:
            return
        fm_groups = [0, 1, 2, 3, 6, 7, 8, 9]
        for g in fm_groups:
            self.tiles.append(("cols", I["w_ada"], [g * 512 + j * 128 for j in range(4)]))
        for ph in range(2 if self.stage >= 2 else 1):
            for g in [4, 5, 10, 11]:
                self.tiles.append(("cols", I["w_ada"], [g * 512 + j * 128 for j in range(4)]))
            for blk in range(1 if ph == 0 else 4):
                for gi in range(0, 30, 4):
                    self.tiles.append(("cols", I["w_in"], [(gi + q_) * 128 for q_ in range(len(INPROJ_ORDER[gi:gi + 4]))]))
                for hf in range(2):
                    self.tiles.append(("cols", I["w_out"], [hf * 512 + j * 128 for j in range(4)]))
                for g in range(8):
                    self.tiles.append(("cols", I["w_up"], [g * 512 + j * 128 for j in range(4)]))
                    self.tiles.append(("rows", I["w_down"], g * 512))
        for gi, g in enumerate(fm_groups):
            wt = self.get_tile()
            ps = self.ps()
            for j in range(4):
                for k in range(8):
                    S.mm(ps.ap(0, 128, j * 32, [(1, 17)]), wt.ap(0, 128, k * 512 + j * 128, [(1, 128)]), self.scT[:, k, :],
                         start=(k == 0), stop=(k == 7), signal=(k == 7 and j == 3))
            for j in range(4):
                fc = g * 4 + j
                S.act(self.modT[:, gi * 4 + j, :], ps.ap(0, 128, j * 32, [(1, 17)]), AF.Identity,
                      bias=vf[:, VC["b_ada"] + fc:VC["b_ada"] + fc + 1], scale=1.0)
        for (G, nk, sc0) in [(self.G1, "norm1", 8), (self.G2, "norm2", 24)]:
            for k in range(8):
                S.ts(G[:, k, :], self.modT[:, sc0 + k, :], 1.0, ALU.add, vf[:, VC[nk] + k:VC[nk] + k + 1], ALU.mult)

    def issue_tile(self, idx):
        S = self.S
        kind, ap, arg = self.tiles[idx]
        wt = self.ring[idx % len(self.ring)]
        if kind == "cols":
            cols = arg
            j = 0
            while j < len(cols):
                n = 1
                while j + n < len(cols) and cols[j + n] == cols[j] + 128 * n:
                    n += 1
                src = ap[:, cols[j]:cols[j] + 128 * n].rearrange("(k p) c -> p k c", p=128)
                dst = wt.ap(0, 128, j * 128, [(512, 8), (1, 128 * n)])
                S.dma("pool", dst.ap, src, writes=[wt])
                j += n
        else:
            r0 = arg
            src = ap[r0:r0 + 512, :].rearrange("(c p) n -> p c n", p=128)
            dst = wt.ap(0, 128, 0, [(1024, 4), (1, 1024)])
            S.dma("pool", dst.ap, src, writes=[wt])

    def get_tile(self):
        i = self.tile_got
        self.tile_got += 1
        target = min(len(self.tiles), i + len(self.ring) - 1)
        while self.tile_issued < target:
            self.issue_tile(self.tile_issued)
            self.tile_issued += 1
        return self.ring[i % len(self.ring)]

    def run_phase(self, stack, sample):
        S, I, O = self.S, self.I, self.O
        vf = self.vecF
        sb = lambda n, s, d=F32: self.sb(stack, ("s_" if sample else "p_") + n, s, d)
        N = 64 if sample else 512
        C = 4 if sample else 64
        NCH = N // C
        HS = 64
        R = min(N, 128)
        NT = max(1, N // 128)
        nblk = 1 if sample else 4
        L = 1 if sample else 3
        idb = self.cstb
        idf = self.cstf

        def rows(fn):
            if not sample:
                fn(0, 128)
            else:
                fn(0, C)
                fn(HS, C)

        xbt = [sb("xb%d" % t_, [128, D]) for t_ in range(NT)]
        st8 = sb("st8", [128, 16])
        hT = sb("hT", [128, 8, N], BF16)
        mixT = Buf("mixT", hT.t)
        mixT.aliases = [hT]
        hT.aliases = [mixT]
        GT1 = sb("GT1", [128, D])
        GT2 = sb("GT2", [128, D])
        NH = N // 2
        NCHH = NCH // 2 if False else (N // 2) // (4 if sample else 64)
        TH = [[sb("T%d_%d" % (hf_, i), [128, N // 2]) for i in range(5)] for hf_ in range(2)]
        rkv = [sb("rkv%d" % i_, [128, 3, N]) for i_ in range(2)]
        wa = sb("wa", [128, N], BF16)
        sgd = sb("sgd", [128, N], BF16)
        notst = sb("notst", [128, N])
        self._tmp = [sb("tmpA", [128, 512]), sb("tmpB", [128, 512])]
        arena = stack.enter_context(self.nc.sbuf_tensor(("s_" if sample else "p_") + "arena", [128, max(28 * N, 2 * NT * D + 8 * N)], BF16))
        aoff = [0]

        def carve(name, nel_bf16, dt, pat=None, **kw):
            ap = arena[:, aoff[0]:aoff[0] + nel_bf16]
            aoff[0] += nel_bf16
            if dt == F32:
                ap = ap.bitcast(F32)
            if pat:
                ap = ap.rearrange(pat, **kw)
            return Buf(name, ap)

        AR = carve("AR", 8 * N, BF16, "p (a c t e) -> p a c t e", a=4, c=NCH, t=2)
        Bt = carve("Bt", 4 * N, BF16, "p (a n) -> p a n", a=4)
        Kt = carve("Kt", 4 * N, BF16, "p (a n) -> p a n", a=4)
        Vt = carve("Vt", 4 * N, BF16, "p (a n) -> p a n", a=4)
        gbf = carve("gbf", 4 * N, BF16, "p (a n) -> p a n", a=4)
        bonus = carve("bonus", 4 * N, BF16, "p (a n) -> p a n", a=4)
        aoff[0] = 0
        dacc = carve("dacc", 2 * NT * D, F32, "p (t d) -> p t d", t=NT)
        uT = [carve("uT%d" % i, 4 * N, BF16, "p (a n) -> p a n", a=4) for i in range(2)]
        mixer_bufs = [AR, Bt, Kt, Vt, gbf, bonus]
        mlp_bufs = [dacc] + uT
        for a in mixer_bufs:
            a.aliases = list(mlp_bufs)
        for a in mlp_bufs:
            a.aliases = list(mixer_bufs)
        gamR = sb("gamR", [128, 4, NCH])
        gamH = sb("gamH", [128, 4, NCH])
        sprev = sb("sprev", [128, 14, SB if sample else 1])
        arena2 = stack.enter_context(self.nc.sbuf_tensor(("s_" if sample else "p_") + "arena2", [128, max(NT * D, 8 * N)], BF16))
        xn = Buf("xn", arena2[:, 0:NT * D].rearrange("p (t d) -> p t d", t=NT))
        Qh = Buf("Qh", arena2[:, 0:4 * N].rearrange("p (a n) -> p a n", a=4))
        Kh = Buf("Kh", arena2[:, 4 * N:8 * N].rearrange("p (a n) -> p a n", a=4))
        screp = Buf("screp", arena2[:, 0:8 * R].rearrange("p (k r) -> p k r", k=8))
        xn.aliases = [Qh, Kh, screp]
        Qh.aliases = [xn, screp]
        Kh.aliases = [xn, screp]
        screp.aliases = [xn, Qh, Kh]
        Vh = sb("Vh", [128, 4, N], BF16)
        ogs = sb("ogs", [128, 4, N], BF16)
        qs = sb("qs", [128, N])
        NS = 3

        class TS:
            pass
        tsets = []
        for i_ in range(NS):
            t_ = TS()
            t_.AkRk = sb("AkRk%d" % i_, [128, 4, 2, C], BF16)
            t_.AbRb = sb("AbRb%d" % i_, [128, 4, 2, C], BF16)
            t_.Pm = [sb("Pm%d_%d" % (i_, q_), [128, 4, C], BF16) for q_ in range(2)]
            t_.PTm = [sb("PTm%d_%d" % (i_, q_), [128, 4, C], BF16) for q_ in range(2)]
            t_.Qm = sb("Qm%d" % i_, [128, 4, C], BF16)
            t_.TTm = [sb("TTm%d_%d" % (i_, q_), [128, 4, C], BF16) for q_ in range(2)]
            if not sample:
                hbn = [sb("HB%d_%d" % (i_, q_), [128, 4, 64], BF16) for q_ in range(3)]
                t_.HB = hbn + [t_.Pm[0], t_.Pm[1], t_.PTm[0], t_.PTm[1], t_.Qm]
            else:
                t_.HB = None
            t_.Btok = sb("Btok%d" % i_, [128, 4, 64], BF16)
            t_.Ktok = sb("Ktok%d" % i_, [128, 4, 64], BF16)
            t_.Vtok = sb("Vtok%d" % i_, [128, 4, 64], BF16)
            tsets.append(t_)
        OTc = sb("OTc", [128, 4, C])
        PT1 = sb("PT1", [128, 4, C])
        PT2 = sb("PT2", [128, 4, C])
        HO = sb("HO", [128, 4, C])
        HT1 = sb("HT1", [128, 4, C])
        if sample:
            psets = [(OTc, PT1, PT2), (sb("OTc2", [128, 4, C]), sb("PT12", [128, 4, C]), sb("PT22", [128, 4, C]))]
        else:
            def alias_view(name, base_buf, ap):
                b_ = Buf(name, ap)
                b_.aliases = [base_buf]
                base_buf.aliases = base_buf.aliases + [b_]
                return b_
            q3 = lambda lo: qs.t[:, lo:lo + 4 * C].rearrange("p (a b) -> p a b", a=4)
            w3 = wa.t[:, :].bitcast(F32).rearrange("p (a b) -> p a b", a=4)
            psets = [(OTc, PT1, PT2), (alias_view("OTc2", qs, q3(0)), alias_view("PT12", qs, q3(4 * C)), alias_view("PT22", wa, w3))]
        Wsb = sb("Wsb", [128, 4, 64], BF16)
        Usb = sb("Usb", [128, 4, 64], BF16)
        tH = sb("tH", [128, 4, 64])
        hKtok = sb("hKtok", [128, 4, 128], BF16)
        hVtok = sb("hVtok", [128, 4, 128], BF16)
        hAT = sb("hAT", [128, 4, C], BF16)
        tS = sb("tS", [128, 4, 128])
        if sample:
            Hst = sb("Hst", [128, 4, SB, 64])
            Hsb = sb("Hsb", [128, 4, SB, 64], BF16)
            Sld = sb("Sld", [128, 4, SB, 64])
            Shs = sb("Shs", [128, 4, SB, 128])
            Shb = sb("Shb", [128, 4, SB, 128], BF16)
        else:
            Hst = sb("Hst", [128, 4, 1, 64])
            Hsb = sb("Hsb", [128, 4, 1, 64], BF16)
            Sld = sb("Sld", [128, 4, 1, 64])
            Shs = sb("Shs", [128, 4, 1, 128])
            Shb = sb("Shb", [128, 4, 1, 128], BF16)

        S.memset(notst[:, :], 1.0)
        S.memset(notst.ap(0, 128, 0, [(C, NCH), (1, 1)]), 0.0)
        if sample:
            if self.stage < 0.12:
                return
            S.dma("sp", sprev[:, :, :].ap, I["sst"], writes=[sprev])
            if self.stage < 0.13:
                return
            for hp in range(4):
                S.dma("sp", Sld[:, hp, :, :].ap,
                      I["swkv"][:, 2 * hp:2 * hp + 2, :, :].rearrange("b h v k -> (h v) b k"), writes=[Sld])
            if self.stage < 0.14:
                return
            for h in range(4):
                S.dma("sp", Shs[:, h, :, :].ap, I["shg"][:, h, :, :].rearrange("b f i -> f b i"), writes=[Shs])
            if self.stage < 0.15:
                return
            S.copy(Shb[:, :, :, :], Shs[:, :, :, :], eng="pool")
            if self.stage < 0.16:
                return
            for hp in range(4):
                for b0 in range(0, SB, 8):
                    ps = self.ps()
                    for b in range(b0, b0 + 8):
                        for h in range(2):
                            S.mm(ps.ap(64 * h, 64, (b - b0) * 64, [(1, 64)]), Sld[64 * h:64 * h + 64, hp, b, :],
                                 idf[64 * h:64 * h + 64, C_IDS:C_IDS + 64], signal=(b == b0 + 7 and h == 1))
                    S.copy(Hst[:, hp, b0:b0 + 8, :], ps.ap(0, 128, 0, [(64, 8), (1, 64)]), eng="dve")
                    S.copy(Hsb[:, hp, b0:b0 + 8, :], ps.ap(0, 128, 0, [(64, 8), (1, 64)]), eng="act")
        else:
            S.memset(sprev[:, :, :], 0.0)
            S.memset(Hst[:, :, :, :], 0.0)
            S.memset(Hsb[:, :, :, :], 0.0)
            S.memset(Shs[:, :, :, :], 0.0)
            S.memset(Shb[:, :, :, :], 0.0)

        if self.stage < 0.2:
            return
        if sample:
            S.copy(screp.ap(0, 128, 0, [(R, 8), (4, 16), (1, 4)]), self.scT.ap(0, 128, 1, [(17, 8), (1, 16), (0, 4)]))
        else:
            S.copy(screp[:, :, :], self.scT.ap(0, 128, 0, [(17, 8), (0, 128)]))
        for gi, (GT, half) in enumerate([(GT1, 0), (GT1, 1), (GT2, 0), (GT2, 1)]):
            wt = self.get_tile()
            ps = self.ps()
            for k in range(8):
                S.mm(ps.full(R, 512), screp[:, k, :], wt.ap(0, 128, k * 512, [(1, 512)]), start=(k == 0), stop=False, signal=False)
            S.mm(ps.full(R, 512), idb.ap(0, 4, C_ID + gi, [(0, R)]), self.bgt[0:4, :], start=False, stop=True)
            S.copy(GT[0:R, half * 512:(half + 1) * 512], ps.full(R, 512), eng="act")

        def rmsnorm_T(G, sh0, dst):
            for t in range(NT):
                S.act(xn[0:R, t, :], xbt[t][0:R, :], AF.Square, accum=st8[0:R, t:t + 1])
            S.ts(st8[0:R, 4:4 + NT], st8[0:R, 0:NT], 1.0 / D, ALU.mult, NORM_EPS, ALU.add)
            S.act(st8[0:R, 4:4 + NT], st8[0:R, 4:4 + NT], AF.Sqrt)
            S.recip(st8[0:R, 8:8 + NT], st8[0:R, 4:4 + NT])
            for t in range(NT):
                S.ts(xn[0:R, t, :], xbt[t][0:R, :], st8[0:R, 8 + t:9 + t], ALU.mult)
            for k in range(8):
                ps = self.ps()
                for t in range(NT):
                    S.mm(ps.ap(0, 128, t * 128, [(1, R)]), xn[0:R, t, k * 128:(k + 1) * 128], idb[0:R, C_ID:C_ID + R],
                         signal=(t == NT - 1))
                if not sample:
                    S.act(dst[:, k, :], ps.full(128, N), AF.Identity, scale=G[:, k, 0:1], bias=self.modT[:, sh0 + k, 0:1])
                else:
                    S.tt(self._tmp[0].ap(0, 128, 0, [(4, 16), (1, 4)]), ps.ap(0, 128, 0, [(4, 16), (1, 4)]),
                         G.ap(0, 128, k * 17 + 1, [(1, 16), (0, 4)]), ALU.mult)
                    S.tt(dst.ap(0, 128, k * N, [(4, 16), (1, 4)]), self._tmp[0].ap(0, 128, 0, [(4, 16), (1, 4)]),
                         self.modT.ap(0, 128, (sh0 + k) * 17 + 1, [(1, 16), (0, 4)]), ALU.add)

        def tv(buf, off=0):
            return buf.ap(0, 128, off, [(C, NCH), (1, C)])

        def token_shift(c, ps, dst, dst_off):
            mu = vf[:, VC["mu"] + c:VC["mu"] + c + 1]
            p1 = dst.ap(0, 128, dst_off, [(1, N)])
            S.act(p1, ps.full(128, N), AF.Identity, scale=self.omm[:, c:c + 1])
            if not sample:
                S.stt(dst.ap(0, 128, dst_off, [(1, 1)]), sprev[:, c, 0:1], mu, dst.ap(0, 128, dst_off, [(1, 1)]), ALU.mult, ALU.add)
                S.stt(dst.ap(0, 128, dst_off + 1, [(1, N - 1)]), ps.ap(0, 128, 0, [(1, N - 1)]), mu,
                      dst.ap(0, 128, dst_off + 1, [(1, N - 1)]), ALU.mult, ALU.add)
                S.copy(sprev[:, c, 0:1], ps.ap(0, 128, N - 1, [(1, 1)]), eng="act")
            else:
                S.stt(dst.ap(0, 128, dst_off, [(4, 16), (1, 1)]), sprev.ap(0, 128, c * SB, [(1, 16), (1, 1)]), mu,
                      dst.ap(0, 128, dst_off, [(4, 16), (1, 1)]), ALU.mult, ALU.add)
                S.stt(dst.ap(0, 128, dst_off + 1, [(4, 16), (1, 3)]), ps.ap(0, 128, 0, [(4, 16), (1, 3)]), mu,
                      dst.ap(0, 128, dst_off + 1, [(4, 16), (1, 3)]), ALU.mult, ALU.add)
                S.copy(sprev.ap(0, 128, c * SB, [(1, 16), (1, 1)]), ps.ap(0, 128, 3, [(4, 16), (1, 1)]), eng="act")

        if sample:
            TG = [[sb("TG%d_%d" % (hf_, i), [128, N // 2]) for i in range(3)] for hf_ in range(2)]
        else:
            mkv = lambda b_: Buf(b_.name + "_v", b_.t[:, :, :].rearrange("p a b -> p (a b)"))
            TG = [[mkv(PT1), mkv(PT2), mkv(OTc)], [mkv(HO), mkv(HT1), mkv(tH)]]
            for (v_, o_) in zip(TG[0] + TG[1], [PT1, PT2, OTc, HO, HT1, tH]):
                v_.aliases = [o_]
                o_.aliases = [v_]

        def run_rr(gens):
            gens = list(gens)
            while gens:
                for g_ in list(gens):
                    try:
                        next(g_)
                    except StopIteration:
                        gens.remove(g_)

        def hv(buf, off, hf):
            return buf.ap(0, 128, off + hf * NH, [(1, NH)])

        def hvc(buf, off, hf):
            return buf.ap(0, 128, off + hf * NH, [(C, NCHH), (1, C)])

        def rwkv_prep(j, rk, hf):
            T = TH[hf]
            r = hv(rk, 0, hf)
            k = hv(rk, N, hf)
            v = hv(rk, 2 * N, hf)
            jc = slice(j * 128, (j + 1) * 128)
            col = lambda n: vf[:, VC[n] + j:VC[n] + j + 1]
            tvh = lambda b_: b_.ap(0, 128, 0, [(C, NCHH), (1, C)])
            pfull = lambda p_: p_.ap(0, 128, 0, [(1, NH)])
            ps1 = self.ps()
            S.mm(pfull(ps1), self.wdec[0:64, jc], hv(wa, 0, hf)[0:64])
            S.act(T[0][:, :], pfull(ps1), AF.Sigmoid, bias=col("w0"), scale=1.0)
            yield
            ps2 = self.ps()
            S.mm(pfull(ps2), self.wdec[64:128, jc], hv(wa, 0, hf)[64:128])
            S.act(T[1][:, :], pfull(ps2), AF.Sigmoid, bias=col("a0"), scale=1.0)
            yield
            ps3 = self.ps()
            S.mm(pfull(ps3), self.wgate[:, jc], hv(sgd, 0, hf))
            S.copy(hv(gbf, j * N, hf), pfull(ps3), eng="act")
            yield
            S.scan(T[2][:, :], hv(notst, 0, hf), T[0][:, :], 0.0, ALU.mult, ALU.add)
            yield
            S.tt(T[0][:, :], T[2][:, :], T[0][:, :], ALU.subtract)
            S.act(T[3][:, :], T[2][:, :], AF.Exp, scale=-WDEC)
            yield
            S.act(T[4][:, :], T[2][:, :], AF.Exp, scale=WDEC)
            S.act(T[0][:, :], T[0][:, :], AF.Exp, scale=-WDEC)
            yield
            S.copy(gamR.ap(0, 128, j * NCH + hf * NCHH, [(1, NCHH)]), T[3].ap(0, 128, C - 1, [(C, NCHH)]), eng="pool")
            S.stt(T[2][:, :], k, self.kk2[:, j:j + 1], k, ALU.mult, ALU.mult)
            yield
            ps4 = self.ps()
            S.mm(pfull(ps4), idf[:, C_BONE:C_BONE + 128], T[2][:, :])
            S.act(T[2][:, :], pfull(ps4), AF.Ln, bias=self.epsc[:, 2:3], scale=1.0)
            yield
            S.act(T[2][:, :], T[2][:, :], AF.Exp, scale=-0.5)
            yield
            S.stt(T[2][:, :], k, col("k_k"), T[2][:, :], ALU.mult, ALU.mult)
            yield
            S.stt(AR.ap(0, 128, j * 2 * N + hf * NCHH * 2 * C, [(2 * C, NCHH), (1, C)]), tvh(T[2]), -1.0, tvh(T[0]), ALU.mult, ALU.mult)
            S.tt(T[0][:, :], T[2][:, :], T[1][:, :], ALU.mult, eng="pool")
            yield
            S.tt(hv(Bt, j * N, hf), T[0][:, :], T[4][:, :], ALU.mult, eng="pool")
            S.ts(T[2][:, :], T[1][:, :], col("k_a"), ALU.mult, self.omka[:, j:j + 1], ALU.add)
            yield
            S.tt(T[2][:, :], k, T[2][:, :], ALU.mult)
            yield
            S.tt(hv(Kt, j * N, hf), T[2][:, :], T[4][:, :], ALU.mult, eng="pool")
            S.tt(AR.ap(0, 128, j * 2 * N + hf * NCHH * 2 * C + C, [(2 * C, NCHH), (1, C)]), hvc(rk, 0, hf), tvh(T[3]), ALU.mult)
            S.copy(hv(Vt, j * N, hf), v, eng="act")
            yield
            S.stt(T[1][:, :], r, col("r_k"), T[2][:, :], ALU.mult, ALU.mult)
            yield
            ps5 = self.ps()
            S.mm(pfull(ps5), idf[:, C_BONE:C_BONE + 128], T[1][:, :])
            S.tt(hv(bonus, j * N, hf), pfull(ps5), v, ALU.mult)

        def hgrn_prep(h, ps, hf):
            T = TG[hf]
            pfh = ps.ap(0, 128, hf * NH, [(1, NH)])
            S.act(T[0][:, :], pfh, AF.Sigmoid)
            yield
            S.ts(T[1][:, :], T[0][:, :], self.oml[:, h:h + 1], ALU.mult, self.lbv[:, h:h + 1], ALU.add)
            yield
            S.ts(T[0][:, :], T[1][:, :], -1.0, ALU.mult, 1.0, ALU.add, eng="pool")
            S.act(T[1][:, :], T[1][:, :], AF.Ln)
            yield
            S.scan(T[2][:, :], hv(notst, 0, hf), T[1][:, :], 0.0, ALU.mult, ALU.add)
            yield
            S.act(T[1][:, :], T[2][:, :], AF.Exp)
            S.act(T[2][:, :], T[2][:, :], AF.Exp, scale=-1.0)
            yield
            S.tt(hv(Qh, h * N, hf), hv(qs, 0, hf), T[1][:, :], ALU.mult)
            S.tt(hv(Kh, h * N, hf), T[0][:, :], T[2][:, :], ALU.mult, eng="pool")
            S.copy(gamH.ap(0, 128, h * NCH + hf * NCHH, [(1, NCHH)]), T[1].ap(0, 128, C - 1, [(C, NCHH)]), eng="pool")

        active_rw = []
        active_hg = []

        def post_inproj(c, ps):
            if c < 12:
                j = c % 4
                token_shift(c, ps, rkv[j % 2], (c // 4) * N)
                if c // 4 == 2:
                    active_rw.extend([rwkv_prep(j, rkv[j % 2], 0), rwkv_prep(j, rkv[j % 2], 1)])
            elif c == 12:
                token_shift(c, ps, self._tmp[0], 0)
                S.act(wa[0:64, :], self._tmp[0][0:64, 0:N], AF.Tanh)
                S.copy(wa[64:128, :], self._tmp[0][64:128, 0:N], eng="act")
            elif c == 13:
                token_shift(c, ps, self._tmp[1], 0)
                S.act(sgd[:, :], self._tmp[1][:, 0:N], AF.Sigmoid)
            elif c < 18:
                S.act(qs[:, :], ps.full(128, N), AF.Silu)
            elif c < 22:
                gs_ = [hgrn_prep(c - 18, ps, 0), hgrn_prep(c - 18, ps, 1)]
                for g_ in gs_:
                    next(g_)
                active_hg.extend(gs_)
            elif c < 26:
                S.copy(Vh[:, c - 22, :], ps.full(128, N), eng="act")
            else:
                S.act(ogs[:, c - 26, :], ps.full(128, N), AF.Silu)

        m1c = idb.ap(0, 128, C_M1S if sample else C_M1, [(0, 4), (1, 2 * C)])
        msc = idb.ap(0, 128, C_MSS if sample else C_BD, [(0, 4), (1, C)])
        m2c = m1c if sample else idb.ap(0, 128, C_M2H, [(0, 4), (1, 2 * C)])
        idcc = idb.ap(0, 128, C_IDS, [(0, 4), (1, C)]) if not sample else None

        def idview(r0, nr):
            if sample:
                return idb.ap(r0, nr, C_ID + r0, [(0, 4), (1, C)])
            return View(idcc.buf, idcc.ap[r0:r0 + nr])

        def getbanks(n):
            while 8 - len(self.ps_open) < n:
                yield
            return [self.ps(hold=True) for _ in range(n)]

        post_gens = {}

        def hmm(out_ps, lhs, rhs, w=C, kb=None):
            for hp in range(4):
                for h in range(2):
                    ob = HS * h
                    S.mm(out_ps.ap(ob, C, hp * w, [(1, w)]), lhs(ob, hp), rhs(ob, hp), signal=(hp == 3 and h == 1))

        def rwkv_inv_gen(c, ts):
            cs = slice(c * C, (c + 1) * C)
            AkRk, AbRb, Pm, PTm, Qm, TTm = ts.AkRk, ts.AbRb, ts.Pm, ts.PTm, ts.Qm, ts.TTm
            (psA,) = yield from getbanks(1)
            for hp in range(4):
                for h in range(2):
                    rb, ob = 64 * h, HS * h
                    arv = AR.ap(rb, 64, (hp * NCH + c) * 2 * C, [(1, 2 * C)])
                    S.mm(psA.ap(ob, C, hp * 2 * C, [(1, 2 * C)]), Kt[rb:rb + 64, hp, cs], arv, signal=(hp == 3 and h == 1))
            yield
            rows(lambda r0, nr: S.tt(AkRk.ap(r0, nr, 0, [(2 * C, 4), (1, 2 * C)]), psA.ap(r0, nr, 0, [(2 * C, 4), (1, 2 * C)]),
                                     View(m1c.buf, m1c.ap[r0:r0 + nr]), ALU.mult))
            psA.done()
            psB, psC = yield from getbanks(2)
            for hp in range(4):
                for h in range(2):
                    rb, ob = 64 * h, HS * h
                    last = (hp == 3 and h == 1)
                    arv = AR.ap(rb, 64, (hp * NCH + c) * 2 * C, [(1, 2 * C)])
                    S.mm(psB.ap(ob, C, hp * 2 * C, [(1, 2 * C)]), Bt[rb:rb + 64, hp, cs], arv, signal=last)
                    S.mm(psC.ap(ob, C, hp * C, [(1, C)]), AR.ap(rb, 64, (hp * NCH + c) * 2 * C, [(1, C)]),
                         Bt[rb:rb + 64, hp, cs], signal=last)
            yield

            def ev(r0, nr):
                S.tt(AbRb.ap(r0, nr, 0, [(2 * C, 4), (1, 2 * C)]), psB.ap(r0, nr, 0, [(2 * C, 4), (1, 2 * C)]),
                     View(m2c.buf, m2c.ap[r0:r0 + nr]), ALU.mult)
                S.tt(Pm[0].ap(r0, nr, 0, [(C, 4), (1, C)]), psC.ap(r0, nr, 0, [(C, 4), (1, C)]),
                     View(msc.buf, msc.ap[r0:r0 + nr]), ALU.mult)
                S.tt(TTm[0].ap(r0, nr, 0, [(C, 4), (1, C)]), AbRb.ap(r0, nr, 0, [(2 * C, 4), (1, C)]), idview(r0, nr), ALU.add, eng="pool")
            rows(ev)
            if not sample:
                mk = lambda off: idb.ap(0, 128, off, [(0, 4), (1, C)])
                HB = ts.HB
                S.tt(HB[0][:, :, :], psC.ap(0, 128, 0, [(C, 4), (1, C)]), mk(C_N0), ALU.mult)
                S.tt(HB[1][:, :, :], psB.ap(0, 128, 0, [(2 * C, 4), (1, C)]), mk(C_N0T), ALU.mult)
                S.tt(HB[2][:, :, :], psC.ap(0, 128, 0, [(C, 4), (1, C)]), mk(C_N1), ALU.mult)
            psB.done()
            psC.done()
            yield
            Pc = Pm[0]
            PTc = lambda ob, hp: AbRb.ap(ob, C, hp * 2 * C, [(1, C)])
            TTc = TTm[0]
            for l in range(1, L + 1):
                if l < L:
                    psP, psPT = yield from getbanks(2)
                else:
                    (psP,) = yield from getbanks(1)
                    psPT = None
                hmm(psP, PTc, lambda ob, hp, Pc=Pc: Pc.ap(ob, C, hp * C, [(1, C)]))
                if l < L:
                    hmm(psPT, lambda ob, hp, Pc=Pc: Pc.ap(ob, C, hp * C, [(1, C)]), PTc)
                yield
                Pn, PTn = Pm[l % 2], PTm[l % 2]

                def ev2(r0, nr):
                    S.tt(Qm.ap(r0, nr, 0, [(C, 4), (1, C)]), psP.ap(r0, nr, 0, [(C, 4), (1, C)]), idview(r0, nr), ALU.add)
                    if l < L:
                        S.copy(Pn.ap(r0, nr, 0, [(C, 4), (1, C)]), psP.ap(r0, nr, 0, [(C, 4), (1, C)]), eng="act")
                        S.copy(PTn.ap(r0, nr, 0, [(C, 4), (1, C)]), psPT.ap(r0, nr, 0, [(C, 4), (1, C)]), eng="act")
                rows(ev2)
                psP.done()
                if psPT is not None:
                    psPT.done()
                (psT,) = yield from getbanks(1)
                hmm(psT, lambda ob, hp: Qm.ap(ob, C, hp * C, [(1, C)]), lambda ob, hp, TTc=TTc: TTc.ap(ob, C, hp * C, [(1, C)]))
                yield
                TTn = TTm[l % 2]
                rows(lambda r0, nr: S.copy(TTn.ap(r0, nr, 0, [(C, 4), (1, C)]), psT.ap(r0, nr, 0, [(C, 4), (1, C)]), eng="act"))
                psT.done()
                Pc = Pn
                PTc = (lambda PTn: (lambda ob, hp: PTn.ap(ob, C, hp * C, [(1, C)])))(PTn)
                TTc = TTn
            pss = []
            pbk, pv_ = yield from getbanks(2)
            for (src, dstb, pst_, co) in [(Bt, ts.Btok, pbk, 0), (Kt, ts.Ktok, pbk, 256), (Vt, ts.Vtok, pv_, 0)]:
                for hp in range(4):
                    for h in range(2):
                        rb, ob = 64 * h, HS * h
                        S.mm(pst_.ap(ob, C, co + hp * 64, [(1, 64)]), src[rb:rb + 64, hp, cs], idb[rb:rb + 64, C_IDS:C_IDS + 64],
                             signal=(hp == 3 and h == 1))
                pss.append((pst_, dstb, co))
            yield
            for pst_, dstb, co in pss:
                rows(lambda r0, nr: S.copy(dstb.ap(r0, nr, 0, [(64, 4), (1, 64)]), pst_.ap(r0, nr, co, [(64, 4), (1, 64)]), eng="act"))
            pbk.done()
            pv_.done()
            if not sample:
                HB = ts.HB
                full = lambda b_: b_.ap(0, 128, 0, [(C, 4), (1, C)])
                pfull = lambda p_: p_.ap(0, 128, 0, [(C, 4), (1, C)])
                bl = lambda b_: (lambda ob, hp: b_.ap(ob, C, hp * C, [(1, C)]))
                N0, N0T, N1, D0, X, D1, X2, D1T = HB
                D0T = TTc
                (p_,) = yield from getbanks(1)
                for hp in range(4):
                    for h in range(2):
                        rb = 64 * h
                        S.mm(p_.ap(rb, 64, hp * C, [(1, C)]), D0T.ap(rb, 64, hp * C, [(1, C)]), idb[rb:rb + 64, C_IDS:C_IDS + 64],
                             signal=(hp == 3 and h == 1))
                yield
                S.copy(full(D0), pfull(p_), eng="act")
                p_.done()
                p_, p2_ = yield from getbanks(2)
                hmm(p_, bl(N0T), bl(D0))
                hmm(p2_, bl(N0), bl(D0T))
                yield
                S.copy(full(X), pfull(p_), eng="act")
                S.copy(full(X2), pfull(p2_), eng="act")
                p_.done()
                p2_.done()
                p_, p2_ = yield from getbanks(2)
                hmm(p_, bl(D0T), bl(X))
                hmm(p2_, bl(D0), bl(X2))
                yield
                S.tt(full(D1), pfull(p_), full(D0), ALU.add)
                S.tt(full(D1T), pfull(p2_), full(D0T), ALU.add)
                p_.done()
                p2_.done()
                (p_,) = yield from getbanks(1)
                hmm(p_, bl(N1), bl(D1T))
                yield
                S.copy(full(X), pfull(p_), eng="act")
                p_.done()
                (p_,) = yield from getbanks(1)
                hmm(p_, bl(D1), bl(X))
                yield
                TTf = TTm[0] if TTc is TTm[1] else TTm[1]
                S.tt(full(TTf), pfull(p_), full(D1T), ALU.add)
                p_.done()
                TTc = TTf
            ts.TTfin = TTc

        def rwkv_chain_gen(c, ts):
            sidx = c if sample else 0
            cs = slice(c * C, (c + 1) * C)
            AkRk, AbRb, Btok, Ktok, Vtok, TTc = ts.AkRk, ts.AbRb, ts.Btok, ts.Ktok, ts.Vtok, ts.TTfin
            (psW,) = yield from getbanks(1)
            for hp in range(4):
                for h in range(2):
                    rb, ob = 64 * h, HS * h
                    S.mm(psW.ap(ob, C, hp * 64, [(1, 64)]), AR.ap(rb, 64, (hp * NCH + c) * 2 * C, [(1, C)]),
                         Hsb[rb:rb + 64, hp, sidx, :], start=True, stop=False, signal=False)
                    S.mm(psW.ap(ob, C, hp * 64, [(1, 64)]), AkRk.ap(ob, C, hp * 2 * C, [(1, C)]),
                         Vtok.ap(ob, C, hp * 64, [(1, 64)]), start=False, stop=True, signal=(hp == 3 and h == 1))
            yield
            rows(lambda r0, nr: S.copy(Wsb.ap(r0, nr, 0, [(64, 4), (1, 64)]), psW.ap(r0, nr, 0, [(64, 4), (1, 64)]), eng="act"))
            psW.done()
            (psU,) = yield from getbanks(1)
            hmm(psU, lambda ob, hp: TTc.ap(ob, C, hp * C, [(1, C)]), lambda ob, hp: Wsb.ap(ob, C, hp * 64, [(1, 64)]), w=64)
            yield
            rows(lambda r0, nr: S.copy(Usb.ap(r0, nr, 0, [(64, 4), (1, 64)]), psU.ap(r0, nr, 0, [(64, 4), (1, 64)]), eng="act"))
            psU.done()
            psO, psH = yield from getbanks(2)
            for hp in range(4):
                for h in range(2):
                    rb, ob = 64 * h, HS * h
                    last = (hp == 3 and h == 1)
                    S.mm(psO.ap(rb, 64, hp * C, [(1, C)]), Hsb[rb:rb + 64, hp, sidx, :],
                         AR.ap(rb, 64, (hp * NCH + c) * 2 * C + C, [(1, C)]), start=True, stop=False, signal=False)
                    S.mm(psO.ap(rb, 64, hp * C, [(1, C)]), Usb.ap(ob, C, hp * 64, [(1, 64)]),
                         AbRb.ap(ob, C, hp * 2 * C + C, [(1, C)]), start=False, stop=False, signal=False)
                    S.mm(psO.ap(rb, 64, hp * C, [(1, C)]), Vtok.ap(ob, C, hp * 64, [(1, 64)]),
                         AkRk.ap(ob, C, hp * 2 * C + C, [(1, C)]), start=False, stop=True, signal=last)
                    S.mm(psH.ap(rb, 64, hp * 64, [(1, 64)]), Btok.ap(ob, C, hp * 64, [(1, 64)]),
                         Usb.ap(ob, C, hp * 64, [(1, 64)]), start=True, stop=False, signal=False)
                    S.mm(psH.ap(rb, 64, hp * 64, [(1, 64)]), Ktok.ap(ob, C, hp * 64, [(1, 64)]),
                         Vtok.ap(ob, C, hp * 64, [(1, 64)]), start=False, stop=True, signal=last)
            yield
            hview = Hst.ap(0, 128, sidx * 64, [(Hst.pstride // 4, 4), (1, 64)])
            hbview = Hsb.ap(0, 128, sidx * 64, [(Hsb.pstride // 4, 4), (1, 64)])
            S.tt(tH[:, :, :], psH.ap(0, 128, 0, [(64, 4), (1, 64)]), hview, ALU.add)
            gv = gamR.ap(0, 128, c, [(NCH, 4), (0, 64)])
            S.tt(hbview, tH[:, :, :], gv, ALU.mult, eng="pool")
            S.tt(hview, tH[:, :, :], gv, ALU.mult, eng="pool")
            OTc, PT1, PT2 = psets[c % 2]
            S.copy(OTc[:, :, :], psO.ap(0, 128, 0, [(C, 4), (1, C)]), eng="act")
            psO.done()
            psH.done()
            post_gens[c] = rwkv_post_gen(c)

        def rwkv_post_gen(c):
            OTc, PT1, PT2 = psets[c % 2]
            flat = lambda b_: b_.ap(0, 128, 0, [(1, 4 * C)])
            (ps1,) = yield from getbanks(1)
            S.mm(ps1.ap(0, 128, 0, [(1, 4 * C)]), idf[:, C_BMEAN:C_BMEAN + 128], flat(OTc))
            yield
            S.tt(flat(PT1), flat(OTc), ps1.ap(0, 128, 0, [(1, 4 * C)]), ALU.subtract)
            ps1.done()
            S.tt(flat(PT2), flat(PT1), flat(PT1), ALU.mult, eng="pool")
            (ps2,) = yield from getbanks(1)
            S.mm(ps2.ap(0, 128, 0, [(1, 4 * C)]), idf[:, C_BMEAN:C_BMEAN + 128], flat(PT2))
            yield
            S.act(flat(PT2), ps2.ap(0, 128, 0, [(1, 4 * C)]), AF.Ln, bias=self.epsc[:, 0:1], scale=1.0)
            ps2.done()
            S.act(flat(PT2), flat(PT2), AF.Exp, scale=-0.5)
            S.tt(flat(PT1), flat(PT1), flat(PT2), ALU.mult, eng="pool")
            lw = vf.ap(0, 128, VC["lnx_w"], [(1, 4), (0, C)])
            lb_ = vf.ap(0, 128, VC["lnx_b"], [(1, 4), (0, C)])
            S.tt(PT1[:, :, :], PT1[:, :, :], lw, ALU.mult, eng="pool")
            S.tt(PT1[:, :, :], PT1[:, :, :], lb_, ALU.add, eng="pool")
            S.tt(PT1[:, :, :], PT1[:, :, :], bonus.ap(0, 128, c * C, [(N, 4), (1, C)]), ALU.add, eng="pool")
            S.tt(mixT.ap(0, 128, c * C, [(N, 4), (1, C)]), PT1[:, :, :], gbf.ap(0, 128, c * C, [(N, 4), (1, C)]), ALU.mult, eng="pool")

        def hgrn_gen():
            mi = idb.ap(0, C, (C_M1S + 4) if sample else (C_M1 + 64), [(0, 4), (1, C)])
            for c in range(NCH):
                sidx = c if sample else 0
                cs = slice(c * C, (c + 1) * C)
                psK, psV = yield from getbanks(2)
                for h in range(4):
                    S.mm(psK.ap(0, C, h * 128, [(1, 128)]), Kh[:, h, cs], idb[:, C_ID:C_ID + 128], signal=(h == 3))
                    S.mm(psV.ap(0, C, h * 128, [(1, 128)]), Vh[:, h, cs], idb[:, C_ID:C_ID + 128], signal=(h == 3))
                yield
                S.copy(hKtok[0:C, :, :], psK.ap(0, C, 0, [(128, 4), (1, 128)]), eng="act")
                S.copy(hVtok[0:C, :, :], psV.ap(0, C, 0, [(128, 4), (1, 128)]), eng="act")
                psK.done()
                psV.done()
                psA, psG = yield from getbanks(2)
                for h in range(4):
                    S.mm(psA.ap(0, C, h * C, [(1, C)]), Kh[:, h, cs], Qh[:, h, cs], signal=(h == 3))
                for h in range(4):
                    S.mm(psG.ap(0, 128, h * 128, [(1, 128)]), hKtok[0:C, h, :], hVtok[0:C, h, :], signal=(h == 3))
                yield
                S.tt(hAT[0:C, :, :], psA.ap(0, C, 0, [(C, 4), (1, C)]), mi, ALU.mult)
                psA.done()
                sview = Shs.ap(0, 128, sidx * 128, [(Shs.pstride // 4, 4), (1, 128)])
                sbview = Shb.ap(0, 128, sidx * 128, [(Shb.pstride // 4, 4), (1, 128)])
                S.tt(tS[:, :, :], psG.ap(0, 128, 0, [(128, 4), (1, 128)]), sview, ALU.add)
                psG.done()
                (psO,) = yield from getbanks(1)
                for h in range(4):
                    S.mm(psO.ap(0, 128, h * C, [(1, C)]), Shb[:, h, sidx, :], Qh[:, h, cs], start=(h == 0), stop=False, signal=False)
                for h in range(4):
                    S.mm(psO.ap(0, 128, h * C, [(1, C)]), hVtok[0:C, h, :], hAT[0:C, h, :], start=False, stop=True, signal=(h == 3))
                gv = gamH.ap(0, 128, c, [(NCH, 4), (0, 128)])
                S.tt(sbview, tS[:, :, :], gv, ALU.mult, eng="pool")
                S.tt(sview, tS[:, :, :], gv, ALU.mult, eng="pool")
                yield
                flat = lambda b_: b_.ap(0, 128, 0, [(1, 4 * C)])
                S.copy(HO[:, :, :], psO.ap(0, 128, 0, [(C, 4), (1, C)]), eng="act")
                psO.done()
                S.tt(flat(HT1), flat(HO), flat(HO), ALU.mult, eng="pool")
                (ps1,) = yield from getbanks(1)
                S.mm(ps1.ap(0, 128, 0, [(1, 4 * C)]), idf[:, C_AMEAN:C_AMEAN + 128], flat(HT1))
                yield
                S.act(flat(HT1), ps1.ap(0, 128, 0, [(1, 4 * C)]), AF.Ln, bias=self.epsc[:, 1:2], scale=1.0)
                ps1.done()
                S.act(flat(HT1), flat(HT1), AF.Exp, scale=-0.5)
                S.tt(flat(HT1), flat(HT1), flat(HO), ALU.mult, eng="pool")
                S.stt(mixT.ap(0, 128, 4 * N + c * C, [(N, 4), (1, C)]), HT1[:, :, :], vf[:, VC["gnorm"]:VC["gnorm"] + 1],
                      ogs.ap(0, 128, c * C, [(N, 4), (1, C)]), ALU.mult, ALU.mult)

        def run_units():
            free = list(range(NS))
            inv = {}
            inv_done = {}
            chain = None
            next_inv = 0
            next_chain = 0
            hg = hgrn_gen()
            hg_alive = True
            while next_chain < NCH or chain is not None or hg_alive or post_gens:
                while free and next_inv < NCH:
                    si = free.pop(0)
                    inv[next_inv] = (rwkv_inv_gen(next_inv, tsets[si]), si)
                    next_inv += 1
                if chain is None and next_chain in inv_done and (next_chain - 2) not in post_gens:
                    si = inv_done.pop(next_chain)
                    chain = (rwkv_chain_gen(next_chain, tsets[si]), next_chain, si)
                for cc in sorted(list(inv.keys())):
                    g_, si = inv[cc]
                    try:
                        next(g_)
                    except StopIteration:
                        del inv[cc]
                        inv_done[cc] = si
                for cc in sorted(list(post_gens.keys())):
                    try:
                        next(post_gens[cc])
                    except StopIteration:
                        del post_gens[cc]
                if hg_alive:
                    try:
                        next(hg)
                    except StopIteration:
                        hg_alive = False
                if chain is not None:
                    try:
                        next(chain[0])
                    except StopIteration:
                        free.append(chain[2])
                        next_chain += 1
                        chain = None

        if self.stage < 0.3:
            return
        for blk in range(nblk):
            xsrc = I["xs"] if sample else I["xp"][blk * 512:(blk + 1) * 512, :]
            ydst = O["ys"] if sample else O["yp"][blk * 512:(blk + 1) * 512, :]
            for t in range(NT):
                S.dma("sp", xbt[t][0:R, :].ap, xsrc[t * 128:t * 128 + R, :], writes=[xbt[t]])
            S.mark(('s' if sample else 'p') + str(blk) + ':norm1')
            rmsnorm_T(self.G1, 0, hT)
            S.mark(('s' if sample else 'p') + str(blk) + ':inproj')
            if self.stage < 0.35:
                return
            def inproj_stream():
                for gi in range(0, 30, 4):
                    chunks = INPROJ_ORDER[gi:gi + 4]
                    wt = self.get_tile()
                    for j, c in enumerate(chunks):
                        if c < 12 and c // 4 == 2:
                            while active_rw:
                                yield
                        if 14 <= c < 22:
                            while active_hg:
                                yield
                        ps = self.ps()
                        for k in range(8):
                            S.mm(ps.full(128, N), wt.ap(0, 128, k * 512 + j * 128, [(1, 128)]), hT[:, k, :], start=(k == 0), stop=(k == 7))
                        post_inproj(c, ps)
                        yield

            ip_ = inproj_stream()
            ip_alive = True
            while ip_alive or active_rw or active_hg:
                if ip_alive:
                    try:
                        next(ip_)
                    except StopIteration:
                        ip_alive = False
                for lst_ in (active_rw, active_hg):
                    for g_ in list(lst_):
                        try:
                            next(g_)
                        except StopIteration:
                            lst_.remove(g_)
            if self.stage < 0.4:
                return
            S.mark(('s' if sample else 'p') + str(blk) + ':units')
            run_units()
            S.mark(('s' if sample else 'p') + str(blk) + ':outproj')
            if sample and self.debug:
                dtmp = sb('dtmp', [128, 8 * N])
                S.copy(dtmp[:, :], mixT.ap(0, 128, 0, [(1, 8 * N)]))
                self.dbg(dtmp[:, :], 8 * N)
            if self.stage < 0.6:
                return
            for hf in range(2):
                wt = self.get_tile()
                for t in range(NT):
                    ps = self.ps()
                    for k in range(8):
                        S.mm(ps.full(R, 512), mixT[:, k, t * 128:t * 128 + R], wt.ap(0, 128, k * 512, [(1, 512)]),
                             start=(k == 0), stop=(k == 7))
                    tmp = self.tmp512(t)
                    S.tt(tmp[0:R, :], ps.full(R, 512), GT1[0:R, hf * 512:(hf + 1) * 512], ALU.mult)
                    S.tt(xbt[t][0:R, hf * 512:(hf + 1) * 512], xbt[t][0:R, hf * 512:(hf + 1) * 512], tmp[0:R, :], ALU.add, eng="pool")
            if sample and self.debug:
                self.dbg(xbt[0][0:R, :], 1024, R)
            if self.stage < 0.7:
                return
            S.mark(('s' if sample else 'p') + str(blk) + ':norm2')
            rmsnorm_T(self.G2, 16, hT)
            S.mark(('s' if sample else 'p') + str(blk) + ':mlp')
            if sample and self.debug:
                S.copy(dtmp[:, :], hT.ap(0, 128, 0, [(1, 8 * N)]))
                self.dbg(dtmp[:, :], 8 * N)
            for g in range(8):
                wu = self.get_tile()
                wd = self.get_tile()
                u = uT[g % 2]
                for j in range(4):
                    ps = self.ps()
                    for k in range(8):
                        S.mm(ps.full(128, N), wu.ap(0, 128, k * 512 + j * 128, [(1, 128)]), hT[:, k, :], start=(k == 0), stop=(k == 7))
                    tmp = self.tmp512(j)
                    S.act(tmp.ap(0, 128, 0, [(1, N)]), ps.full(128, N), AF.Relu)
                    S.tt(u[:, j, :], tmp.ap(0, 128, 0, [(1, N)]), tmp.ap(0, 128, 0, [(1, N)]), ALU.mult, eng="pool")
                for t in range(NT):
                    for hf in range(2):
                        ps = self.ps()
                        for j in range(4):
                            S.mm(ps.full(R, 512), u[:, j, t * 128:t * 128 + R], wd.ap(0, 128, j * 1024 + hf * 512, [(1, 512)]),
                                 start=(j == 0), stop=(j == 3))
                        dv = dacc[0:R, t, hf * 512:(hf + 1) * 512]
                        if g == 0:
                            S.copy(dv, ps.full(R, 512), eng="act")
                        else:
                            S.tt(dv, ps.full(R, 512), dv, ALU.add)
            if sample and self.debug:
                self.dbg(dacc[0:R, 0, :], 1024, R)
                self.dbg(GT1[0:R, :], 1024, R)
                self.dbg(GT2[0:R, :], 1024, R)
            S.mark(('s' if sample else 'p') + str(blk) + ':final')
            for t in range(NT):
                for hf in range(2):
                    tmp = self.tmp512(hf)
                    S.tt(tmp[0:R, :], dacc[0:R, t, hf * 512:(hf + 1) * 512], GT2[0:R, hf * 512:(hf + 1) * 512], ALU.mult, eng="pool")
                    S.tt(xbt[t][0:R, hf * 512:(hf + 1) * 512], xbt[t][0:R, hf * 512:(hf + 1) * 512], tmp[0:R, :], ALU.add)
            for t in range(NT):
                S.act(xn[0:R, t, :], xbt[t][0:R, :], AF.Square, accum=st8[0:R, t:t + 1])
            S.ts(st8[0:R, 4:4 + NT], st8[0:R, 0:NT], 1.0 / D, ALU.mult, NORM_EPS, ALU.add)
            S.act(st8[0:R, 4:4 + NT], st8[0:R, 4:4 + NT], AF.Sqrt)
            S.recip(st8[0:R, 8:8 + NT], st8[0:R, 4:4 + NT])
            for t in range(NT):
                S.stt(xbt[t][0:R, :], xbt[t][0:R, :], st8[0:R, 8 + t:9 + t], self.normf[0:R, :], ALU.mult, ALU.mult)
                S.dma("sp", ydst[t * 128:t * 128 + R, :], xbt[t][0:R, :].ap, reads=[xbt[t]])

        if self.stage < 0.9:
            return
        if sample:
            shsb = sb("shsb", [SB, 14, 128])
            for c0 in range(0, 14, 4):
                ps = self.ps()
                n = min(4, 14 - c0)
                for c in range(c0, c0 + n):
                    S.mm(ps.ap(0, SB, (c - c0) * 128, [(1, 128)]), sprev[:, c, :], idf[:, C_ID:C_ID + 128], signal=(c == c0 + n - 1))
                S.copy(shsb[0:SB, c0:c0 + n, :], ps.ap(0, SB, 0, [(128, n), (1, 128)]), eng="act")
            S.dma("sp", O["shs"], shsb[:, :, :].ap, reads=[shsb])
            for hp in range(4):
                for b0 in range(0, SB, 8):
                    ps = self.ps()
                    for b in range(b0, b0 + 8):
                        for h in range(2):
                            S.mm(ps.ap(64 * h, 64, (b - b0) * 64, [(1, 64)]), Hst[64 * h:64 * h + 64, hp, b, :],
                                 idf[64 * h:64 * h + 64, C_IDS:C_IDS + 64], signal=(b == b0 + 7 and h == 1))
                    S.copy(Sld[:, hp, b0:b0 + 8, :], ps.ap(0, 128, 0, [(64, 8), (1, 64)]), eng="act")
                S.dma("sp", O["wkvs"][:, 2 * hp:2 * hp + 2, :, :].rearrange("b h v k -> (h v) b k"), Sld[:, hp, :, :].ap, reads=[Sld])
            for h in range(4):
                S.dma("sp", O["hgs"][:, h, :, :].rearrange("b f i -> f b i"), Shs[:, h, :, :].ap, reads=[Shs])
        else:
            shpb = sb("shpb", [14, 128])
            ps = self.ps()
            S.mm(ps.ap(0, 14, 0, [(1, 128)]), sprev[:, :, 0], idf[:, C_ID:C_ID + 128])
            S.copy(shpb[:, :], ps.ap(0, 14, 0, [(1, 128)]), eng="act")
            S.dma("sp", O["shp"], shpb[:, :].ap, reads=[shpb])
            ps = self.ps()
            for hp in range(4):
                for h in range(2):
                    S.mm(ps.ap(64 * h, 64, hp * 64, [(1, 64)]), Hst[64 * h:64 * h + 64, hp, 0, :],
                         idf[64 * h:64 * h + 64, C_IDS:C_IDS + 64], signal=(hp == 3 and h == 1))
            S.copy(Sld[:, :, 0, :], ps.ap(0, 128, 0, [(64, 4), (1, 64)]), eng="act")
            S.dma("sp", O["wkvp"].rearrange("(hp h) v k -> (h v) hp k", h=2), Sld[:, :, 0, :].ap, reads=[Sld])
            S.dma("sp", O["hgp"].rearrange("h f i -> f h i"), Shs[:, :, 0, :].ap, reads=[Shs])

    def tmp512(self, i):
        return self._tmp[i % len(self._tmp)]


_NC_CACHE = {}


def _prep_inputs(inp):
    f = lambda a: np.ascontiguousarray(np.asarray(a, dtype=np.float32))
    chunks = lambda v: f(v).reshape(-1, 128).T
    vec = np.zeros((128, NVC), np.float32)

    def put(name, v):
        c = chunks(v)
        vec[:, VC[name]:VC[name] + c.shape[1]] = c

    put("norm1", inp["norm1"][0]); put("norm2", inp["norm2"][0]); put("b_ada", inp["b_ada"][0])
    put("mu", inp["mu_shift"][0]); put("w0", inp["w0"][0]); put("a0", inp["a0"][0]); put("k_k", inp["k_k"][0])
    put("k_a", inp["k_a"][0]); put("r_k", np.asarray(inp["r_k"][0]).reshape(-1)); put("lnx_w", inp["lnx_w"][0])
    put("lnx_b", inp["lnx_b"][0]); put("lb0", inp["hgrn_lb"][0]); put("lb1", inp["hgrn_lb"][1])
    put("gnorm", inp["hgrn_gnorm"][0])
    b_ada = f(inp["b_ada"][0])
    shared = {
        "w_ada": f(inp["w_ada"][0]),
        "w_in": np.ascontiguousarray(f(inp["w_in"][0])[:, np.concatenate([np.arange(c * 128, (c + 1) * 128) for c in INPROJ_ORDER])]),
        "w_out": f(inp["w_out"][0]),
        "w_up": f(inp["w_up"][0]), "w_down": f(inp["w_down"][0]), "w_dec": f(inp["w_decay_up"][0]),
        "w_aaa": f(inp["w_aaa_up"][0]), "w_gate": f(inp["w_gate_up"][0]), "vecF": vec,
        "normf": np.ascontiguousarray(np.broadcast_to(f(inp["norm_f"])[None, :], (128, D))),
        "bgt": np.ascontiguousarray(np.concatenate([b_ada[2 * D:3 * D], b_ada[5 * D:6 * D]])[None, :]),
        "cst": make_consts(),
    }
    xp, xs = f(inp["x_prompt"]), f(inp["x_sample"])
    cp, cs_ = f(inp["c_prompt"]), f(inp["c_sample"])
    sst, swkv, shg = f(inp["state_shift"][0]), f(inp["state_wkv"][0]), f(inp["state_hgrn"][0])
    maps = []
    for c in range(NCORES):
        bs = slice(c * SB, (c + 1) * SB)
        call = np.concatenate([cp[c:c + 1], cs_[bs]], axis=0)
        cT = np.ascontiguousarray(call.reshape(17, 8, 128).transpose(2, 1, 0))
        sstc = np.ascontiguousarray(sst[bs].reshape(SB, 14, 128).transpose(2, 1, 0))
        m = dict(shared)
        m.update({"xp": xp[c], "xs": np.ascontiguousarray(xs[bs].reshape(SB * ST, D)), "cT": cT, "sst": sstc,
                  "swkv": np.ascontiguousarray(swkv[bs]), "shg": np.ascontiguousarray(shg[bs])})
        maps.append(m)
    return maps


def _run(inp, stage=99, debug=False, lite=False):
    key = (stage, debug, lite)
    if key not in _NC_CACHE:
        _NC_CACHE[key] = Builder(stage=stage, debug=debug, lite=lite).build()
    nc = _NC_CACHE[key]
    maps = _prep_inputs(inp)
    if lite:
        for m in maps:
            for n in ["w_ada", "w_in", "w_out", "w_up", "w_down"]:
                m[n] = np.zeros((8, 8), np.float32)
    res = run_bass_kernel_spmd(nc, maps, core_ids=list(range(NCORES)))
    return res.results


def kernel(**inp):
    rs = _run(inp)
    g = lambda k: [np.asarray(r[k], dtype=np.float32) for r in rs]
    y_prompt = np.stack(g("yp"), 0)
    y_sample = np.concatenate(g("ys"), 0).reshape(NCORES * SB, ST, D)
    shift_p = np.stack([a.reshape(SHW) for a in g("shp")], 0)[None]
    wkv_p = np.stack(g("wkvp"), 0)[None]
    hgrn_p = np.stack(g("hgp"), 0)[None]
    shift_s = np.concatenate([a.reshape(SB, SHW) for a in g("shs")], 0)[None]
    wkv_s = np.concatenate(g("wkvs"), 0)[None]
    hgrn_s = np.concatenate(g("hgs"), 0)[None]
    return (y_prompt, y_sample, shift_p, wkv_p, hgrn_p, shift_s, wkv_s, hgrn_s)
```

```python
import contextlib
import numpy as np
import concourse.bass as bass
import concourse.mybir as mybir
from concourse.bass_utils import run_bass_kernel_spmd

F32 = mybir.dt.float32
BF16 = mybir.dt.bfloat16
AF = mybir.ActivationFunctionType
ALU = mybir.AluOpType

D = 1024
NCORES = 8
SEQ = 2048
SB = 16
ST = 4
DFF = 4096
INW = 3840
SHW = 1792
WDEC = 0.6065306597126334
NORM_EPS = 1e-6
LNX_EPS = 64e-5

CH_R, CH_K, CH_V, CH_WA, CH_G, CH_Q, CH_F, CH_I, CH_OG = 0, 4, 8, 12, 13, 14, 18, 22, 26
INPROJ_ORDER = [12, 13, 0, 4, 8, 14, 18, 22, 26, 1, 5, 9, 15, 19, 23, 27,
                2, 6, 10, 16, 20, 24, 28, 3, 7, 11, 17, 21, 25, 29]

VC = {}
_off = 0
for _n, _c in [("norm1", 8), ("norm2", 8), ("b_ada", 48), ("mu", 14), ("w0", 4), ("a0", 4), ("k_k", 4),
               ("k_a", 4), ("r_k", 4), ("lnx_w", 4), ("lnx_b", 4), ("lb0", 4), ("lb1", 4), ("gnorm", 1)]:
    VC[_n] = _off
    _off += _c
NVC = _off


class Buf:
    def __init__(self, name, t):
        self.name = name
        self.w = None
        self.r = {}
        self.aliases = []
        self.set_t(t)

    def set_t(self, t):
        self.t = t
        if t is None:
            return
        if hasattr(t, "offset") and hasattr(t, "ap"):
            self.th = t.tensor
            self.pstride = int(t.ap[0][0])
            self.base = int(t.offset)
        else:
            self.th = t.tensor if hasattr(t, "tensor") else t
            self.pstride = int(np.prod(list(t.shape)[1:]))
            self.base = 0

    def __getitem__(self, idx):
        return View(self, self.t[idx])

    def ap(self, p0, npart, off, dims):
        return View(self, bass.AP(self.th, self.base + p0 * self.pstride + off,
                                  [[self.pstride, npart]] + [list(d) for d in dims]))


class View:
    def __init__(self, buf, ap):
        self.buf = buf
        self.ap = ap

    def re(self, pat, **kw):
        return View(self.buf, self.ap.rearrange(pat, **kw))

    def bc(self, shape):
        return View(self.buf, self.ap.to_broadcast(shape))

    def __getitem__(self, idx):
        return View(self.buf, self.ap[idx])


class Eng:
    def __init__(self, name, sem):
        self.name = name
        self.sem = sem
        self.count = 0
        self.waited = {}
        self.ops = []


class DSem:
    def __init__(self, sem):
        self.sem = sem
        self.total = 0


class Sched:
    def __init__(self, nc, stack):
        self.nc = nc
        self.E = {}
        for n in ["pe", "act", "dve", "pool", "sp"]:
            self.E[n] = Eng(n, stack.enter_context(nc.semaphore("s_" + n)))
        self.dsems = [DSem(stack.enter_context(nc.semaphore("d%d" % i))) for i in range(32)]
        self.drr = 0
        self.nops = 0
        self.npe = 0
        self.marks = []

    def _collect(self, E, reads, writes, extra=()):
        deps = {}

        def need(s, v):
            if v > deps.get(id(s), (s, 0))[1]:
                deps[id(s)] = (s, v)

        for b0 in reads:
            for b in [b0] + b0.aliases:
                if b.w is not None:
                    need(*b.w)
            if getattr(b0, "excl", False):
                for s, v in b0.r.values():
                    if s is not E.sem:
                        need(s, v)
        for b0 in writes:
            for b in [b0] + b0.aliases:
                if b.w is not None:
                    need(*b.w)
                for s, v in b.r.values():
                    need(s, v)
        for s, v in extra:
            need(s, v)
        waits = []
        for s, v in deps.values():
            if E.name == "pe" and s is E.sem:
                continue
            if E.waited.get(id(s), 0) < v:
                E.waited[id(s)] = v
                waits.append((s, v))
        return waits

    @staticmethod
    def _update(tok, reads, writes):
        s, v = tok
        for b in reads:
            if b.r.get(id(s), (s, 0))[1] < v:
                b.r[id(s)] = (s, v)
        for b in writes:
            b.w = tok
            b.r = {}

    def emit(self, en, fn, reads=(), writes=(), signal=True):
        E = self.E[en]
        reads = [v.buf if isinstance(v, View) else v for v in reads]
        writes = [v.buf if isinstance(v, View) else v for v in writes]
        waits = self._collect(E, reads, writes)
        if signal:
            E.count += 1
            tokv = E.count
        else:
            tokv = E.count + 1
        E.ops.append((waits, fn, ("sig", E.sem) if signal else None))
        self._update((E.sem, tokv), reads, writes)
        self.nops += 1

    def dma(self, en, out, in_, reads=(), writes=(), **kw):
        E = self.E[en]
        reads = [v.buf if isinstance(v, View) else v for v in reads]
        writes = [v.buf if isinstance(v, View) else v for v in writes]
        ds = self.dsems[self.drr % len(self.dsems)]
        self.drr += 1
        waits = self._collect(E, reads, writes, extra=[(ds.sem, ds.total)] if ds.total else [])
        ds.total += 16
        E.ops.append((waits, lambda h: h.dma_start(out=out, in_=in_, **kw), ("dma", ds.sem)))
        self._update((ds.sem, ds.total), reads, writes)
        self.nops += 1

    def barrier(self):
        for E in self.E.values():
            waits = []
            for O in self.E.values():
                if O is E or O.count == 0:
                    continue
                if E.waited.get(id(O.sem), 0) < O.count:
                    E.waited[id(O.sem)] = O.count
                    waits.append((O.sem, O.count))
            for ds in self.dsems:
                if ds.total and E.waited.get(id(ds.sem), 0) < ds.total:
                    E.waited[id(ds.sem)] = ds.total
                    waits.append((ds.sem, ds.total))
            if waits:
                E.ops.append((waits, None, None))

    def flush(self):
        nc = self.nc
        hmap = {"pe": "tensor", "act": "scalar", "dve": "vector", "pool": "gpsimd", "sp": "sync"}
        with nc.Block() as block:
            for n, E in self.E.items():
                ops = E.ops

                def body(h, ops=ops):
                    for waits, fn, sig in ops:
                        for s, v in waits:
                            h.wait_ge(s, v)
                        if fn is None:
                            continue
                        ins = fn(h)
                        if sig is not None:
                            if sig[0] == "sig":
                                ins.then_inc(sig[1], 1)
                            else:
                                ins.then_inc(sig[1], 16)

                getattr(block, hmap[n])(body)
        for E in self.E.values():
            E.ops = []

    def mark(self, name):
        self.marks.append((name, self.npe))

    def mm(self, out, lhsT, rhs, start=True, stop=True, signal=None):
        self.npe += 1
        if signal is None:
            signal = stop
        self.emit("pe", lambda e: e.matmul(out.ap, lhsT.ap, rhs.ap, start=start, stop=stop),
                  reads=[lhsT, rhs], writes=[out], signal=signal)

    def act(self, out, in_, func, bias=None, scale=None, accum=None, eng="act"):
        kw = {}
        rd = [in_]
        wr = [out]
        if bias is not None:
            if isinstance(bias, View):
                kw["bias"] = bias.ap
                rd.append(bias)
            else:
                kw["bias"] = bias
        if scale is not None:
            if isinstance(scale, View):
                kw["scale"] = scale.ap
                rd.append(scale)
            else:
                kw["scale"] = scale
        if accum is not None:
            kw["accum_out"] = accum.ap
            wr.append(accum)
        self.emit(eng, lambda e: e.activation(out=out.ap, in_=in_.ap, func=func, **kw), reads=rd, writes=wr)

    def tt(self, out, in0, in1, op, eng="dve"):
        self.emit(eng, lambda e: e.tensor_tensor(out=out.ap, in0=in0.ap, in1=in1.ap, op=op),
                  reads=[in0, in1], writes=[out])

    def ts(self, out, in0, s1, op0, s2=None, op1=None, eng="dve"):
        rd = [in0]
        a1 = s1
        a2 = s2
        if isinstance(s1, View):
            rd.append(s1)
            a1 = s1.ap
        if isinstance(s2, View):
            rd.append(s2)
            a2 = s2.ap
        if op1 is None:
            self.emit(eng, lambda e: e.tensor_scalar(out=out.ap, in0=in0.ap, scalar1=a1, scalar2=None, op0=op0),
                      reads=rd, writes=[out])
        else:
            self.emit(eng, lambda e: e.tensor_scalar(out=out.ap, in0=in0.ap, scalar1=a1, scalar2=a2, op0=op0, op1=op1),
                      reads=rd, writes=[out])

    def stt(self, out, in0, s, in1, op0, op1):
        rd = [in0, in1]
        a = s
        if isinstance(s, View):
            rd.append(s)
            a = s.ap
        self.emit("dve", lambda e: e.scalar_tensor_tensor(out=out.ap, in0=in0.ap, scalar=a, in1=in1.ap, op0=op0, op1=op1),
                  reads=rd, writes=[out])

    def copy(self, out, in_, eng="dve"):
        if eng == "act":
            self.emit("act", lambda e: e.activation(out=out.ap, in_=in_.ap, func=AF.Identity), reads=[in_], writes=[out])
        else:
            self.emit(eng, lambda e: e.tensor_copy(out=out.ap, in_=in_.ap), reads=[in_], writes=[out])

    def memset(self, out, val, eng="pool"):
        self.emit(eng, lambda e: e.memset(out.ap, val), reads=[], writes=[out])

    def recip(self, out, in_):
        self.emit("dve", lambda e: e.reciprocal(out=out.ap, in_=in_.ap), reads=[in_], writes=[out])

    def scan(self, out, d0, d1, init, op0, op1):
        self.emit("dve", lambda e: e.tensor_tensor_scan(out=out.ap, data0=d0.ap, data1=d1.ap, initial=init, op0=op0, op1=op1),
                  reads=[d0, d1], writes=[out])


NCST = 1296
C_ID, C_IDS, C_BONE, C_BMEAN, C_AMEAN = 0, 128, 192, 320, 448
C_M1, C_MS = 576, 704
C_M1S, C_MSS = 768, 776
C_ONES = 780
C_M2H = 912
C_BD = 1040
C_N0, C_N0T, C_N1 = 1104, 1168, 1232


def make_consts():
    c = np.zeros((128, NCST), np.float32)
    c[:, C_ID:C_ID + 128] = np.eye(128)
    c[0:64, C_IDS:C_IDS + 64] = np.eye(64)
    c[64:128, C_IDS:C_IDS + 64] = np.eye(64)
    blk = np.kron(np.eye(2), np.ones((64, 64)))
    c[:, C_BONE:C_BONE + 128] = blk
    c[:, C_BMEAN:C_BMEAN + 128] = blk / 64.0
    c[:, C_AMEAN:C_AMEAN + 128] = 1.0 / 128.0
    s = np.arange(128)[:, None] % 64
    t = np.arange(64)[None, :]
    c[:, C_M1:C_M1 + 64] = (s < t)
    c[:, C_M1 + 64:C_M1 + 128] = (s <= t)
    c[:, C_MS:C_MS + 64] = (t < s)
    s4 = np.arange(128)[:, None] % 32
    t4 = np.arange(4)[None, :]
    ok = (s4 < 4)
    c[:, C_M1S:C_M1S + 4] = (s4 < t4) & ok
    c[:, C_M1S + 4:C_M1S + 8] = (s4 <= t4) & ok
    c[:, C_MSS:C_MSS + 4] = (t4 < s4) & ok
    c[:, C_ONES:C_ONES + 128] = 1.0
    r = np.arange(128)[:, None] % 64
    q = np.arange(64)[None, :]
    same16 = (r // 16) == (q // 16)
    same32 = (r // 32) == (q // 32)
    c[:, C_M2H:C_M2H + 64] = (r < q) & same16
    c[:, C_M2H + 64:C_M2H + 128] = (r <= q)
    c[:, C_BD:C_BD + 64] = (q < r) & same16
    c[:, C_N0:C_N0 + 64] = (q < r) & same32 & ~same16
    c[:, C_N0T:C_N0T + 64] = (r < q) & same32 & ~same16
    c[:, C_N1:C_N1 + 64] = (r >= 32) & (q < 32)
    return c


class PsBank:
    def __init__(self, buf, col):
        self.buf = buf
        self.col = col

    def ap(self, p0, npart, off, dims):
        return View(self.buf, bass.AP(self.buf.th, p0 * 4096 + self.col + off, [[4096, npart]] + [list(d) for d in dims]))

    def full(self, npart=128, n=512):
        return self.ap(0, npart, 0, [(1, n)])

    def done(self):
        self.owner.ps_open.discard(self.idx)


class Builder:
    def __init__(self, stage=99, debug=False, lite=False):
        self.stage = stage
        self.debug = debug
        self.lite = lite
        self.dbg_col = 0

    def dram_in(self, name, shape, dt=F32):
        return self.nc.dram_tensor(name, list(shape), dt, kind="ExternalInput").ap()

    def dram_out(self, name, shape, dt=F32):
        return self.nc.dram_tensor(name, list(shape), dt, kind="ExternalOutput").ap()

    def sb(self, stack, name, shape, dt=F32):
        t = stack.enter_context(self.nc.sbuf_tensor("sb_" + name, list(shape), dt))
        return Buf(name, t)

    def dbg(self, view, ncols, npart=128):
        if not self.debug:
            return
        c0 = self.dbg_col
        self.dbg_col += ncols
        assert self.dbg_col <= 8192
        self.S.dma("sp", self.O["dbg"][0:npart, c0:c0 + ncols], view.ap, reads=[view])
        return c0

    def build(self):
        nc = bass.Bass("TRN2", target_bir_lowering=False)
        self.nc = nc
        I = {}
        I["xp"] = self.dram_in("xp", [SEQ, D])
        I["xs"] = self.dram_in("xs", [SB * ST, D])
        I["cT"] = self.dram_in("cT", [128, 8, 17])
        I["sst"] = self.dram_in("sst", [128, 14, SB])
        I["swkv"] = self.dram_in("swkv", [SB, 8, 64, 64])
        I["shg"] = self.dram_in("shg", [SB, 4, 128, 128])
        if self.lite:
            for n in ["w_ada", "w_in", "w_out", "w_up", "w_down"]:
                I[n] = self.dram_in(n, [8, 8])
        else:
            I["w_ada"] = self.dram_in("w_ada", [D, 6 * D])
            I["w_in"] = self.dram_in("w_in", [D, INW])
            I["w_out"] = self.dram_in("w_out", [D, D])
            I["w_up"] = self.dram_in("w_up", [D, DFF])
            I["w_down"] = self.dram_in("w_down", [DFF, D])
        I["w_dec"] = self.dram_in("w_dec", [64, 512])
        I["w_aaa"] = self.dram_in("w_aaa", [64, 512])
        I["w_gate"] = self.dram_in("w_gate", [128, 512])
        I["vecF"] = self.dram_in("vecF", [128, NVC])
        I["normf"] = self.dram_in("normf", [128, D])
        I["bgt"] = self.dram_in("bgt", [1, 2 * D])
        I["cst"] = self.dram_in("cst", [128, NCST])
        O = {}
        O["yp"] = self.dram_out("yp", [SEQ, D])
        O["ys"] = self.dram_out("ys", [SB * ST, D])
        O["shp"] = self.dram_out("shp", [14, 128])
        O["wkvp"] = self.dram_out("wkvp", [8, 64, 64])
        O["hgp"] = self.dram_out("hgp", [4, 128, 128])
        O["shs"] = self.dram_out("shs", [SB, 14, 128])
        O["wkvs"] = self.dram_out("wkvs", [SB, 8, 64, 64])
        O["hgs"] = self.dram_out("hgs", [SB, 4, 128, 128])
        if self.debug:
            O["dbg"] = self.dram_out("dbg", [128, 8192])
        self.I, self.O = I, O

        with contextlib.ExitStack() as stack:
            S = Sched(nc, stack)
            self.S = S
            self.setup_persistent(stack)
            if self.stage < 0.1:
                S.barrier()
                S.flush()
                return nc
            with contextlib.ExitStack() as st2:
                self.run_phase(st2, sample=True)
                S.barrier()
                S.flush()
            if self.stage >= 2:
                with contextlib.ExitStack() as st3:
                    self.run_phase(st3, sample=False)
                    S.barrier()
                    S.flush()
        return nc

    def ps(self, hold=False):
        for _ in range(8):
            i = self.ps_rr % 8
            self.ps_rr += 1
            if i not in self.ps_open:
                if hold:
                    self.ps_open.add(i)
                pb = PsBank(self.psum[i], i * 512)
                pb.owner = self
                pb.idx = i
                return pb
        raise RuntimeError("no free PSUM bank")

    def setup_persistent(self, stack):
        S, I = self.S, self.I
        sb = lambda n, s, d=F32: self.sb(stack, n, s, d)
        pst = stack.enter_context(self.nc.psum_tensor("psall", [128, 4096], F32))
        self.psum = []
        for i in range(8):
            b = Buf("ps%d" % i, None)
            b.t = pst
            b.th = pst.tensor if hasattr(pst, "tensor") else pst
            b.pstride = 4096
            b.base = 0
            b.excl = True
            self.psum.append(b)
        self.ps_rr = 0
        self.ps_open = set()
        self.vecF = sb("vecF", [128, NVC])
        self.cstf = sb("cstf", [128, 576])
        self.cstf_full = None
        self.cstb = sb("cstb", [128, NCST], BF16)
        self.normf = sb("normf", [128, D])
        self.bgt = sb("bgt", [4, 512], BF16)
        self.scT = sb("scT", [128, 8, 17], BF16)
        self.cTf = sb("cTf", [128, 8, 17])
        self.modT = sb("modT", [128, 32, 17])
        self.G1 = sb("G1", [128, 8, 17])
        self.G2 = sb("G2", [128, 8, 17])
        self.omm = sb("omm", [128, 14])
        self.epsc = sb("epsc", [128, 4])
        self.kk2 = sb("kk2", [128, 4])
        self.omka = sb("omka", [128, 4])
        self.lbv = sb("lbv", [128, 4])
        self.oml = sb("oml", [128, 4])
        self.wdec = sb("wdec", [128, 512], BF16)
        self.wgate = sb("wgate", [128, 512], BF16)
        self.ring = [sb("ring%d" % i, [128, 4096], BF16) for i in range(3)]
        self.tiles = []
        self.tile_issued = 0
        self.tile_got = 0
        S.dma("sp", self.vecF[:, :].ap, I["vecF"], writes=[self.vecF])
        S.dma("sp", self.cstf[:, :].ap, I["cst"][:, 0:576], writes=[self.cstf])
        S.dma("pool", self.cstb[:, :].ap, I["cst"], writes=[self.cstb])
        S.dma("sp", self.normf[:, :].ap, I["normf"], writes=[self.normf])
        S.dma("sp", self.cTf[:, :, :].ap, I["cT"], writes=[self.cTf])
        S.dma("pool", self.bgt[:, :].ap, I["bgt"].rearrange("o (g n) -> (o g) n", g=4), writes=[self.bgt])
        S.dma("pool", self.wdec[0:64, :].ap, I["w_dec"], writes=[self.wdec])
        S.dma("pool", self.wdec[64:128, :].ap, I["w_aaa"], writes=[self.wdec])
        S.dma("pool", self.wgate[:, :].ap, I["w_gate"], writes=[self.wgate])
        vf = self.vecF
        S.memset(self.epsc[:, 0:1], LNX_EPS)
        S.memset(self.epsc[:, 1:2], NORM_EPS)
        S.memset(self.epsc[:, 2:3], 1e-30)
        S.tt(self.kk2[:, :], vf[:, VC["k_k"]:VC["k_k"] + 4], vf[:, VC["k_k"]:VC["k_k"] + 4], ALU.mult)
        S.act(self.scT[:, :, :], self.cTf[:, :, :], AF.Silu)
        S.ts(self.omm[:, :], vf[:, VC["mu"]:VC["mu"] + 14], -1.0, ALU.mult, 1.0, ALU.add)
        S.ts(self.omka[:, :], vf[:, VC["k_a"]:VC["k_a"] + 4], -1.0, ALU.mult, 1.0, ALU.add)
        S.tt(self.oml[:, :], vf[:, VC["lb0"]:VC["lb0"] + 4], vf[:, VC["lb1"]:VC["lb1"] + 4], ALU.subtract)
        S.act(self.lbv[:, :], self.oml[:, :], AF.Sigmoid)
        S.ts(self.oml[:, :], self.lbv[:, :], -1.0, ALU.mult, 1.0, ALU.add)
        if self.lite:
            return
        fm_groups = [0, 1, 2, 3, 6, 7, 8, 9]
        for g in fm_groups:
            self.tiles.append(("cols", I["w_ada"], [g * 512 + j * 128 for j in range(4)]))
        for ph in range(2 if self.stage >= 2 else 1):
            for g in [4, 5, 10, 11]:
                self.tiles.append(("cols", I["w_ada"], [g * 512 + j * 128 for j in range(4)]))
            for blk in range(1 if ph == 0 else 4):
                for gi in range(0, 30, 4):
                    self.tiles.append(("cols", I["w_in"], [(gi + q_) * 128 for q_ in range(len(INPROJ_ORDER[gi:gi + 4]))]))
                for hf in range(2):
                    self.tiles.append(("cols", I["w_out"], [hf * 512 + j * 128 for j in range(4)]))
                for g in range(8):
                    self.tiles.append(("cols", I["w_up"], [g * 512 + j * 128 for j in range(4)]))
                    self.tiles.append(("rows", I["w_down"], g * 512))
        for gi, g in enumerate(fm_groups):
            wt = self.get_tile()
            ps = self.ps()
            for j in range(4):
                for k in range(8):
                    S.mm(ps.ap(0, 128, j * 32, [(1, 17)]), wt.ap(0, 128, k * 512 + j * 128, [(1, 128)]), self.scT[:, k, :],
                         start=(k == 0), stop=(k == 7), signal=(k == 7 and j == 3))
            for j in range(4):
                fc = g * 4 + j
                S.act(self.modT[:, gi * 4 + j, :], ps.ap(0, 128, j * 32, [(1, 17)]), AF.Identity,
                      bias=vf[:, VC["b_ada"] + fc:VC["b_ada"] + fc + 1], scale=1.0)
        for (G, nk, sc0) in [(self.G1, "norm1", 8), (self.G2, "norm2", 24)]:
            for k in range(8):
                S.ts(G[:, k, :], self.modT[:, sc0 + k, :], 1.0, ALU.add, vf[:, VC[nk] + k:VC[nk] + k + 1], ALU.mult)

    def issue_tile(self, idx):
        S = self.S
        kind, ap, arg = self.tiles[idx]
        wt = self.ring[idx % len(self.ring)]
        if kind == "cols":
            cols = arg
            j = 0
            while j < len(cols):
                n = 1
                while j + n < len(cols) and cols[j + n] == cols[j] + 128 * n:
                    n += 1
                src = ap[:, cols[j]:cols[j] + 128 * n].rearrange("(k p) c -> p k c", p=128)
                dst = wt.ap(0, 128, j * 128, [(512, 8), (1, 128 * n)])
                S.dma("pool", dst.ap, src, writes=[wt])
                j += n
        else:
            r0 = arg
            src = ap[r0:r0 + 512, :].rearrange("(c p) n -> p c n", p=128)
            dst = wt.ap(0, 128, 0, [(1024, 4), (1, 1024)])
            S.dma("pool", dst.ap, src, writes=[wt])

    def get_tile(self):
        i = self.tile_got
        self.tile_got += 1
        target = min(len(self.tiles), i + len(self.ring) - 1)
        while self.tile_issued < target:
            self.issue_tile(self.tile_issued)
            self.tile_issued += 1
        return self.ring[i % len(self.ring)]

    def run_phase(self, stack, sample):
        S, I, O = self.S, self.I, self.O
        vf = self.vecF
        sb = lambda n, s, d=F32: self.sb(stack, ("s_" if sample else "p_") + n, s, d)
        N = 64 if sample else 512
        C = 4 if sample else 64
        NCH = N // C
        HS = 64
        R = min(N, 128)
        NT = max(1, N // 128)
        nblk = 1 if sample else 4
        L = 1 if sample else 3
        idb = self.cstb
        idf = self.cstf

        def rows(fn):
            if not sample:
                fn(0, 128)
            else:
                fn(0, C)
                fn(HS, C)

        xbt = [sb("xb%d" % t_, [128, D]) for t_ in range(NT)]
        st8 = sb("st8", [128, 16])
        hT = sb("hT", [128, 8, N], BF16)
        mixT = Buf("mixT", hT.t)
        mixT.aliases = [hT]
        hT.aliases = [mixT]
        GT1 = sb("GT1", [128, D])
        GT2 = sb("GT2", [128, D])
        NH = N // 2
        NCHH = NCH // 2 if False else (N // 2) // (4 if sample else 64)
        TH = [[sb("T%d_%d" % (hf_, i), [128, N // 2]) for i in range(5)] for hf_ in range(2)]
        rkv = [sb("rkv%d" % i_, [128, 3, N]) for i_ in range(2)]
        wa = sb("wa", [128, N], BF16)
        sgd = sb("sgd", [128, N], BF16)
        notst = sb("notst", [128, N])
        self._tmp = [sb("tmpA", [128, 512]), sb("tmpB", [128, 512])]
        arena = stack.enter_context(self.nc.sbuf_tensor(("s_" if sample else "p_") + "arena", [128, max(28 * N, 2 * NT * D + 8 * N)], BF16))
        aoff = [0]

        def carve(name, nel_bf16, dt, pat=None, **kw):
            ap = arena[:, aoff[0]:aoff[0] + nel_bf16]
            aoff[0] += nel_bf16
            if dt == F32:
                ap = ap.bitcast(F32)
            if pat:
                ap = ap.rearrange(pat, **kw)
            return Buf(name, ap)

        AR = carve("AR", 8 * N, BF16, "p (a c t e) -> p a c t e", a=4, c=NCH, t=2)
        Bt = carve("Bt", 4 * N, BF16, "p (a n) -> p a n", a=4)
        Kt = carve("Kt", 4 * N, BF16, "p (a n) -> p a n", a=4)
        Vt = carve("Vt", 4 * N, BF16, "p (a n) -> p a n", a=4)
        gbf = carve("gbf", 4 * N, BF16, "p (a n) -> p a n", a=4)
        bonus = carve("bonus", 4 * N, BF16, "p (a n) -> p a n", a=4)
        aoff[0] = 0
        dacc = carve("dacc", 2 * NT * D, F32, "p (t d) -> p t d", t=NT)
        uT = [carve("uT%d" % i, 4 * N, BF16, "p (a n) -> p a n", a=4) for i in range(2)]
        mixer_bufs = [AR, Bt, Kt, Vt, gbf, bonus]
        mlp_bufs = [dacc] + uT
        for a in mixer_bufs:
            a.aliases = list(mlp_bufs)
        for a in mlp_bufs:
            a.aliases = list(mixer_bufs)
        gamR = sb("gamR", [128, 4, NCH])
        gamH = sb("gamH", [128, 4, NCH])
        sprev = sb("sprev", [128, 14, SB if sample else 1])
        arena2 = stack.enter_context(self.nc.sbuf_tensor(("s_" if sample else "p_") + "arena2", [128, max(NT * D, 8 * N)], BF16))
        xn = Buf("xn", arena2[:, 0:NT * D].rearrange("p (t d) -> p t d", t=NT))
        Qh = Buf("Qh", arena2[:, 0:4 * N].rearrange("p (a n) -> p a n", a=4))
        Kh = Buf("Kh", arena2[:, 4 * N:8 * N].rearrange("p (a n) -> p a n", a=4))
        screp = Buf("screp", arena2[:, 0:8 * R].rearrange("p (k r) -> p k r", k=8))
        xn.aliases = [Qh, Kh, screp]
        Qh.aliases = [xn, screp]
        Kh.aliases = [xn, screp]
        screp.aliases = [xn, Qh, Kh]
        Vh = sb("Vh", [128, 4, N], BF16)
        ogs = sb("ogs", [128, 4, N], BF16)
        qs = sb("qs", [128, N])
        NS = 3

        class TS:
            pass
        tsets = []
        for i_ in range(NS):
            t_ = TS()
            t_.AkRk = sb("AkRk%d" % i_, [128, 4, 2, C], BF16)
            t_.AbRb = sb("AbRb%d" % i_, [128, 4, 2, C], BF16)
            t_.Pm = [sb("Pm%d_%d" % (i_, q_), [128, 4, C], BF16) for q_ in range(2)]
            t_.PTm = [sb("PTm%d_%d" % (i_, q_), [128, 4, C], BF16) for q_ in range(2)]
            t_.Qm = sb("Qm%d" % i_, [128, 4, C], BF16)
            t_.TTm = [sb("TTm%d_%d" % (i_, q_), [128, 4, C], BF16) for q_ in range(2)]
            if not sample:
                hbn = [sb("HB%d_%d" % (i_, q_), [128, 4, 64], BF16) for q_ in range(3)]
                t_.HB = hbn + [t_.Pm[0], t_.Pm[1], t_.PTm[0], t_.PTm[1], t_.Qm]
            else:
                t_.HB = None
            t_.Btok = sb("Btok%d" % i_, [128, 4, 64], BF16)
            t_.Ktok = sb("Ktok%d" % i_, [128, 4, 64], BF16)
            t_.Vtok = sb("Vtok%d" % i_, [128, 4, 64], BF16)
            tsets.append(t_)
        OTc = sb("OTc", [128, 4, C])
        PT1 = sb("PT1", [128, 4, C])
        PT2 = sb("PT2", [128, 4, C])
        HO = sb("HO", [128, 4, C])
        HT1 = sb("HT1", [128, 4, C])
        if sample:
            psets = [(OTc, PT1, PT2), (sb("OTc2", [128, 4, C]), sb("PT12", [128, 4, C]), sb("PT22", [128, 4, C]))]
        else:
            def alias_view(name, base_buf, ap):
                b_ = Buf(name, ap)
                b_.aliases = [base_buf]
                base_buf.aliases = base_buf.aliases + [b_]
                return b_
            q3 = lambda lo: qs.t[:, lo:lo + 4 * C].rearrange("p (a b) -> p a b", a=4)
            w3 = wa.t[:, :].bitcast(F32).rearrange("p (a b) -> p a b", a=4)
            psets = [(OTc, PT1, PT2), (alias_view("OTc2", qs, q3(0)), alias_view("PT12", qs, q3(4 * C)), alias_view("PT22", wa, w3))]
        Wsb = sb("Wsb", [128, 4, 64], BF16)
        Usb = sb("Usb", [128, 4, 64], BF16)
        tH = sb("tH", [128, 4, 64])
        hKtok = sb("hKtok", [128, 4, 128], BF16)
        hVtok = sb("hVtok", [128, 4, 128], BF16)
        hAT = sb("hAT", [128, 4, C], BF16)
        tS = sb("tS", [128, 4, 128])
        if sample:
            Hst = sb("Hst", [128, 4, SB, 64])
            Hsb = sb("Hsb", [128, 4, SB, 64], BF16)
            Sld = sb("Sld", [128, 4, SB, 64])
            Shs = sb("Shs", [128, 4, SB, 128])
            Shb = sb("Shb", [128, 4, SB, 128], BF16)
        else:
            Hst = sb("Hst", [128, 4, 1, 64])
            Hsb = sb("Hsb", [128, 4, 1, 64], BF16)
            Sld = sb("Sld", [128, 4, 1, 64])
            Shs = sb("Shs", [128, 4, 1, 128])
            Shb = sb("Shb", [128, 4, 1, 128], BF16)

        S.memset(notst[:, :], 1.0)
        S.memset(notst.ap(0, 128, 0, [(C, NCH), (1, 1)]), 0.0)
        if sample:
            if self.stage < 0.12:
                return
            S.dma("sp", sprev[:, :, :].ap, I["sst"], writes=[sprev])
            if self.stage < 0.13:
                return
            for hp in range(4):
                S.dma("sp", Sld[:, hp, :, :].ap,
                      I["swkv"][:, 2 * hp:2 * hp + 2, :, :].rearrange("b h v k -> (h v) b k"), writes=[Sld])
            if self.stage < 0.14:
                return
            for h in range(4):
                S.dma("sp", Shs[:, h, :, :].ap, I["shg"][:, h, :, :].rearrange("b f i -> f b i"), writes=[Shs])
            if self.stage < 0.15:
                return
            S.copy(Shb[:, :, :, :], Shs[:, :, :, :], eng="pool")
            if self.stage < 0.16:
                return
            for hp in range(4):
                for b0 in range(0, SB, 8):
                    ps = self.ps()
                    for b in range(b0, b0 + 8):
                        for h in range(2):
                            S.mm(ps.ap(64 * h, 64, (b - b0) * 64, [(1, 64)]), Sld[64 * h:64 * h + 64, hp, b, :],
                                 idf[64 * h:64 * h + 64, C_IDS:C_IDS + 64], signal=(b == b0 + 7 and h == 1))
                    S.copy(Hst[:, hp, b0:b0 + 8, :], ps.ap(0, 128, 0, [(64, 8), (1, 64)]), eng="dve")
                    S.copy(Hsb[:, hp, b0:b0 + 8, :], ps.ap(0, 128, 0, [(64, 8), (1, 64)]), eng="act")
        else:
            S.memset(sprev[:, :, :], 0.0)
            S.memset(Hst[:, :, :, :], 0.0)
            S.memset(Hsb[:, :, :, :], 0.0)
            S.memset(Shs[:, :, :, :], 0.0)
            S.memset(Shb[:, :, :, :], 0.0)

        if self.stage < 0.2:
            return
        if sample:
            S.copy(screp.ap(0, 128, 0, [(R, 8), (4, 16), (1, 4)]), self.scT.ap(0, 128, 1, [(17, 8), (1, 16), (0, 4)]))
        else:
            S.copy(screp[:, :, :], self.scT.ap(0, 128, 0, [(17, 8), (0, 128)]))
        for gi, (GT, half) in enumerate([(GT1, 0), (GT1, 1), (GT2, 0), (GT2, 1)]):
            wt = self.get_tile()
            ps = self.ps()
            for k in range(8):
                S.mm(ps.full(R, 512), screp[:, k, :], wt.ap(0, 128, k * 512, [(1, 512)]), start=(k == 0), stop=False, signal=False)
            S.mm(ps.full(R, 512), idb.ap(0, 4, C_ID + gi, [(0, R)]), self.bgt[0:4, :], start=False, stop=True)
            S.copy(GT[0:R, half * 512:(half + 1) * 512], ps.full(R, 512), eng="act")

        def rmsnorm_T(G, sh0, dst):
            for t in range(NT):
                S.act(xn[0:R, t, :], xbt[t][0:R, :], AF.Square, accum=st8[0:R, t:t + 1])
            S.ts(st8[0:R, 4:4 + NT], st8[0:R, 0:NT], 1.0 / D, ALU.mult, NORM_EPS, ALU.add)
            S.act(st8[0:R, 4:4 + NT], st8[0:R, 4:4 + NT], AF.Sqrt)
            S.recip(st8[0:R, 8:8 + NT], st8[0:R, 4:4 + NT])
            for t in range(NT):
                S.ts(xn[0:R, t, :], xbt[t][0:R, :], st8[0:R, 8 + t:9 + t], ALU.mult)
            for k in range(8):
                ps = self.ps()
                for t in range(NT):
                    S.mm(ps.ap(0, 128, t * 128, [(1, R)]), xn[0:R, t, k * 128:(k + 1) * 128], idb[0:R, C_ID:C_ID + R],
                         signal=(t == NT - 1))
                if not sample:
                    S.act(dst[:, k, :], ps.full(128, N), AF.Identity, scale=G[:, k, 0:1], bias=self.modT[:, sh0 + k, 0:1])
                else:
                    S.tt(self._tmp[0].ap(0, 128, 0, [(4, 16), (1, 4)]), ps.ap(0, 128, 0, [(4, 16), (1, 4)]),
                         G.ap(0, 128, k * 17 + 1, [(1, 16), (0, 4)]), ALU.mult)
                    S.tt(dst.ap(0, 128, k * N, [(4, 16), (1, 4)]), self._tmp[0].ap(0, 128, 0, [(4, 16), (1, 4)]),
                         self.modT.ap(0, 128, (sh0 + k) * 17 + 1, [(1, 16), (0, 4)]), ALU.add)

        def tv(buf, off=0):
            return buf.ap(0, 128, off, [(C, NCH), (1, C)])

        def token_shift(c, ps, dst, dst_off):
            mu = vf[:, VC["mu"] + c:VC["mu"] + c + 1]
            p1 = dst.ap(0, 128, dst_off, [(1, N)])
            S.act(p1, ps.full(128, N), AF.Identity, scale=self.omm[:, c:c + 1])
            if not sample:
                S.stt(dst.ap(0, 128, dst_off, [(1, 1)]), sprev[:, c, 0:1], mu, dst.ap(0, 128, dst_off, [(1, 1)]), ALU.mult, ALU.add)
                S.stt(dst.ap(0, 128, dst_off + 1, [(1, N - 1)]), ps.ap(0, 128, 0, [(1, N - 1)]), mu,
                      dst.ap(0, 128, dst_off + 1, [(1, N - 1)]), ALU.mult, ALU.add)
                S.copy(sprev[:, c, 0:1], ps.ap(0, 128, N - 1, [(1, 1)]), eng="act")
            else:
                S.stt(dst.ap(0, 128, dst_off, [(4, 16), (1, 1)]), sprev.ap(0, 128, c * SB, [(1, 16), (1, 1)]), mu,
                      dst.ap(0, 128, dst_off, [(4, 16), (1, 1)]), ALU.mult, ALU.add)
                S.stt(dst.ap(0, 128, dst_off + 1, [(4, 16), (1, 3)]), ps.ap(0, 128, 0, [(4, 16), (1, 3)]), mu,
                      dst.ap(0, 128, dst_off + 1, [(4, 16), (1, 3)]), ALU.mult, ALU.add)
                S.copy(sprev.ap(0, 128, c * SB, [(1, 16), (1, 1)]), ps.ap(0, 128, 3, [(4, 16), (1, 1)]), eng="act")

        if sample:
            TG = [[sb("TG%d_%d" % (hf_, i), [128, N // 2]) for i in range(3)] for hf_ in range(2)]
        else:
            mkv = lambda b_: Buf(b_.name + "_v", b_.t[:, :, :].rearrange("p a b -> p (a b)"))
            TG = [[mkv(PT1), mkv(PT2), mkv(OTc)], [mkv(HO), mkv(HT1), mkv(tH)]]
            for (v_, o_) in zip(TG[0] + TG[1], [PT1, PT2, OTc, HO, HT1, tH]):
                v_.aliases = [o_]
                o_.aliases = [v_]

        def run_rr(gens):
            gens = list(gens)
            while gens:
                for g_ in list(gens):
                    try:
                        next(g_)
                    except StopIteration:
                        gens.remove(g_)

        def hv(buf, off, hf):
            return buf.ap(0, 128, off + hf * NH, [(1, NH)])

        def hvc(buf, off, hf):
            return buf.ap(0, 128, off + hf * NH, [(C, NCHH), (1, C)])

        def rwkv_prep(j, rk, hf):
            T = TH[hf]
            r = hv(rk, 0, hf)
            k = hv(rk, N, hf)
            v = hv(rk, 2 * N, hf)
            jc = slice(j * 128, (j + 1) * 128)
            col = lambda n: vf[:, VC[n] + j:VC[n] + j + 1]
            tvh = lambda b_: b_.ap(0, 128, 0, [(C, NCHH), (1, C)])
            pfull = lambda p_: p_.ap(0, 128, 0, [(1, NH)])
            ps1 = self.ps()
            S.mm(pfull(ps1), self.wdec[0:64, jc], hv(wa, 0, hf)[0:64])
            S.act(T[0][:, :], pfull(ps1), AF.Sigmoid, bias=col("w0"), scale=1.0)
            yield
            ps2 = self.ps()
            S.mm(pfull(ps2), self.wdec[64:128, jc], hv(wa, 0, hf)[64:128])
            S.act(T[1][:, :], pfull(ps2), AF.Sigmoid, bias=col("a0"), scale=1.0)
            yield
            ps3 = self.ps()
            S.mm(pfull(ps3), self.wgate[:, jc], hv(sgd, 0, hf))
            S.copy(hv(gbf, j * N, hf), pfull(ps3), eng="act")
            yield
            S.scan(T[2][:, :], hv(notst, 0, hf), T[0][:, :], 0.0, ALU.mult, ALU.add)
            yield
            S.tt(T[0][:, :], T[2][:, :], T[0][:, :], ALU.subtract)
            S.act(T[3][:, :], T[2][:, :], AF.Exp, scale=-WDEC)
            yield
            S.act(T[4][:, :], T[2][:, :], AF.Exp, scale=WDEC)
            S.act(T[0][:, :], T[0][:, :], AF.Exp, scale=-WDEC)
            yield
            S.copy(gamR.ap(0, 128, j * NCH + hf * NCHH, [(1, NCHH)]), T[3].ap(0, 128, C - 1, [(C, NCHH)]), eng="pool")
            S.stt(T[2][:, :], k, self.kk2[:, j:j + 1], k, ALU.mult, ALU.mult)
            yield
            ps4 = self.ps()
            S.mm(pfull(ps4), idf[:, C_BONE:C_BONE + 128], T[2][:, :])
            S.act(T[2][:, :], pfull(ps4), AF.Ln, bias=self.epsc[:, 2:3], scale=1.0)
            yield
            S.act(T[2][:, :], T[2][:, :], AF.Exp, scale=-0.5)
            yield
            S.stt(T[2][:, :], k, col("k_k"), T[2][:, :], ALU.mult, ALU.mult)
            yield
            S.stt(AR.ap(0, 128, j * 2 * N + hf * NCHH * 2 * C, [(2 * C, NCHH), (1, C)]), tvh(T[2]), -1.0, tvh(T[0]), ALU.mult, ALU.mult)
            S.tt(T[0][:, :], T[2][:, :], T[1][:, :], ALU.mult, eng="pool")
            yield
            S.tt(hv(Bt, j * N, hf), T[0][:, :], T[4][:, :], ALU.mult, eng="pool")
            S.ts(T[2][:, :], T[1][:, :], col("k_a"), ALU.mult, self.omka[:, j:j + 1], ALU.add)
            yield
            S.tt(T[2][:, :], k, T[2][:, :], ALU.mult)
            yield
            S.tt(hv(Kt, j * N, hf), T[2][:, :], T[4][:, :], ALU.mult, eng="pool")
            S.tt(AR.ap(0, 128, j * 2 * N + hf * NCHH * 2 * C + C, [(2 * C, NCHH), (1, C)]), hvc(rk, 0, hf), tvh(T[3]), ALU.mult)
            S.copy(hv(Vt, j * N, hf), v, eng="act")
            yield
            S.stt(T[1][:, :], r, col("r_k"), T[2][:, :], ALU.mult, ALU.mult)
            yield
            ps5 = self.ps()
            S.mm(pfull(ps5), idf[:, C_BONE:C_BONE + 128], T[1][:, :])
            S.tt(hv(bonus, j * N, hf), pfull(ps5), v, ALU.mult)

        def hgrn_prep(h, ps, hf):
            T = TG[hf]
            pfh = ps.ap(0, 128, hf * NH, [(1, NH)])
            S.act(T[0][:, :], pfh, AF.Sigmoid)
            yield
            S.ts(T[1][:, :], T[0][:, :], self.oml[:, h:h + 1], ALU.mult, self.lbv[:, h:h + 1], ALU.add)
            yield
            S.ts(T[0][:, :], T[1][:, :], -1.0, ALU.mult, 1.0, ALU.add, eng="pool")
            S.act(T[1][:, :], T[1][:, :], AF.Ln)
            yield
            S.scan(T[2][:, :], hv(notst, 0, hf), T[1][:, :], 0.0, ALU.mult, ALU.add)
            yield
            S.act(T[1][:, :], T[2][:, :], AF.Exp)
            S.act(T[2][:, :], T[2][:, :], AF.Exp, scale=-1.0)
            yield
            S.tt(hv(Qh, h * N, hf), hv(qs, 0, hf), T[1][:, :], ALU.mult)
            S.tt(hv(Kh, h * N, hf), T[0][:, :], T[2][:, :], ALU.mult, eng="pool")
            S.copy(gamH.ap(0, 128, h * NCH + hf * NCHH, [(1, NCHH)]), T[1].ap(0, 128, C - 1, [(C, NCHH)]), eng="pool")

        active_rw = []
        active_hg = []

        def post_inproj(c, ps):
            if c < 12:
                j = c % 4
                token_shift(c, ps, rkv[j % 2], (c // 4) * N)
                if c // 4 == 2:
                    active_rw.extend([rwkv_prep(j, rkv[j % 2], 0), rwkv_prep(j, rkv[j % 2], 1)])
            elif c == 12:
                token_shift(c, ps, self._tmp[0], 0)
                S.act(wa[0:64, :], self._tmp[0][0:64, 0:N], AF.Tanh)
                S.copy(wa[64:128, :], self._tmp[0][64:128, 0:N], eng="act")
            elif c == 13:
                token_shift(c, ps, self._tmp[1], 0)
                S.act(sgd[:, :], self._tmp[1][:, 0:N], AF.Sigmoid)
            elif c < 18:
                S.act(qs[:, :], ps.full(128, N), AF.Silu)
            elif c < 22:
                gs_ = [hgrn_prep(c - 18, ps, 0), hgrn_prep(c - 18, ps, 1)]
                for g_ in gs_:
                    next(g_)
                active_hg.extend(gs_)
            elif c < 26:
                S.copy(Vh[:, c - 22, :], ps.full(128, N), eng="act")
            else:
                S.act(ogs[:, c - 26, :], ps.full(128, N), AF.Silu)

        m1c = idb.ap(0, 128, C_M1S if sample else C_M1, [(0, 4), (1, 2 * C)])
        msc = idb.ap(0, 128, C_MSS if sample else C_BD, [(0, 4), (1, C)])
        m2c = m1c if sample else idb.ap(0, 128, C_M2H, [(0, 4), (1, 2 * C)])
        idcc = idb.ap(0, 128, C_IDS, [(0, 4), (1, C)]) if not sample else None

        def idview(r0, nr):
            if sample:
                return idb.ap(r0, nr, C_ID + r0, [(0, 4), (1, C)])
            return View(idcc.buf, idcc.ap[r0:r0 + nr])

        def getbanks(n):
            while 8 - len(self.ps_open) < n:
                yield
            return [self.ps(hold=True) for _ in range(n)]

        post_gens = {}

        def hmm(out_ps, lhs, rhs, w=C, kb=None):
            for hp in range(4):
                for h in range(2):
                    ob = HS * h
                    S.mm(out_ps.ap(ob, C, hp * w, [(1, w)]), lhs(ob, hp), rhs(ob, hp), signal=(hp == 3 and h == 1))

        def rwkv_inv_gen(c, ts):
            cs = slice(c * C, (c + 1) * C)
            AkRk, AbRb, Pm, PTm, Qm, TTm = ts.AkRk, ts.AbRb, ts.Pm, ts.PTm, ts.Qm, ts.TTm
            (psA,) = yield from getbanks(1)
            for hp in range(4):
                for h in range(2):
                    rb, ob = 64 * h, HS * h
                    arv = AR.ap(rb, 64, (hp * NCH + c) * 2 * C, [(1, 2 * C)])
                    S.mm(psA.ap(ob, C, hp * 2 * C, [(1, 2 * C)]), Kt[rb:rb + 64, hp, cs], arv, signal=(hp == 3 and h == 1))
            yield
            rows(lambda r0, nr: S.tt(AkRk.ap(r0, nr, 0, [(2 * C, 4), (1, 2 * C)]), psA.ap(r0, nr, 0, [(2 * C, 4), (1, 2 * C)]),
                                     View(m1c.buf, m1c.ap[r0:r0 + nr]), ALU.mult))
            psA.done()
            psB, psC = yield from getbanks(2)
            for hp in range(4):
                for h in range(2):
                    rb, ob = 64 * h, HS * h
                    last = (hp == 3 and h == 1)
                    arv = AR.ap(rb, 64, (hp * NCH + c) * 2 * C, [(1, 2 * C)])
                    S.mm(psB.ap(ob, C, hp * 2 * C, [(1, 2 * C)]), Bt[rb:rb + 64, hp, cs], arv, signal=last)
                    S.mm(psC.ap(ob, C, hp * C, [(1, C)]), AR.ap(rb, 64, (hp * NCH + c) * 2 * C, [(1, C)]),
                         Bt[rb:rb + 64, hp, cs], signal=last)
            yield

            def ev(r0, nr):
                S.tt(AbRb.ap(r0, nr, 0, [(2 * C, 4), (1, 2 * C)]), psB.ap(r0, nr, 0, [(2 * C, 4), (1, 2 * C)]),
                     View(m2c.buf, m2c.ap[r0:r0 + nr]), ALU.mult)
                S.tt(Pm[0].ap(r0, nr, 0, [(C, 4), (1, C)]), psC.ap(r0, nr, 0, [(C, 4), (1, C)]),
                     View(msc.buf, msc.ap[r0:r0 + nr]), ALU.mult)
                S.tt(TTm[0].ap(r0, nr, 0, [(C, 4), (1, C)]), AbRb.ap(r0, nr, 0, [(2 * C, 4), (1, C)]), idview(r0, nr), ALU.add, eng="pool")
            rows(ev)
            if not sample:
                mk = lambda off: idb.ap(0, 128, off, [(0, 4), (1, C)])
                HB = ts.HB
                S.tt(HB[0][:, :, :], psC.ap(0, 128, 0, [(C, 4), (1, C)]), mk(C_N0), ALU.mult)
                S.tt(HB[1][:, :, :], psB.ap(0, 128, 0, [(2 * C, 4), (1, C)]), mk(C_N0T), ALU.mult)
                S.tt(HB[2][:, :, :], psC.ap(0, 128, 0, [(C, 4), (1, C)]), mk(C_N1), ALU.mult)
            psB.done()
            psC.done()
            yield
            Pc = Pm[0]
            PTc = lambda ob, hp: AbRb.ap(ob, C, hp * 2 * C, [(1, C)])
            TTc = TTm[0]
            for l in range(1, L + 1):
                if l < L:
                    psP, psPT = yield from getbanks(2)
                else:
                    (psP,) = yield from getbanks(1)
                    psPT = None
                hmm(psP, PTc, lambda ob, hp, Pc=Pc: Pc.ap(ob, C, hp * C, [(1, C)]))
                if l < L:
                    hmm(psPT, lambda ob, hp, Pc=Pc: Pc.ap(ob, C, hp * C, [(1, C)]), PTc)
                yield
                Pn, PTn = Pm[l % 2], PTm[l % 2]

                def ev2(r0, nr):
                    S.tt(Qm.ap(r0, nr, 0, [(C, 4), (1, C)]), psP.ap(r0, nr, 0, [(C, 4), (1, C)]), idview(r0, nr), ALU.add)
                    if l < L:
                        S.copy(Pn.ap(r0, nr, 0, [(C, 4), (1, C)]), psP.ap(r0, nr, 0, [(C, 4), (1, C)]), eng="act")
                        S.copy(PTn.ap(r0, nr, 0, [(C, 4), (1, C)]), psPT.ap(r0, nr, 0, [(C, 4), (1, C)]), eng="dve")
                rows(ev2)
                psP.done()
                if psPT is not None:
                    psPT.done()
                (psT,) = yield from getbanks(1)
                hmm(psT, lambda ob, hp: Qm.ap(ob, C, hp * C, [(1, C)]), lambda ob, hp, TTc=TTc: TTc.ap(ob, C, hp * C, [(1, C)]))
                yield
                TTn = TTm[l % 2]
                rows(lambda r0, nr: S.copy(TTn.ap(r0, nr, 0, [(C, 4), (1, C)]), psT.ap(r0, nr, 0, [(C, 4), (1, C)]), eng="act"))
                psT.done()
                Pc = Pn
                PTc = (lambda PTn: (lambda ob, hp: PTn.ap(ob, C, hp * C, [(1, C)])))(PTn)
                TTc = TTn
            pss = []
            pbk, pv_ = yield from getbanks(2)
            for (src, dstb, pst_, co) in [(Bt, ts.Btok, pbk, 0), (Kt, ts.Ktok, pbk, 256), (Vt, ts.Vtok, pv_, 0)]:
                for hp in range(4):
                    for h in range(2):
                        rb, ob = 64 * h, HS * h
                        S.mm(pst_.ap(ob, C, co + hp * 64, [(1, 64)]), src[rb:rb + 64, hp, cs], idb[rb:rb + 64, C_IDS:C_IDS + 64],
                             signal=(hp == 3 and h == 1))
                pss.append((pst_, dstb, co))
            yield
            for pst_, dstb, co in pss:
                rows(lambda r0, nr: S.copy(dstb.ap(r0, nr, 0, [(64, 4), (1, 64)]), pst_.ap(r0, nr, co, [(64, 4), (1, 64)]), eng="act"))
            pbk.done()
            pv_.done()
            if not sample:
                HB = ts.HB
                full = lambda b_: b_.ap(0, 128, 0, [(C, 4), (1, C)])
                pfull = lambda p_: p_.ap(0, 128, 0, [(C, 4), (1, C)])
                bl = lambda b_: (lambda ob, hp: b_.ap(ob, C, hp * C, [(1, C)]))
                N0, N0T, N1, D0, X, D1, X2, D1T = HB
                D0T = TTc
                (p_,) = yield from getbanks(1)
                for hp in range(4):
                    for h in range(2):
                        rb = 64 * h
                        S.mm(p_.ap(rb, 64, hp * C, [(1, C)]), D0T.ap(rb, 64, hp * C, [(1, C)]), idb[rb:rb + 64, C_IDS:C_IDS + 64],
                             signal=(hp == 3 and h == 1))
                yield
                S.copy(full(D0), pfull(p_), eng="act")
                p_.done()
                p_, p2_ = yield from getbanks(2)
                hmm(p_, bl(N0T), bl(D0))
                hmm(p2_, bl(N0), bl(D0T))
                yield
                S.copy(full(X), pfull(p_), eng="act")
                S.copy(full(X2), pfull(p2_), eng="act")
                p_.done()
                p2_.done()
                p_, p2_ = yield from getbanks(2)
                hmm(p_, bl(D0T), bl(X))
                hmm(p2_, bl(D0), bl(X2))
                yield
                S.tt(full(D1), pfull(p_), full(D0), ALU.add)
                S.tt(full(D1T), pfull(p2_), full(D0T), ALU.add)
                p_.done()
                p2_.done()
                (p_,) = yield from getbanks(1)
                hmm(p_, bl(N1), bl(D1T))
                yield
                S.copy(full(X), pfull(p_), eng="act")
                p_.done()
                (p_,) = yield from getbanks(1)
                hmm(p_, bl(D1), bl(X))
                yield
                TTf = TTm[0] if TTc is TTm[1] else TTm[1]
                S.tt(full(TTf), pfull(p_), full(D1T), ALU.add)
                p_.done()
                TTc = TTf
            ts.TTfin = TTc

        def rwkv_chain_gen(c, ts):
            sidx = c if sample else 0
            cs = slice(c * C, (c + 1) * C)
            AkRk, AbRb, Btok, Ktok, Vtok, TTc = ts.AkRk, ts.AbRb, ts.Btok, ts.Ktok, ts.Vtok, ts.TTfin
            (psW,) = yield from getbanks(1)
            for hp in range(4):
                for h in range(2):
                    rb, ob = 64 * h, HS * h
                    S.mm(psW.ap(ob, C, hp * 64, [(1, 64)]), AR.ap(rb, 64, (hp * NCH + c) * 2 * C, [(1, C)]),
                         Hsb[rb:rb + 64, hp, sidx, :], start=True, stop=False, signal=False)
                    S.mm(psW.ap(ob, C, hp * 64, [(1, 64)]), AkRk.ap(ob, C, hp * 2 * C, [(1, C)]),
                         Vtok.ap(ob, C, hp * 64, [(1, 64)]), start=False, stop=True, signal=(hp == 3 and h == 1))
            yield
            rows(lambda r0, nr: S.copy(Wsb.ap(r0, nr, 0, [(64, 4), (1, 64)]), psW.ap(r0, nr, 0, [(64, 4), (1, 64)]), eng="act"))
            psW.done()
            (psU,) = yield from getbanks(1)
            hmm(psU, lambda ob, hp: TTc.ap(ob, C, hp * C, [(1, C)]), lambda ob, hp: Wsb.ap(ob, C, hp * 64, [(1, 64)]), w=64)
            yield
            rows(lambda r0, nr: S.copy(Usb.ap(r0, nr, 0, [(64, 4), (1, 64)]), psU.ap(r0, nr, 0, [(64, 4), (1, 64)]), eng="act"))
            psU.done()
            psO, psH = yield from getbanks(2)
            for hp in range(4):
                for h in range(2):
                    rb, ob = 64 * h, HS * h
                    last = (hp == 3 and h == 1)
                    S.mm(psO.ap(rb, 64, hp * C, [(1, C)]), Hsb[rb:rb + 64, hp, sidx, :],
                         AR.ap(rb, 64, (hp * NCH + c) * 2 * C + C, [(1, C)]), start=True, stop=False, signal=False)
                    S.mm(psO.ap(rb, 64, hp * C, [(1, C)]), Usb.ap(ob, C, hp * 64, [(1, 64)]),
                         AbRb.ap(ob, C, hp * 2 * C + C, [(1, C)]), start=False, stop=False, signal=False)
                    S.mm(psO.ap(rb, 64, hp * C, [(1, C)]), Vtok.ap(ob, C, hp * 64, [(1, 64)]),
                         AkRk.ap(ob, C, hp * 2 * C + C, [(1, C)]), start=False, stop=True, signal=last)
                    S.mm(psH.ap(rb, 64, hp * 64, [(1, 64)]), Btok.ap(ob, C, hp * 64, [(1, 64)]),
                         Usb.ap(ob, C, hp * 64, [(1, 64)]), start=True, stop=False, signal=False)
                    S.mm(psH.ap(rb, 64, hp * 64, [(1, 64)]), Ktok.ap(ob, C, hp * 64, [(1, 64)]),
                         Vtok.ap(ob, C, hp * 64, [(1, 64)]), start=False, stop=True, signal=last)
            yield
            hview = Hst.ap(0, 128, sidx * 64, [(Hst.pstride // 4, 4), (1, 64)])
            hbview = Hsb.ap(0, 128, sidx * 64, [(Hsb.pstride // 4, 4), (1, 64)])
            S.tt(tH[:, :, :], psH.ap(0, 128, 0, [(64, 4), (1, 64)]), hview, ALU.add)
            gv = gamR.ap(0, 128, c, [(NCH, 4), (0, 64)])
            S.tt(hbview, tH[:, :, :], gv, ALU.mult, eng="pool")
            S.tt(hview, tH[:, :, :], gv, ALU.mult, eng="pool")
            OTc, PT1, PT2 = psets[c % 2]
            S.copy(OTc[:, :, :], psO.ap(0, 128, 0, [(C, 4), (1, C)]), eng="act")
            psO.done()
            psH.done()
            post_gens[c] = rwkv_post_gen(c)

        def rwkv_post_gen(c):
            OTc, PT1, PT2 = psets[c % 2]
            flat = lambda b_: b_.ap(0, 128, 0, [(1, 4 * C)])
            (ps1,) = yield from getbanks(1)
            S.mm(ps1.ap(0, 128, 0, [(1, 4 * C)]), idf[:, C_BMEAN:C_BMEAN + 128], flat(OTc))
            yield
            S.tt(flat(PT1), flat(OTc), ps1.ap(0, 128, 0, [(1, 4 * C)]), ALU.subtract)
            ps1.done()
            S.tt(flat(PT2), flat(PT1), flat(PT1), ALU.mult, eng="pool")
            (ps2,) = yield from getbanks(1)
            S.mm(ps2.ap(0, 128, 0, [(1, 4 * C)]), idf[:, C_BMEAN:C_BMEAN + 128], flat(PT2))
            yield
            S.act(flat(PT2), ps2.ap(0, 128, 0, [(1, 4 * C)]), AF.Ln, bias=self.epsc[:, 0:1], scale=1.0)
            ps2.done()
            S.act(flat(PT2), flat(PT2), AF.Exp, scale=-0.5)
            S.tt(flat(PT1), flat(PT1), flat(PT2), ALU.mult, eng="pool")
            lw = vf.ap(0, 128, VC["lnx_w"], [(1, 4), (0, C)])
            lb_ = vf.ap(0, 128, VC["lnx_b"], [(1, 4), (0, C)])
            S.tt(PT1[:, :, :], PT1[:, :, :], lw, ALU.mult, eng="pool")
            S.tt(PT1[:, :, :], PT1[:, :, :], lb_, ALU.add, eng="pool")
            S.tt(PT1[:, :, :], PT1[:, :, :], bonus.ap(0, 128, c * C, [(N, 4), (1, C)]), ALU.add, eng="pool")
            S.tt(mixT.ap(0, 128, c * C, [(N, 4), (1, C)]), PT1[:, :, :], gbf.ap(0, 128, c * C, [(N, 4), (1, C)]), ALU.mult, eng="pool")

        def hgrn_gen():
            mi = idb.ap(0, C, (C_M1S + 4) if sample else (C_M1 + 64), [(0, 4), (1, C)])
            for c in range(NCH):
                sidx = c if sample else 0
                cs = slice(c * C, (c + 1) * C)
                psK, psV = yield from getbanks(2)
                for h in range(4):
                    S.mm(psK.ap(0, C, h * 128, [(1, 128)]), Kh[:, h, cs], idb[:, C_ID:C_ID + 128], signal=(h == 3))
                    S.mm(psV.ap(0, C, h * 128, [(1, 128)]), Vh[:, h, cs], idb[:, C_ID:C_ID + 128], signal=(h == 3))
                yield
                S.copy(hKtok[0:C, :, :], psK.ap(0, C, 0, [(128, 4), (1, 128)]), eng="act")
                S.copy(hVtok[0:C, :, :], psV.ap(0, C, 0, [(128, 4), (1, 128)]), eng="act")
                psK.done()
                psV.done()
                psA, psG = yield from getbanks(2)
                for h in range(4):
                    S.mm(psA.ap(0, C, h * C, [(1, C)]), Kh[:, h, cs], Qh[:, h, cs], signal=(h == 3))
                for h in range(4):
                    S.mm(psG.ap(0, 128, h * 128, [(1, 128)]), hKtok[0:C, h, :], hVtok[0:C, h, :], signal=(h == 3))
                yield
                S.tt(hAT[0:C, :, :], psA.ap(0, C, 0, [(C, 4), (1, C)]), mi, ALU.mult)
                psA.done()
                sview = Shs.ap(0, 128, sidx * 128, [(Shs.pstride // 4, 4), (1, 128)])
                sbview = Shb.ap(0, 128, sidx * 128, [(Shb.pstride // 4, 4), (1, 128)])
                S.tt(tS[:, :, :], psG.ap(0, 128, 0, [(128, 4), (1, 128)]), sview, ALU.add)
                psG.done()
                (psO,) = yield from getbanks(1)
                for h in range(4):
                    S.mm(psO.ap(0, 128, h * C, [(1, C)]), Shb[:, h, sidx, :], Qh[:, h, cs], start=(h == 0), stop=False, signal=False)
                for h in range(4):
                    S.mm(psO.ap(0, 128, h * C, [(1, C)]), hVtok[0:C, h, :], hAT[0:C, h, :], start=False, stop=True, signal=(h == 3))
                gv = gamH.ap(0, 128, c, [(NCH, 4), (0, 128)])
                S.tt(sbview, tS[:, :, :], gv, ALU.mult, eng="pool")
                S.tt(sview, tS[:, :, :], gv, ALU.mult, eng="pool")
                yield
                flat = lambda b_: b_.ap(0, 128, 0, [(1, 4 * C)])
                S.copy(HO[:, :, :], psO.ap(0, 128, 0, [(C, 4), (1, C)]), eng="act")
                psO.done()
                S.tt(flat(HT1), flat(HO), flat(HO), ALU.mult, eng="pool")
                (ps1,) = yield from getbanks(1)
                S.mm(ps1.ap(0, 128, 0, [(1, 4 * C)]), idf[:, C_AMEAN:C_AMEAN + 128], flat(HT1))
                yield
                S.act(flat(HT1), ps1.ap(0, 128, 0, [(1, 4 * C)]), AF.Ln, bias=self.epsc[:, 1:2], scale=1.0)
                ps1.done()
                S.act(flat(HT1), flat(HT1), AF.Exp, scale=-0.5)
                S.tt(flat(HT1), flat(HT1), flat(HO), ALU.mult, eng="pool")
                S.stt(mixT.ap(0, 128, 4 * N + c * C, [(N, 4), (1, C)]), HT1[:, :, :], vf[:, VC["gnorm"]:VC["gnorm"] + 1],
                      ogs.ap(0, 128, c * C, [(N, 4), (1, C)]), ALU.mult, ALU.mult)

        def run_units():
            free = list(range(NS))
            inv = {}
            inv_done = {}
            chain = None
            next_inv = 0
            next_chain = 0
            hg = hgrn_gen()
            hg_alive = True
            while next_chain < NCH or chain is not None or hg_alive or post_gens:
                while free and next_inv < NCH:
                    si = free.pop(0)
                    inv[next_inv] = (rwkv_inv_gen(next_inv, tsets[si]), si)
                    next_inv += 1
                if chain is None and next_chain in inv_done and (next_chain - 2) not in post_gens:
                    si = inv_done.pop(next_chain)
                    chain = (rwkv_chain_gen(next_chain, tsets[si]), next_chain, si)
                if chain is not None:
                    try:
                        next(chain[0])
                    except StopIteration:
                        free.append(chain[2])
                        next_chain += 1
                        chain = None
                for cc in sorted(list(inv.keys())):
                    g_, si = inv[cc]
                    try:
                        next(g_)
                    except StopIteration:
                        del inv[cc]
                        inv_done[cc] = si
                for cc in sorted(list(post_gens.keys())):
                    try:
                        next(post_gens[cc])
                    except StopIteration:
                        del post_gens[cc]
                if hg_alive:
                    try:
                        next(hg)
                    except StopIteration:
                        hg_alive = False

        if self.stage < 0.3:
            return
        for blk in range(nblk):
            xsrc = I["xs"] if sample else I["xp"][blk * 512:(blk + 1) * 512, :]
            ydst = O["ys"] if sample else O["yp"][blk * 512:(blk + 1) * 512, :]
            for t in range(NT):
                S.dma("sp", xbt[t][0:R, :].ap, xsrc[t * 128:t * 128 + R, :], writes=[xbt[t]])
            S.mark(('s' if sample else 'p') + str(blk) + ':norm1')
            rmsnorm_T(self.G1, 0, hT)
            S.mark(('s' if sample else 'p') + str(blk) + ':inproj')
            if self.stage < 0.35:
                return
            def inproj_stream():
                for gi in range(0, 30, 4):
                    chunks = INPROJ_ORDER[gi:gi + 4]
                    wt = self.get_tile()
                    for j, c in enumerate(chunks):
                        if c < 12 and c // 4 == 2:
                            while active_rw:
                                yield
                        if 14 <= c < 22:
                            while active_hg:
                                yield
                        ps = self.ps()
                        for k in range(8):
                            S.mm(ps.full(128, N), wt.ap(0, 128, k * 512 + j * 128, [(1, 128)]), hT[:, k, :], start=(k == 0), stop=(k == 7))
                        post_inproj(c, ps)
                        yield

            ip_ = inproj_stream()
            ip_alive = True
            while ip_alive or active_rw or active_hg:
                if ip_alive:
                    try:
                        next(ip_)
                    except StopIteration:
                        ip_alive = False
                for lst_ in (active_rw, active_hg):
                    for g_ in list(lst_):
                        try:
                            next(g_)
                        except StopIteration:
                            lst_.remove(g_)
            if self.stage < 0.4:
                return
            S.mark(('s' if sample else 'p') + str(blk) + ':units')
            run_units()
            S.mark(('s' if sample else 'p') + str(blk) + ':outproj')
            if sample and self.debug:
                dtmp = sb('dtmp', [128, 8 * N])
                S.copy(dtmp[:, :], mixT.ap(0, 128, 0, [(1, 8 * N)]))
                self.dbg(dtmp[:, :], 8 * N)
            if self.stage < 0.6:
                return
            for hf in range(2):
                wt = self.get_tile()
                for t in range(NT):
                    ps = self.ps()
                    for k in range(8):
                        S.mm(ps.full(R, 512), mixT[:, k, t * 128:t * 128 + R], wt.ap(0, 128, k * 512, [(1, 512)]),
                             start=(k == 0), stop=(k == 7))
                    tmp = self.tmp512(t)
                    S.tt(tmp[0:R, :], ps.full(R, 512), GT1[0:R, hf * 512:(hf + 1) * 512], ALU.mult)
                    S.tt(xbt[t][0:R, hf * 512:(hf + 1) * 512], xbt[t][0:R, hf * 512:(hf + 1) * 512], tmp[0:R, :], ALU.add, eng="pool")
            if sample and self.debug:
                self.dbg(xbt[0][0:R, :], 1024, R)
            if self.stage < 0.7:
                return
            S.mark(('s' if sample else 'p') + str(blk) + ':norm2')
            rmsnorm_T(self.G2, 16, hT)
            S.mark(('s' if sample else 'p') + str(blk) + ':mlp')
            if sample and self.debug:
                S.copy(dtmp[:, :], hT.ap(0, 128, 0, [(1, 8 * N)]))
                self.dbg(dtmp[:, :], 8 * N)
            for g in range(8):
                wu = self.get_tile()
                wd = self.get_tile()
                u = uT[g % 2]
                for j in range(4):
                    ps = self.ps()
                    for k in range(8):
                        S.mm(ps.full(128, N), wu.ap(0, 128, k * 512 + j * 128, [(1, 128)]), hT[:, k, :], start=(k == 0), stop=(k == 7))
                    tmp = self.tmp512(j)
                    S.act(tmp.ap(0, 128, 0, [(1, N)]), ps.full(128, N), AF.Relu)
                    S.tt(u[:, j, :], tmp.ap(0, 128, 0, [(1, N)]), tmp.ap(0, 128, 0, [(1, N)]), ALU.mult, eng="pool")
                for t in range(NT):
                    for hf in range(2):
                        ps = self.ps()
                        for j in range(4):
                            S.mm(ps.full(R, 512), u[:, j, t * 128:t * 128 + R], wd.ap(0, 128, j * 1024 + hf * 512, [(1, 512)]),
                                 start=(j == 0), stop=(j == 3))
                        dv = dacc[0:R, t, hf * 512:(hf + 1) * 512]
                        if g == 0:
                            S.copy(dv, ps.full(R, 512), eng="act")
                        else:
                            S.tt(dv, ps.full(R, 512), dv, ALU.add)
            if sample and self.debug:
                self.dbg(dacc[0:R, 0, :], 1024, R)
                self.dbg(GT1[0:R, :], 1024, R)
                self.dbg(GT2[0:R, :], 1024, R)
            S.mark(('s' if sample else 'p') + str(blk) + ':final')
            for t in range(NT):
                for hf in range(2):
                    tmp = self.tmp512(hf)
                    S.tt(tmp[0:R, :], dacc[0:R, t, hf * 512:(hf + 1) * 512], GT2[0:R, hf * 512:(hf + 1) * 512], ALU.mult, eng="pool")
                    S.tt(xbt[t][0:R, hf * 512:(hf + 1) * 512], xbt[t][0:R, hf * 512:(hf + 1) * 512], tmp[0:R, :], ALU.add)
            for t in range(NT):
                S.act(xn[0:R, t, :], xbt[t][0:R, :], AF.Square, accum=st8[0:R, t:t + 1])
            S.ts(st8[0:R, 4:4 + NT], st8[0:R, 0:NT], 1.0 / D, ALU.mult, NORM_EPS, ALU.add)
            S.act(st8[0:R, 4:4 + NT], st8[0:R, 4:4 + NT], AF.Sqrt)
            S.recip(st8[0:R, 8:8 + NT], st8[0:R, 4:4 + NT])
            for t in range(NT):
                S.stt(xbt[t][0:R, :], xbt[t][0:R, :], st8[0:R, 8 + t:9 + t], self.normf[0:R, :], ALU.mult, ALU.mult)
                S.dma("sp", ydst[t * 128:t * 128 + R, :], xbt[t][0:R, :].ap, reads=[xbt[t]])

        if self.stage < 0.9:
            return
        if sample:
            shsb = sb("shsb", [SB, 14, 128])
            for c0 in range(0, 14, 4):
                ps = self.ps()
                n = min(4, 14 - c0)
                for c in range(c0, c0 + n):
                    S.mm(ps.ap(0, SB, (c - c0) * 128, [(1, 128)]), sprev[:, c, :], idf[:, C_ID:C_ID + 128], signal=(c == c0 + n - 1))
                S.copy(shsb[0:SB, c0:c0 + n, :], ps.ap(0, SB, 0, [(128, n), (1, 128)]), eng="act")
            S.dma("sp", O["shs"], shsb[:, :, :].ap, reads=[shsb])
            for hp in range(4):
                for b0 in range(0, SB, 8):
                    ps = self.ps()
                    for b in range(b0, b0 + 8):
                        for h in range(2):
                            S.mm(ps.ap(64 * h, 64, (b - b0) * 64, [(1, 64)]), Hst[64 * h:64 * h + 64, hp, b, :],
                                 idf[64 * h:64 * h + 64, C_IDS:C_IDS + 64], signal=(b == b0 + 7 and h == 1))
                    S.copy(Sld[:, hp, b0:b0 + 8, :], ps.ap(0, 128, 0, [(64, 8), (1, 64)]), eng="act")
                S.dma("sp", O["wkvs"][:, 2 * hp:2 * hp + 2, :, :].rearrange("b h v k -> (h v) b k"), Sld[:, hp, :, :].ap, reads=[Sld])
            for h in range(4):
                S.dma("sp", O["hgs"][:, h, :, :].rearrange("b f i -> f b i"), Shs[:, h, :, :].ap, reads=[Shs])
        else:
            shpb = sb("shpb", [14, 128])
            ps = self.ps()
            S.mm(ps.ap(0, 14, 0, [(1, 128)]), sprev[:, :, 0], idf[:, C_ID:C_ID + 128])
            S.copy(shpb[:, :], ps.ap(0, 14, 0, [(1, 128)]), eng="act")
            S.dma("sp", O["shp"], shpb[:, :].ap, reads=[shpb])
            ps = self.ps()
            for hp in range(4):
                for h in range(2):
                    S.mm(ps.ap(64 * h, 64, hp * 64, [(1, 64)]), Hst[64 * h:64 * h + 64, hp, 0, :],
                         idf[64 * h:64 * h + 64, C_IDS:C_IDS + 64], signal=(hp == 3 and h == 1))
            S.copy(Sld[:, :, 0, :], ps.ap(0, 128, 0, [(64, 4), (1, 64)]), eng="act")
            S.dma("sp", O["wkvp"].rearrange("(hp h) v k -> (h v) hp k", h=2), Sld[:, :, 0, :].ap, reads=[Sld])
            S.dma("sp", O["hgp"].rearrange("h f i -> f h i"), Shs[:, :, 0, :].ap, reads=[Shs])

    def tmp512(self, i):
        return self._tmp[i % len(self._tmp)]


_NC_CACHE = {}


def _prep_inputs(inp):
    f = lambda a: np.ascontiguousarray(np.asarray(a, dtype=np.float32))
    chunks = lambda v: f(v).reshape(-1, 128).T
    vec = np.zeros((128, NVC), np.float32)

    def put(name, v):
        c = chunks(v)
        vec[:, VC[name]:VC[name] + c.shape[1]] = c

    put("norm1", inp["norm1"][0]); put("norm2", inp["norm2"][0]); put("b_ada", inp["b_ada"][0])
    put("mu", inp["mu_shift"][0]); put("w0", inp["w0"][0]); put("a0", inp["a0"][0]); put("k_k", inp["k_k"][0])
    put("k_a", inp["k_a"][0]); put("r_k", np.asarray(inp["r_k"][0]).reshape(-1)); put("lnx_w", inp["lnx_w"][0])
    put("lnx_b", inp["lnx_b"][0]); put("lb0", inp["hgrn_lb"][0]); put("lb1", inp["hgrn_lb"][1])
    put("gnorm", inp["hgrn_gnorm"][0])
    b_ada = f(inp["b_ada"][0])
    shared = {
        "w_ada": f(inp["w_ada"][0]),
        "w_in": np.ascontiguousarray(f(inp["w_in"][0])[:, np.concatenate([np.arange(c * 128, (c + 1) * 128) for c in INPROJ_ORDER])]),
        "w_out": f(inp["w_out"][0]),
        "w_up": f(inp["w_up"][0]), "w_down": f(inp["w_down"][0]), "w_dec": f(inp["w_decay_up"][0]),
        "w_aaa": f(inp["w_aaa_up"][0]), "w_gate": f(inp["w_gate_up"][0]), "vecF": vec,
        "normf": np.ascontiguousarray(np.broadcast_to(f(inp["norm_f"])[None, :], (128, D))),
        "bgt": np.ascontiguousarray(np.concatenate([b_ada[2 * D:3 * D], b_ada[5 * D:6 * D]])[None, :]),
        "cst": make_consts(),
    }
    xp, xs = f(inp["x_prompt"]), f(inp["x_sample"])
    cp, cs_ = f(inp["c_prompt"]), f(inp["c_sample"])
    sst, swkv, shg = f(inp["state_shift"][0]), f(inp["state_wkv"][0]), f(inp["state_hgrn"][0])
    maps = []
    for c in range(NCORES):
        bs = slice(c * SB, (c + 1) * SB)
        call = np.concatenate([cp[c:c + 1], cs_[bs]], axis=0)
        cT = np.ascontiguousarray(call.reshape(17, 8, 128).transpose(2, 1, 0))
        sstc = np.ascontiguousarray(sst[bs].reshape(SB, 14, 128).transpose(2, 1, 0))
        m = dict(shared)
        m.update({"xp": xp[c], "xs": np.ascontiguousarray(xs[bs].reshape(SB * ST, D)), "cT": cT, "sst": sstc,
                  "swkv": np.ascontiguousarray(swkv[bs]), "shg": np.ascontiguousarray(shg[bs])})
        maps.append(m)
    return maps


def _run(inp, stage=99, debug=False, lite=False):
    key = (stage, debug, lite)
    if key not in _NC_CACHE:
        _NC_CACHE[key] = Builder(stage=stage, debug=debug, lite=lite).build()
    nc = _NC_CACHE[key]
    maps = _prep_inputs(inp)
    if lite:
        for m in maps:
            for n in ["w_ada", "w_in", "w_out", "w_up", "w_down"]:
                m[n] = np.zeros((8, 8), np.float32)
    res = run_bass_kernel_spmd(nc, maps, core_ids=list(range(NCORES)))
    return res.results


def kernel(**inp):
    rs = _run(inp)
    g = lambda k: [np.asarray(r[k], dtype=np.float32) for r in rs]
    y_prompt = np.stack(g("yp"), 0)
    y_sample = np.concatenate(g("ys"), 0).reshape(NCORES * SB, ST, D)
    shift_p = np.stack([a.reshape(SHW) for a in g("shp")], 0)[None]
    wkv_p = np.stack(g("wkvp"), 0)[None]
    hgrn_p = np.stack(g("hgp"), 0)[None]
    shift_s = np.concatenate([a.reshape(SB, SHW) for a in g("shs")], 0)[None]
    wkv_s = np.concatenate(g("wkvs"), 0)[None]
    hgrn_s = np.concatenate(g("hgs"), 0)[None]
    return (y_prompt, y_sample, shift_p, wkv_p, hgrn_p, shift_s, wkv_s, hgrn_s)
```

```python
import contextlib
import numpy as np
import concourse.bass as bass
import concourse.mybir as mybir
from concourse.bass_utils import run_bass_kernel_spmd

F32 = mybir.dt.float32
BF16 = mybir.dt.bfloat16
AF = mybir.ActivationFunctionType
ALU = mybir.AluOpType

D = 1024
NCORES = 8
SEQ = 2048
SB = 16
ST = 4
DFF = 4096
INW = 3840
SHW = 1792
WDEC = 0.6065306597126334
NORM_EPS = 1e-6
LNX_EPS = 64e-5

CH_R, CH_K, CH_V, CH_WA, CH_G, CH_Q, CH_F, CH_I, CH_OG = 0, 4, 8, 12, 13, 14, 18, 22, 26
INPROJ_ORDER = [12, 13, 0, 4, 8, 14, 18, 22, 26, 1, 5, 9, 15, 19, 23, 27,
                2, 6, 10, 16, 20, 24, 28, 3, 7, 11, 17, 21, 25, 29]

VC = {}
_off = 0
for _n, _c in [("norm1", 8), ("norm2", 8), ("b_ada", 48), ("mu", 14), ("w0", 4), ("a0", 4), ("k_k", 4),
               ("k_a", 4), ("r_k", 4), ("lnx_w", 4), ("lnx_b", 4), ("lb0", 4), ("lb1", 4), ("gnorm", 1)]:
    VC[_n] = _off
    _off += _c
NVC = _off


class Buf:
    def __init__(self, name, t):
        self.name = name
        self.w = None
        self.r = {}
        self.aliases = []
        self.set_t(t)

    def set_t(self, t):
        self.t = t
        if t is None:
            return
        if hasattr(t, "offset") and hasattr(t, "ap"):
            self.th = t.tensor
            self.pstride = int(t.ap[0][0])
            self.base = int(t.offset)
        else:
            self.th = t.tensor if hasattr(t, "tensor") else t
            self.pstride = int(np.prod(list(t.shape)[1:]))
            self.base = 0

    def __getitem__(self, idx):
        return View(self, self.t[idx])

    def ap(self, p0, npart, off, dims):
        return View(self, bass.AP(self.th, self.base + p0 * self.pstride + off,
                                  [[self.pstride, npart]] + [list(d) for d in dims]))


class View:
    def __init__(self, buf, ap):
        self.buf = buf
        self.ap = ap

    def re(self, pat, **kw):
        return View(self.buf, self.ap.rearrange(pat, **kw))

    def bc(self, shape):
        return View(self.buf, self.ap.to_broadcast(shape))

    def __getitem__(self, idx):
        return View(self.buf, self.ap[idx])


class Eng:
    def __init__(self, name, sem):
        self.name = name
        self.sem = sem
        self.count = 0
        self.waited = {}
        self.ops = []


class DSem:
    def __init__(self, sem):
        self.sem = sem
        self.total = 0


class Sched:
    def __init__(self, nc, stack):
        self.nc = nc
        self.E = {}
        for n in ["pe", "act", "dve", "pool", "sp"]:
            self.E[n] = Eng(n, stack.enter_context(nc.semaphore("s_" + n)))
        self.dsems = [DSem(stack.enter_context(nc.semaphore("d%d" % i))) for i in range(32)]
        self.drr = 0
        self.nops = 0
        self.npe = 0
        self.marks = []

    def _collect(self, E, reads, writes, extra=()):
        deps = {}

        def need(s, v):
            if v > deps.get(id(s), (s, 0))[1]:
                deps[id(s)] = (s, v)

        for b0 in reads:
            for b in [b0] + b0.aliases:
                if b.w is not None:
                    need(*b.w)
            if getattr(b0, "excl", False):
                for s, v in b0.r.values():
                    if s is not E.sem:
                        need(s, v)
        for b0 in writes:
            for b in [b0] + b0.aliases:
                if b.w is not None:
                    need(*b.w)
                for s, v in b.r.values():
                    need(s, v)
        for s, v in extra:
            need(s, v)
        waits = []
        for s, v in deps.values():
            if E.name == "pe" and s is E.sem:
                continue
            if E.waited.get(id(s), 0) < v:
                E.waited[id(s)] = v
                waits.append((s, v))
        return waits

    @staticmethod
    def _update(tok, reads, writes):
        s, v = tok
        for b in reads:
            if b.r.get(id(s), (s, 0))[1] < v:
                b.r[id(s)] = (s, v)
        for b in writes:
            b.w = tok
            b.r = {}

    def emit(self, en, fn, reads=(), writes=(), signal=True):
        E = self.E[en]
        reads = [v.buf if isinstance(v, View) else v for v in reads]
        writes = [v.buf if isinstance(v, View) else v for v in writes]
        waits = self._collect(E, reads, writes)
        if signal:
            E.count += 1
            tokv = E.count
        else:
            tokv = E.count + 1
        E.ops.append((waits, fn, ("sig", E.sem) if signal else None))
        self._update((E.sem, tokv), reads, writes)
        self.nops += 1

    def dma(self, en, out, in_, reads=(), writes=(), **kw):
        E = self.E[en]
        reads = [v.buf if isinstance(v, View) else v for v in reads]
        writes = [v.buf if isinstance(v, View) else v for v in writes]
        ds = self.dsems[self.drr % len(self.dsems)]
        self.drr += 1
        waits = self._collect(E, reads, writes, extra=[(ds.sem, ds.total)] if ds.total else [])
        ds.total += 16
        E.ops.append((waits, lambda h: h.dma_start(out=out, in_=in_, **kw), ("dma", ds.sem)))
        self._update((ds.sem, ds.total), reads, writes)
        self.nops += 1

    def barrier(self):
        for E in self.E.values():
            waits = []
            for O in self.E.values():
                if O is E or O.count == 0:
                    continue
                if E.waited.get(id(O.sem), 0) < O.count:
                    E.waited[id(O.sem)] = O.count
                    waits.append((O.sem, O.count))
            for ds in self.dsems:
                if ds.total and E.waited.get(id(ds.sem), 0) < ds.total:
                    E.waited[id(ds.sem)] = ds.total
                    waits.append((ds.sem, ds.total))
            if waits:
                E.ops.append((waits, None, None))

    def flush(self):
        nc = self.nc
        hmap = {"pe": "tensor", "act": "scalar", "dve": "vector", "pool": "gpsimd", "sp": "sync"}
        with nc.Block() as block:
            for n, E in self.E.items():
                ops = E.ops

                def body(h, ops=ops):
                    for waits, fn, sig in ops:
                        for s, v in waits:
                            h.wait_ge(s, v)
                        if fn is None:
                            continue
                        ins = fn(h)
                        if sig is not None:
                            if sig[0] == "sig":
                                ins.then_inc(sig[1], 1)
                            else:
                                ins.then_inc(sig[1], 16)

                getattr(block, hmap[n])(body)
        for E in self.E.values():
            E.ops = []

    def mark(self, name):
        self.marks.append((name, self.npe))

    def mm(self, out, lhsT, rhs, start=True, stop=True, signal=None):
        self.npe += 1
        if signal is None:
            signal = stop
        self.emit("pe", lambda e: e.matmul(out.ap, lhsT.ap, rhs.ap, start=start, stop=stop),
                  reads=[lhsT, rhs], writes=[out], signal=signal)

    def act(self, out, in_, func, bias=None, scale=None, accum=None, eng="act"):
        kw = {}
        rd = [in_]
        wr = [out]
        if bias is not None:
            if isinstance(bias, View):
                kw["bias"] = bias.ap
                rd.append(bias)
            else:
                kw["bias"] = bias
        if scale is not None:
            if isinstance(scale, View):
                kw["scale"] = scale.ap
                rd.append(scale)
            else:
                kw["scale"] = scale
        if accum is not None:
            kw["accum_out"] = accum.ap
            wr.append(accum)
        self.emit(eng, lambda e: e.activation(out=out.ap, in_=in_.ap, func=func, **kw), reads=rd, writes=wr)

    def tt(self, out, in0, in1, op, eng="dve"):
        self.emit(eng, lambda e: e.tensor_tensor(out=out.ap, in0=in0.ap, in1=in1.ap, op=op),
                  reads=[in0, in1], writes=[out])

    def ts(self, out, in0, s1, op0, s2=None, op1=None, eng="dve"):
        rd = [in0]
        a1 = s1
        a2 = s2
        if isinstance(s1, View):
            rd.append(s1)
            a1 = s1.ap
        if isinstance(s2, View):
            rd.append(s2)
            a2 = s2.ap
        if op1 is None:
            self.emit(eng, lambda e: e.tensor_scalar(out=out.ap, in0=in0.ap, scalar1=a1, scalar2=None, op0=op0),
                      reads=rd, writes=[out])
        else:
            self.emit(eng, lambda e: e.tensor_scalar(out=out.ap, in0=in0.ap, scalar1=a1, scalar2=a2, op0=op0, op1=op1),
                      reads=rd, writes=[out])

    def stt(self, out, in0, s, in1, op0, op1):
        rd = [in0, in1]
        a = s
        if isinstance(s, View):
            rd.append(s)
            a = s.ap
        self.emit("dve", lambda e: e.scalar_tensor_tensor(out=out.ap, in0=in0.ap, scalar=a, in1=in1.ap, op0=op0, op1=op1),
                  reads=rd, writes=[out])

    def copy(self, out, in_, eng="dve"):
        if eng == "act":
            self.emit("act", lambda e: e.activation(out=out.ap, in_=in_.ap, func=AF.Identity), reads=[in_], writes=[out])
        else:
            self.emit(eng, lambda e: e.tensor_copy(out=out.ap, in_=in_.ap), reads=[in_], writes=[out])

    def memset(self, out, val, eng="pool"):
        self.emit(eng, lambda e: e.memset(out.ap, val), reads=[], writes=[out])

    def recip(self, out, in_):
        self.emit("dve", lambda e: e.reciprocal(out=out.ap, in_=in_.ap), reads=[in_], writes=[out])

    def scan(self, out, d0, d1, init, op0, op1):
        self.emit("dve", lambda e: e.tensor_tensor_scan(out=out.ap, data0=d0.ap, data1=d1.ap, initial=init, op0=op0, op1=op1),
                  reads=[d0, d1], writes=[out])


NCST = 1296
C_ID, C_IDS, C_BONE, C_BMEAN, C_AMEAN = 0, 128, 192, 320, 448
C_M1, C_MS = 576, 704
C_M1S, C_MSS = 768, 776
C_ONES = 780
C_M2H = 912
C_BD = 1040
C_N0, C_N0T, C_N1 = 1104, 1168, 1232


def make_consts():
    c = np.zeros((128, NCST), np.float32)
    c[:, C_ID:C_ID + 128] = np.eye(128)
    c[0:64, C_IDS:C_IDS + 64] = np.eye(64)
    c[64:128, C_IDS:C_IDS + 64] = np.eye(64)
    blk = np.kron(np.eye(2), np.ones((64, 64)))
    c[:, C_BONE:C_BONE + 128] = blk
    c[:, C_BMEAN:C_BMEAN + 128] = blk / 64.0
    c[:, C_AMEAN:C_AMEAN + 128] = 1.0 / 128.0
    s = np.arange(128)[:, None] % 64
    t = np.arange(64)[None, :]
    c[:, C_M1:C_M1 + 64] = (s < t)
    c[:, C_M1 + 64:C_M1 + 128] = (s <= t)
    c[:, C_MS:C_MS + 64] = (t < s)
    s4 = np.arange(128)[:, None] % 32
    t4 = np.arange(4)[None, :]
    ok = (s4 < 4)
    c[:, C_M1S:C_M1S + 4] = (s4 < t4) & ok
    c[:, C_M1S + 4:C_M1S + 8] = (s4 <= t4) & ok
    c[:, C_MSS:C_MSS + 4] = (t4 < s4) & ok
    c[:, C_ONES:C_ONES + 128] = 1.0
    r = np.arange(128)[:, None] % 64
    q = np.arange(64)[None, :]
    same16 = (r // 16) == (q // 16)
    same32 = (r // 32) == (q // 32)
    c[:, C_M2H:C_M2H + 64] = (r < q) & same16
    c[:, C_M2H + 64:C_M2H + 128] = (r <= q)
    c[:, C_BD:C_BD + 64] = (q < r) & same16
    c[:, C_N0:C_N0 + 64] = (q < r) & same32 & ~same16
    c[:, C_N0T:C_N0T + 64] = (r < q) & same32 & ~same16
    c[:, C_N1:C_N1 + 64] = (r >= 32) & (q < 32)
    return c


class PsBank:
    def __init__(self, buf, col):
        self.buf = buf
        self.col = col

    def ap(self, p0, npart, off, dims):
        return View(self.buf, bass.AP(self.buf.th, p0 * 4096 + self.col + off, [[4096, npart]] + [list(d) for d in dims]))

    def full(self, npart=128, n=512):
        return self.ap(0, npart, 0, [(1, n)])

    def done(self):
        self.owner.ps_open.discard(self.idx)


class Builder:
    def __init__(self, stage=99, debug=False, lite=False):
        self.stage = stage
        self.debug = debug
        self.lite = lite
        self.dbg_col = 0

    def dram_in(self, name, shape, dt=F32):
        return self.nc.dram_tensor(name, list(shape), dt, kind="ExternalInput").ap()

    def dram_out(self, name, shape, dt=F32):
        return self.nc.dram_tensor(name, list(shape), dt, kind="ExternalOutput").ap()

    def sb(self, stack, name, shape, dt=F32):
        t = stack.enter_context(self.nc.sbuf_tensor("sb_" + name, list(shape), dt))
        return Buf(name, t)

    def dbg(self, view, ncols, npart=128):
        if not self.debug:
            return
        c0 = self.dbg_col
        self.dbg_col += ncols
        assert self.dbg_col <= 8192
        self.S.dma("sp", self.O["dbg"][0:npart, c0:c0 + ncols], view.ap, reads=[view])
        return c0

    def build(self):
        nc = bass.Bass("TRN2", target_bir_lowering=False)
        self.nc = nc
        I = {}
        I["xp"] = self.dram_in("xp", [SEQ, D])
        I["xs"] = self.dram_in("xs", [SB * ST, D])
        I["cT"] = self.dram_in("cT", [128, 8, 17])
        I["sst"] = self.dram_in("sst", [128, 14, SB])
        I["swkv"] = self.dram_in("swkv", [SB, 8, 64, 64])
        I["shg"] = self.dram_in("shg", [SB, 4, 128, 128])
        if self.lite:
            for n in ["w_ada", "w_in", "w_out", "w_up", "w_down"]:
                I[n] = self.dram_in(n, [8, 8])
        else:
            I["w_ada"] = self.dram_in("w_ada", [D, 6 * D])
            I["w_in"] = self.dram_in("w_in", [D, INW])
            I["w_out"] = self.dram_in("w_out", [D, D])
            I["w_up"] = self.dram_in("w_up", [D, DFF])
            I["w_down"] = self.dram_in("w_down", [DFF, D])
        I["w_dec"] = self.dram_in("w_dec", [64, 512])
        I["w_aaa"] = self.dram_in("w_aaa", [64, 512])
        I["w_gate"] = self.dram_in("w_gate", [128, 512])
        I["vecF"] = self.dram_in("vecF", [128, NVC])
        I["normf"] = self.dram_in("normf", [128, D])
        I["bgt"] = self.dram_in("bgt", [1, 2 * D])
        I["cst"] = self.dram_in("cst", [128, NCST])
        O = {}
        O["yp"] = self.dram_out("yp", [SEQ, D])
        O["ys"] = self.dram_out("ys", [SB * ST, D])
        O["shp"] = self.dram_out("shp", [14, 128])
        O["wkvp"] = self.dram_out("wkvp", [8, 64, 64])
        O["hgp"] = self.dram_out("hgp", [4, 128, 128])
        O["shs"] = self.dram_out("shs", [SB, 14, 128])
        O["wkvs"] = self.dram_out("wkvs", [SB, 8, 64, 64])
        O["hgs"] = self.dram_out("hgs", [SB, 4, 128, 128])
        if self.debug:
            O["dbg"] = self.dram_out("dbg", [128, 8192])
        self.I, self.O = I, O

        with contextlib.ExitStack() as stack:
            S = Sched(nc, stack)
            self.S = S
            self.setup_persistent(stack)
            if self.stage < 0.1:
                S.barrier()
                S.flush()
                return nc
            with contextlib.ExitStack() as st2:
                self.run_phase(st2, sample=True)
                S.barrier()
                S.flush()
            if self.stage >= 2:
                with contextlib.ExitStack() as st3:
                    self.run_phase(st3, sample=False)
                    S.barrier()
                    S.flush()
        return nc

    def ps(self, hold=False):
        for _ in range(8):
            i = self.ps_rr % 8
            self.ps_rr += 1
            if i not in self.ps_open:
                if hold:
                    self.ps_open.add(i)
                pb = PsBank(self.psum[i], i * 512)
                pb.owner = self
                pb.idx = i
                return pb
        raise RuntimeError("no free PSUM bank")

    def setup_persistent(self, stack):
        S, I = self.S, self.I
        sb = lambda n, s, d=F32: self.sb(stack, n, s, d)
        pst = stack.enter_context(self.nc.psum_tensor("psall", [128, 4096], F32))
        self.psum = []
        for i in range(8):
            b = Buf("ps%d" % i, None)
            b.t = pst
            b.th = pst.tensor if hasattr(pst, "tensor") else pst
            b.pstride = 4096
            b.base = 0
            b.excl = True
            self.psum.append(b)
        self.ps_rr = 0
        self.ps_open = set()
        self.vecF = sb("vecF", [128, NVC])
        self.cstf = sb("cstf", [128, 576])
        self.cstf_full = None
        self.cstb = sb("cstb", [128, NCST], BF16)
        self.normf = sb("normf", [128, D])
        self.bgt = sb("bgt", [4, 512], BF16)
        self.scT = sb("scT", [128, 8, 17], BF16)
        self.cTf = sb("cTf", [128, 8, 17])
        self.modT = sb("modT", [128, 32, 17])
        self.G1 = sb("G1", [128, 8, 17])
        self.G2 = sb("G2", [128, 8, 17])
        self.omm = sb("omm", [128, 14])
        self.epsc = sb("epsc", [128, 4])
        self.kk2 = sb("kk2", [128, 4])
        self.omka = sb("omka", [128, 4])
        self.lbv = sb("lbv", [128, 4])
        self.oml = sb("oml", [128, 4])
        self.wdec = sb("wdec", [128, 512], BF16)
        self.wgate = sb("wgate", [128, 512], BF16)
        self.ring = [sb("ring%d" % i, [128, 4096], BF16) for i in range(3)]
        self.tiles = []
        self.tile_issued = 0
        self.tile_got = 0
        S.dma("sp", self.vecF[:, :].ap, I["vecF"], writes=[self.vecF])
        S.dma("sp", self.cstf[:, :].ap, I["cst"][:, 0:576], writes=[self.cstf])
        S.dma("pool", self.cstb[:, :].ap, I["cst"], writes=[self.cstb])
        S.dma("sp", self.normf[:, :].ap, I["normf"], writes=[self.normf])
        S.dma("sp", self.cTf[:, :, :].ap, I["cT"], writes=[self.cTf])
        S.dma("pool", self.bgt[:, :].ap, I["bgt"].rearrange("o (g n) -> (o g) n", g=4), writes=[self.bgt])
        S.dma("pool", self.wdec[0:64, :].ap, I["w_dec"], writes=[self.wdec])
        S.dma("pool", self.wdec[64:128, :].ap, I["w_aaa"], writes=[self.wdec])
        S.dma("pool", self.wgate[:, :].ap, I["w_gate"], writes=[self.wgate])
        vf = self.vecF
        S.memset(self.epsc[:, 0:1], LNX_EPS)
        S.memset(self.epsc[:, 1:2], NORM_EPS)
        S.memset(self.epsc[:, 2:3], 1e-30)
        S.tt(self.kk2[:, :], vf[:, VC["k_k"]:VC["k_k"] + 4], vf[:, VC["k_k"]:VC["k_k"] + 4], ALU.mult)
        S.act(self.scT[:, :, :], self.cTf[:, :, :], AF.Silu)
        S.ts(self.omm[:, :], vf[:, VC["mu"]:VC["mu"] + 14], -1.0, ALU.mult, 1.0, ALU.add)
        S.ts(self.omka[:, :], vf[:, VC["k_a"]:VC["k_a"] + 4], -1.0, ALU.mult, 1.0, ALU.add)
        S.tt(self.oml[:, :], vf[:, VC["lb0"]:VC["lb0"] + 4], vf[:, VC["lb1"]:VC["lb1"] + 4], ALU.subtract)
        S.act(self.lbv[:, :], self.oml[:, :], AF.Sigmoid)
        S.ts(self.oml[:, :], self.lbv[:, :], -1.0, ALU.mult, 1.0, ALU.add)
        if self.lite:
            return
        fm_groups = [0, 1, 2, 3, 6, 7, 8, 9]
        for g in fm_groups:
            self.tiles.append(("cols", I["w_ada"], [g * 512 + j * 128 for j in range(4)]))
        for ph in range(2 if self.stage >= 2 else 1):
            for g in [4, 5, 10, 11]:
                self.tiles.append(("cols", I["w_ada"], [g * 512 + j * 128 for j in range(4)]))
            for blk in range(1 if ph == 0 else 4):
                for gi in range(0, 30, 4):
                    self.tiles.append(("cols", I["w_in"], [(gi + q_) * 128 for q_ in range(len(INPROJ_ORDER[gi:gi + 4]))]))
                for hf in range(2):
                    self.tiles.append(("cols", I["w_out"], [hf * 512 + j * 128 for j in range(4)]))
                for g in range(8):
                    self.tiles.append(("cols", I["w_up"], [g * 512 + j * 128 for j in range(4)]))
                    self.tiles.append(("rows", I["w_down"], g * 512))
        for gi, g in enumerate(fm_groups):
            wt = self.get_tile()
            ps = self.ps()
            for j in range(4):
                for k in range(8):
                    S.mm(ps.ap(0, 128, j * 32, [(1, 17)]), wt.ap(0, 128, k * 512 + j * 128, [(1, 128)]), self.scT[:, k, :],
                         start=(k == 0), stop=(k == 7), signal=(k == 7 and j == 3))
            for j in range(4):
                fc = g * 4 + j
                S.act(self.modT[:, gi * 4 + j, :], ps.ap(0, 128, j * 32, [(1, 17)]), AF.Identity,
                      bias=vf[:, VC["b_ada"] + fc:VC["b_ada"] + fc + 1], scale=1.0)
        for (G, nk, sc0) in [(self.G1, "norm1", 8), (self.G2, "norm2", 24)]:
            for k in range(8):
                S.ts(G[:, k, :], self.modT[:, sc0 + k, :], 1.0, ALU.add, vf[:, VC[nk] + k:VC[nk] + k + 1], ALU.mult)

    def issue_tile(self, idx):
        S = self.S
        kind, ap, arg = self.tiles[idx]
        wt = self.ring[idx % len(self.ring)]
        if kind == "cols":
            cols = arg
            j = 0
            while j < len(cols):
                n = 1
                while j + n < len(cols) and cols[j + n] == cols[j] + 128 * n:
                    n += 1
                src = ap[:, cols[j]:cols[j] + 128 * n].rearrange("(k p) c -> p k c", p=128)
                dst = wt.ap(0, 128, j * 128, [(512, 8), (1, 128 * n)])
                S.dma("pool", dst.ap, src, writes=[wt])
                j += n
        else:
            r0 = arg
            src = ap[r0:r0 + 512, :].rearrange("(c p) n -> p c n", p=128)
            dst = wt.ap(0, 128, 0, [(1024, 4), (1, 1024)])
            S.dma("pool", dst.ap, src, writes=[wt])

    def get_tile(self):
        i = self.tile_got
        self.tile_got += 1
        target = min(len(self.tiles), i + len(self.ring) - 1)
        while self.tile_issued < target:
            self.issue_tile(self.tile_issued)
            self.tile_issued += 1
        return self.ring[i % len(self.ring)]

    def run_phase(self, stack, sample):
        S, I, O = self.S, self.I, self.O
        vf = self.vecF
        sb = lambda n, s, d=F32: self.sb(stack, ("s_" if sample else "p_") + n, s, d)
        N = 64 if sample else 512
        C = 4 if sample else 64
        NCH = N // C
        HS = 64
        R = min(N, 128)
        NT = max(1, N // 128)
        nblk = 1 if sample else 4
        L = 1 if sample else 3
        idb = self.cstb
        idf = self.cstf

        def rows(fn):
            if not sample:
                fn(0, 128)
            else:
                fn(0, C)
                fn(HS, C)

        xbt = [sb("xb%d" % t_, [128, D]) for t_ in range(NT)]
        st8 = sb("st8", [128, 16])
        hT = sb("hT", [128, 8, N], BF16)
        mixT = Buf("mixT", hT.t)
        mixT.aliases = [hT]
        hT.aliases = [mixT]
        GT1 = sb("GT1", [128, D])
        GT2 = sb("GT2", [128, D])
        NH = N // 2
        NCHH = NCH // 2 if False else (N // 2) // (4 if sample else 64)
        TH = [[sb("T%d_%d" % (hf_, i), [128, N // 2]) for i in range(5)] for hf_ in range(2)]
        rkv = [sb("rkv%d" % i_, [128, 3, N]) for i_ in range(2)]
        wa = sb("wa", [128, N], BF16)
        sgd = sb("sgd", [128, N], BF16)
        notst = sb("notst", [128, N])
        self._tmp = [sb("tmpA", [128, 512]), sb("tmpB", [128, 512])]
        arena = stack.enter_context(self.nc.sbuf_tensor(("s_" if sample else "p_") + "arena", [128, max(28 * N, 2 * NT * D + 8 * N)], BF16))
        aoff = [0]

        def carve(name, nel_bf16, dt, pat=None, **kw):
            ap = arena[:, aoff[0]:aoff[0] + nel_bf16]
            aoff[0] += nel_bf16
            if dt == F32:
                ap = ap.bitcast(F32)
            if pat:
                ap = ap.rearrange(pat, **kw)
            return Buf(name, ap)

        AR = carve("AR", 8 * N, BF16, "p (a c t e) -> p a c t e", a=4, c=NCH, t=2)
        Bt = carve("Bt", 4 * N, BF16, "p (a n) -> p a n", a=4)
        Kt = carve("Kt", 4 * N, BF16, "p (a n) -> p a n", a=4)
        Vt = carve("Vt", 4 * N, BF16, "p (a n) -> p a n", a=4)
        gbf = carve("gbf", 4 * N, BF16, "p (a n) -> p a n", a=4)
        bonus = carve("bonus", 4 * N, BF16, "p (a n) -> p a n", a=4)
        aoff[0] = 0
        dacc = carve("dacc", 2 * NT * D, F32, "p (t d) -> p t d", t=NT)
        uT = [carve("uT%d" % i, 4 * N, BF16, "p (a n) -> p a n", a=4) for i in range(2)]
        mixer_bufs = [AR, Bt, Kt, Vt, gbf, bonus]
        mlp_bufs = [dacc] + uT
        for a in mixer_bufs:
            a.aliases = list(mlp_bufs)
        for a in mlp_bufs:
            a.aliases = list(mixer_bufs)
        gamR = sb("gamR", [128, 4, NCH])
        gamH = sb("gamH", [128, 4, NCH])
        sprev = sb("sprev", [128, 14, SB if sample else 1])
        arena2 = stack.enter_context(self.nc.sbuf_tensor(("s_" if sample else "p_") + "arena2", [128, max(NT * D, 8 * N)], BF16))
        xn = Buf("xn", arena2[:, 0:NT * D].rearrange("p (t d) -> p t d", t=NT))
        Qh = Buf("Qh", arena2[:, 0:4 * N].rearrange("p (a n) -> p a n", a=4))
        Kh = Buf("Kh", arena2[:, 4 * N:8 * N].rearrange("p (a n) -> p a n", a=4))
        screp = Buf("screp", arena2[:, 0:8 * R].rearrange("p (k r) -> p k r", k=8))
        xn.aliases = [Qh, Kh, screp]
        Qh.aliases = [xn, screp]
        Kh.aliases = [xn, screp]
        screp.aliases = [xn, Qh, Kh]
        Vh = sb("Vh", [128, 4, N], BF16)
        ogs = sb("ogs", [128, 4, N], BF16)
        qs = sb("qs", [128, N])
        NS = 3

        class TS:
            pass
        tsets = []
        for i_ in range(NS):
            t_ = TS()
            t_.AkRk = sb("AkRk%d" % i_, [128, 4, 2, C], BF16)
            t_.AbRb = sb("AbRb%d" % i_, [128, 4, 2, C], BF16)
            t_.Pm = [sb("Pm%d_%d" % (i_, q_), [128, 4, C], BF16) for q_ in range(2)]
            t_.PTm = [sb("PTm%d_%d" % (i_, q_), [128, 4, C], BF16) for q_ in range(2)]
            t_.Qm = sb("Qm%d" % i_, [128, 4, C], BF16)
            t_.TTm = [sb("TTm%d_%d" % (i_, q_), [128, 4, C], BF16) for q_ in range(2)]
            if not sample:
                hbn = [sb("HB%d_%d" % (i_, q_), [128, 4, 64], BF16) for q_ in range(3)]
                t_.HB = hbn + [t_.Pm[0], t_.Pm[1], t_.PTm[0], t_.PTm[1], t_.Qm]
            else:
                t_.HB = None
            t_.Btok = sb("Btok%d" % i_, [128, 4, 64], BF16)
            t_.Ktok = sb("Ktok%d" % i_, [128, 4, 64], BF16)
            t_.Vtok = sb("Vtok%d" % i_, [128, 4, 64], BF16)
            tsets.append(t_)
        OTc = sb("OTc", [128, 4, C])
        PT1 = sb("PT1", [128, 4, C])
        PT2 = sb("PT2", [128, 4, C])
        HO = sb("HO", [128, 4, C])
        HT1 = sb("HT1", [128, 4, C])
        if sample:
            psets = [(OTc, PT1, PT2), (sb("OTc2", [128, 4, C]), sb("PT12", [128, 4, C]), sb("PT22", [128, 4, C]))]
        else:
            def alias_view(name, base_buf, ap):
                b_ = Buf(name, ap)
                b_.aliases = [base_buf]
                base_buf.aliases = base_buf.aliases + [b_]
                return b_
            q3 = lambda lo: qs.t[:, lo:lo + 4 * C].rearrange("p (a b) -> p a b", a=4)
            w3 = wa.t[:, :].bitcast(F32).rearrange("p (a b) -> p a b", a=4)
            psets = [(OTc, PT1, PT2), (alias_view("OTc2", qs, q3(0)), alias_view("PT12", qs, q3(4 * C)), alias_view("PT22", wa, w3))]
        Wsb = sb("Wsb", [128, 4, 64], BF16)
        Usb = sb("Usb", [128, 4, 64], BF16)
        tH = sb("tH", [128, 4, 64])
        hKtok = sb("hKtok", [128, 4, 128], BF16)
        hVtok = sb("hVtok", [128, 4, 128], BF16)
        hAT = sb("hAT", [128, 4, C], BF16)
        tS = sb("tS", [128, 4, 128])
        if sample:
            Hst = sb("Hst", [128, 4, SB, 64])
            Hsb = sb("Hsb", [128, 4, SB, 64], BF16)
            Sld = sb("Sld", [128, 4, SB, 64])
            Shs = sb("Shs", [128, 4, SB, 128])
            Shb = sb("Shb", [128, 4, SB, 128], BF16)
        else:
            Hst = sb("Hst", [128, 4, 1, 64])
            Hsb = sb("Hsb", [128, 4, 1, 64], BF16)
            Sld = sb("Sld", [128, 4, 1, 64])
            Shs = sb("Shs", [128, 4, 1, 128])
            Shb = sb("Shb", [128, 4, 1, 128], BF16)

        S.memset(notst[:, :], 1.0)
        S.memset(notst.ap(0, 128, 0, [(C, NCH), (1, 1)]), 0.0)
        if sample:
            if self.stage < 0.12:
                return
            S.dma("sp", sprev[:, :, :].ap, I["sst"], writes=[sprev])
            if self.stage < 0.13:
                return
            for hp in range(4):
                S.dma("sp", Sld[:, hp, :, :].ap,
                      I["swkv"][:, 2 * hp:2 * hp + 2, :, :].rearrange("b h v k -> (h v) b k"), writes=[Sld])
            if self.stage < 0.14:
                return
            for h in range(4):
                S.dma("sp", Shs[:, h, :, :].ap, I["shg"][:, h, :, :].rearrange("b f i -> f b i"), writes=[Shs])
            if self.stage < 0.15:
                return
            S.copy(Shb[:, :, :, :], Shs[:, :, :, :], eng="pool")
            if self.stage < 0.16:
                return
            for hp in range(4):
                for b0 in range(0, SB, 8):
                    ps = self.ps()
                    for b in range(b0, b0 + 8):
                        for h in range(2):
                            S.mm(ps.ap(64 * h, 64, (b - b0) * 64, [(1, 64)]), Sld[64 * h:64 * h + 64, hp, b, :],
                                 idf[64 * h:64 * h + 64, C_IDS:C_IDS + 64], signal=(b == b0 + 7 and h == 1))
                    S.copy(Hst[:, hp, b0:b0 + 8, :], ps.ap(0, 128, 0, [(64, 8), (1, 64)]), eng="dve")
                    S.copy(Hsb[:, hp, b0:b0 + 8, :], ps.ap(0, 128, 0, [(64, 8), (1, 64)]), eng="act")
        else:
            S.memset(sprev[:, :, :], 0.0)
            S.memset(Hst[:, :, :, :], 0.0)
            S.memset(Hsb[:, :, :, :], 0.0)
            S.memset(Shs[:, :, :, :], 0.0)
            S.memset(Shb[:, :, :, :], 0.0)

        if self.stage < 0.2:
            return
        if sample:
            S.copy(screp.ap(0, 128, 0, [(R, 8), (4, 16), (1, 4)]), self.scT.ap(0, 128, 1, [(17, 8), (1, 16), (0, 4)]))
        else:
            S.copy(screp[:, :, :], self.scT.ap(0, 128, 0, [(17, 8), (0, 128)]))
        for gi, (GT, half) in enumerate([(GT1, 0), (GT1, 1), (GT2, 0), (GT2, 1)]):
            wt = self.get_tile()
            ps = self.ps()
            for k in range(8):
                S.mm(ps.full(R, 512), screp[:, k, :], wt.ap(0, 128, k * 512, [(1, 512)]), start=(k == 0), stop=False, signal=False)
            S.mm(ps.full(R, 512), idb.ap(0, 4, C_ID + gi, [(0, R)]), self.bgt[0:4, :], start=False, stop=True)
            S.copy(GT[0:R, half * 512:(half + 1) * 512], ps.full(R, 512), eng="act")

        def rmsnorm_T(G, sh0, dst):
            for t in range(NT):
                S.act(xn[0:R, t, :], xbt[t][0:R, :], AF.Square, accum=st8[0:R, t:t + 1])
            S.ts(st8[0:R, 4:4 + NT], st8[0:R, 0:NT], 1.0 / D, ALU.mult, NORM_EPS, ALU.add)
            S.act(st8[0:R, 4:4 + NT], st8[0:R, 4:4 + NT], AF.Sqrt)
            S.recip(st8[0:R, 8:8 + NT], st8[0:R, 4:4 + NT])
            for t in range(NT):
                S.ts(xn[0:R, t, :], xbt[t][0:R, :], st8[0:R, 8 + t:9 + t], ALU.mult)
            for k in range(8):
                ps = self.ps()
                for t in range(NT):
                    S.mm(ps.ap(0, 128, t * 128, [(1, R)]), xn[0:R, t, k * 128:(k + 1) * 128], idb[0:R, C_ID:C_ID + R],
                         signal=(t == NT - 1))
                if not sample:
                    S.act(dst[:, k, :], ps.full(128, N), AF.Identity, scale=G[:, k, 0:1], bias=self.modT[:, sh0 + k, 0:1])
                else:
                    S.tt(self._tmp[0].ap(0, 128, 0, [(4, 16), (1, 4)]), ps.ap(0, 128, 0, [(4, 16), (1, 4)]),
                         G.ap(0, 128, k * 17 + 1, [(1, 16), (0, 4)]), ALU.mult)
                    S.tt(dst.ap(0, 128, k * N, [(4, 16), (1, 4)]), self._tmp[0].ap(0, 128, 0, [(4, 16), (1, 4)]),
                         self.modT.ap(0, 128, (sh0 + k) * 17 + 1, [(1, 16), (0, 4)]), ALU.add)

        def tv(buf, off=0):
            return buf.ap(0, 128, off, [(C, NCH), (1, C)])

        def token_shift(c, ps, dst, dst_off):
            mu = vf[:, VC["mu"] + c:VC["mu"] + c + 1]
            p1 = dst.ap(0, 128, dst_off, [(1, N)])
            S.act(p1, ps.full(128, N), AF.Identity, scale=self.omm[:, c:c + 1])
            if not sample:
                S.stt(dst.ap(0, 128, dst_off, [(1, 1)]), sprev[:, c, 0:1], mu, dst.ap(0, 128, dst_off, [(1, 1)]), ALU.mult, ALU.add)
                S.stt(dst.ap(0, 128, dst_off + 1, [(1, N - 1)]), ps.ap(0, 128, 0, [(1, N - 1)]), mu,
                      dst.ap(0, 128, dst_off + 1, [(1, N - 1)]), ALU.mult, ALU.add)
                S.copy(sprev[:, c, 0:1], ps.ap(0, 128, N - 1, [(1, 1)]), eng="act")
            else:
                S.stt(dst.ap(0, 128, dst_off, [(4, 16), (1, 1)]), sprev.ap(0, 128, c * SB, [(1, 16), (1, 1)]), mu,
                      dst.ap(0, 128, dst_off, [(4, 16), (1, 1)]), ALU.mult, ALU.add)
                S.stt(dst.ap(0, 128, dst_off + 1, [(4, 16), (1, 3)]), ps.ap(0, 128, 0, [(4, 16), (1, 3)]), mu,
                      dst.ap(0, 128, dst_off + 1, [(4, 16), (1, 3)]), ALU.mult, ALU.add)
                S.copy(sprev.ap(0, 128, c * SB, [(1, 16), (1, 1)]), ps.ap(0, 128, 3, [(4, 16), (1, 1)]), eng="act")

        if sample:
            TG = [[sb("TG%d_%d" % (hf_, i), [128, N // 2]) for i in range(3)] for hf_ in range(2)]
        else:
            mkv = lambda b_: Buf(b_.name + "_v", b_.t[:, :, :].rearrange("p a b -> p (a b)"))
            TG = [[mkv(PT1), mkv(PT2), mkv(OTc)], [mkv(HO), mkv(HT1), mkv(tH)]]
            for (v_, o_) in zip(TG[0] + TG[1], [PT1, PT2, OTc, HO, HT1, tH]):
                v_.aliases = [o_]
                o_.aliases = [v_]

        def run_rr(gens):
            gens = list(gens)
            while gens:
                for g_ in list(gens):
                    try:
                        next(g_)
                    except StopIteration:
                        gens.remove(g_)

        def hv(buf, off, hf):
            return buf.ap(0, 128, off + hf * NH, [(1, NH)])

        def hvc(buf, off, hf):
            return buf.ap(0, 128, off + hf * NH, [(C, NCHH), (1, C)])

        def rwkv_prep(j, rk, hf):
            T = TH[hf]
            r = hv(rk, 0, hf)
            k = hv(rk, N, hf)
            v = hv(rk, 2 * N, hf)
            jc = slice(j * 128, (j + 1) * 128)
            col = lambda n: vf[:, VC[n] + j:VC[n] + j + 1]
            tvh = lambda b_: b_.ap(0, 128, 0, [(C, NCHH), (1, C)])
            pfull = lambda p_: p_.ap(0, 128, 0, [(1, NH)])
            ps1 = self.ps()
            S.mm(pfull(ps1), self.wdec[0:64, jc], hv(wa, 0, hf)[0:64])
            S.act(T[0][:, :], pfull(ps1), AF.Sigmoid, bias=col("w0"), scale=1.0)
            yield
            ps2 = self.ps()
            S.mm(pfull(ps2), self.wdec[64:128, jc], hv(wa, 0, hf)[64:128])
            S.act(T[1][:, :], pfull(ps2), AF.Sigmoid, bias=col("a0"), scale=1.0)
            yield
            ps3 = self.ps()
            S.mm(pfull(ps3), self.wgate[:, jc], hv(sgd, 0, hf))
            S.copy(hv(gbf, j * N, hf), pfull(ps3), eng="act")
            yield
            S.scan(T[2][:, :], hv(notst, 0, hf), T[0][:, :], 0.0, ALU.mult, ALU.add)
            yield
            S.tt(T[0][:, :], T[2][:, :], T[0][:, :], ALU.subtract)
            S.act(T[3][:, :], T[2][:, :], AF.Exp, scale=-WDEC)
            yield
            S.act(T[4][:, :], T[2][:, :], AF.Exp, scale=WDEC)
            S.act(T[0][:, :], T[0][:, :], AF.Exp, scale=-WDEC)
            yield
            S.copy(gamR.ap(0, 128, j * NCH + hf * NCHH, [(1, NCHH)]), T[3].ap(0, 128, C - 1, [(C, NCHH)]), eng="pool")
            S.stt(T[2][:, :], k, self.kk2[:, j:j + 1], k, ALU.mult, ALU.mult)
            yield
            ps4 = self.ps()
            S.mm(pfull(ps4), idf[:, C_BONE:C_BONE + 128], T[2][:, :])
            S.act(T[2][:, :], pfull(ps4), AF.Ln, bias=self.epsc[:, 2:3], scale=1.0)
            yield
            S.act(T[2][:, :], T[2][:, :], AF.Exp, scale=-0.5)
            yield
            S.stt(T[2][:, :], k, col("k_k"), T[2][:, :], ALU.mult, ALU.mult)
            yield
            S.stt(AR.ap(0, 128, j * 2 * N + hf * NCHH * 2 * C, [(2 * C, NCHH), (1, C)]), tvh(T[2]), -1.0, tvh(T[0]), ALU.mult, ALU.mult)
            S.tt(T[0][:, :], T[2][:, :], T[1][:, :], ALU.mult, eng="pool")
            yield
            S.tt(hv(Bt, j * N, hf), T[0][:, :], T[4][:, :], ALU.mult, eng="pool")
            S.ts(T[2][:, :], T[1][:, :], col("k_a"), ALU.mult, self.omka[:, j:j + 1], ALU.add)
            yield
            S.tt(T[2][:, :], k, T[2][:, :], ALU.mult)
            yield
            S.tt(hv(Kt, j * N, hf), T[2][:, :], T[4][:, :], ALU.mult, eng="pool")
            S.tt(AR.ap(0, 128, j * 2 * N + hf * NCHH * 2 * C + C, [(2 * C, NCHH), (1, C)]), hvc(rk, 0, hf), tvh(T[3]), ALU.mult)
            S.copy(hv(Vt, j * N, hf), v, eng="act")
            yield
            S.stt(T[1][:, :], r, col("r_k"), T[2][:, :], ALU.mult, ALU.mult)
            yield
            ps5 = self.ps()
            S.mm(pfull(ps5), idf[:, C_BONE:C_BONE + 128], T[1][:, :])
            S.tt(hv(bonus, j * N, hf), pfull(ps5), v, ALU.mult)

        def hgrn_prep(h, ps, hf):
            T = TG[hf]
            pfh = ps.ap(0, 128, hf * NH, [(1, NH)])
            S.act(T[0][:, :], pfh, AF.Sigmoid)
            yield
            S.ts(T[1][:, :], T[0][:, :], self.oml[:, h:h + 1], ALU.mult, self.lbv[:, h:h + 1], ALU.add)
            yield
            S.ts(T[0][:, :], T[1][:, :], -1.0, ALU.mult, 1.0, ALU.add, eng="pool")
            S.act(T[1][:, :], T[1][:, :], AF.Ln)
            yield
            S.scan(T[2][:, :], hv(notst, 0, hf), T[1][:, :], 0.0, ALU.mult, ALU.add)
            yield
            S.act(T[1][:, :], T[2][:, :], AF.Exp)
            S.act(T[2][:, :], T[2][:, :], AF.Exp, scale=-1.0)
            yield
            S.tt(hv(Qh, h * N, hf), hv(qs, 0, hf), T[1][:, :], ALU.mult)
            S.tt(hv(Kh, h * N, hf), T[0][:, :], T[2][:, :], ALU.mult, eng="pool")
            S.copy(gamH.ap(0, 128, h * NCH + hf * NCHH, [(1, NCHH)]), T[1].ap(0, 128, C - 1, [(C, NCHH)]), eng="pool")

        active_rw = []
        active_hg = []

        def post_inproj(c, ps):
            if c < 12:
                j = c % 4
                token_shift(c, ps, rkv[j % 2], (c // 4) * N)
                if c // 4 == 2:
                    active_rw.extend([rwkv_prep(j, rkv[j % 2], 0), rwkv_prep(j, rkv[j % 2], 1)])
            elif c == 12:
                token_shift(c, ps, self._tmp[0], 0)
                S.act(wa[0:64, :], self._tmp[0][0:64, 0:N], AF.Tanh)
                S.copy(wa[64:128, :], self._tmp[0][64:128, 0:N], eng="act")
            elif c == 13:
                token_shift(c, ps, self._tmp[1], 0)
                S.act(sgd[:, :], self._tmp[1][:, 0:N], AF.Sigmoid)
            elif c < 18:
                S.act(qs[:, :], ps.full(128, N), AF.Silu)
            elif c < 22:
                gs_ = [hgrn_prep(c - 18, ps, 0), hgrn_prep(c - 18, ps, 1)]
                for g_ in gs_:
                    next(g_)
                active_hg.extend(gs_)
            elif c < 26:
                S.copy(Vh[:, c - 22, :], ps.full(128, N), eng="act")
            else:
                S.act(ogs[:, c - 26, :], ps.full(128, N), AF.Silu)

        m1c = idb.ap(0, 128, C_M1S if sample else C_M1, [(0, 4), (1, 2 * C)])
        msc = idb.ap(0, 128, C_MSS if sample else C_BD, [(0, 4), (1, C)])
        m2c = m1c if sample else idb.ap(0, 128, C_M2H, [(0, 4), (1, 2 * C)])
        idcc = idb.ap(0, 128, C_IDS, [(0, 4), (1, C)]) if not sample else None

        def idview(r0, nr):
            if sample:
                return idb.ap(r0, nr, C_ID + r0, [(0, 4), (1, C)])
            return View(idcc.buf, idcc.ap[r0:r0 + nr])

        def getbanks(n):
            while 8 - len(self.ps_open) < n:
                yield
            return [self.ps(hold=True) for _ in range(n)]

        post_gens = {}

        def hmm(out_ps, lhs, rhs, w=C, kb=None):
            for hp in range(4):
                for h in range(2):
                    ob = HS * h
                    S.mm(out_ps.ap(ob, C, hp * w, [(1, w)]), lhs(ob, hp), rhs(ob, hp), signal=(hp == 3 and h == 1))

        def rwkv_inv_gen(c, ts):
            cs = slice(c * C, (c + 1) * C)
            AkRk, AbRb, Pm, PTm, Qm, TTm = ts.AkRk, ts.AbRb, ts.Pm, ts.PTm, ts.Qm, ts.TTm
            (psA,) = yield from getbanks(1)
            for hp in range(4):
                for h in range(2):
                    rb, ob = 64 * h, HS * h
                    arv = AR.ap(rb, 64, (hp * NCH + c) * 2 * C, [(1, 2 * C)])
                    S.mm(psA.ap(ob, C, hp * 2 * C, [(1, 2 * C)]), Kt[rb:rb + 64, hp, cs], arv, signal=(hp == 3 and h == 1))
            yield
            rows(lambda r0, nr: S.tt(AkRk.ap(r0, nr, 0, [(2 * C, 4), (1, 2 * C)]), psA.ap(r0, nr, 0, [(2 * C, 4), (1, 2 * C)]),
                                     View(m1c.buf, m1c.ap[r0:r0 + nr]), ALU.mult))
            psA.done()
            psB, psC = yield from getbanks(2)
            for hp in range(4):
                for h in range(2):
                    rb, ob = 64 * h, HS * h
                    last = (hp == 3 and h == 1)
                    arv = AR.ap(rb, 64, (hp * NCH + c) * 2 * C, [(1, 2 * C)])
                    S.mm(psB.ap(ob, C, hp * 2 * C, [(1, 2 * C)]), Bt[rb:rb + 64, hp, cs], arv, signal=last)
                    S.mm(psC.ap(ob, C, hp * C, [(1, C)]), AR.ap(rb, 64, (hp * NCH + c) * 2 * C, [(1, C)]),
                         Bt[rb:rb + 64, hp, cs], signal=last)
            yield

            def ev(r0, nr):
                S.tt(AbRb.ap(r0, nr, 0, [(2 * C, 4), (1, 2 * C)]), psB.ap(r0, nr, 0, [(2 * C, 4), (1, 2 * C)]),
                     View(m2c.buf, m2c.ap[r0:r0 + nr]), ALU.mult)
                S.tt(Pm[0].ap(r0, nr, 0, [(C, 4), (1, C)]), psC.ap(r0, nr, 0, [(C, 4), (1, C)]),
                     View(msc.buf, msc.ap[r0:r0 + nr]), ALU.mult)
                S.tt(TTm[0].ap(r0, nr, 0, [(C, 4), (1, C)]), AbRb.ap(r0, nr, 0, [(2 * C, 4), (1, C)]), idview(r0, nr), ALU.add, eng="pool")
            rows(ev)
            if not sample:
                mk = lambda off: idb.ap(0, 128, off, [(0, 4), (1, C)])
                HB = ts.HB
                S.tt(HB[0][:, :, :], psC.ap(0, 128, 0, [(C, 4), (1, C)]), mk(C_N0), ALU.mult)
                S.tt(HB[1][:, :, :], psB.ap(0, 128, 0, [(2 * C, 4), (1, C)]), mk(C_N0T), ALU.mult)
                S.tt(HB[2][:, :, :], psC.ap(0, 128, 0, [(C, 4), (1, C)]), mk(C_N1), ALU.mult)
            psB.done()
            psC.done()
            yield
            Pc = Pm[0]
            PTc = lambda ob, hp: AbRb.ap(ob, C, hp * 2 * C, [(1, C)])
            TTc = TTm[0]
            for l in range(1, L + 1):
                if l < L:
                    psP, psPT = yield from getbanks(2)
                else:
                    (psP,) = yield from getbanks(1)
                    psPT = None
                hmm(psP, PTc, lambda ob, hp, Pc=Pc: Pc.ap(ob, C, hp * C, [(1, C)]))
                if l < L:
                    hmm(psPT, lambda ob, hp, Pc=Pc: Pc.ap(ob, C, hp * C, [(1, C)]), PTc)
                yield
                Pn, PTn = Pm[l % 2], PTm[l % 2]

                def ev2(r0, nr):
                    S.tt(Qm.ap(r0, nr, 0, [(C, 4), (1, C)]), psP.ap(r0, nr, 0, [(C, 4), (1, C)]), idview(r0, nr), ALU.add)
                    if l < L:
                        S.copy(Pn.ap(r0, nr, 0, [(C, 4), (1, C)]), psP.ap(r0, nr, 0, [(C, 4), (1, C)]), eng="act")
                        S.copy(PTn.ap(r0, nr, 0, [(C, 4), (1, C)]), psPT.ap(r0, nr, 0, [(C, 4), (1, C)]), eng="dve")
                rows(ev2)
                psP.done()
                if psPT is not None:
                    psPT.done()
                (psT,) = yield from getbanks(1)
                hmm(psT, lambda ob, hp: Qm.ap(ob, C, hp * C, [(1, C)]), lambda ob, hp, TTc=TTc: TTc.ap(ob, C, hp * C, [(1, C)]))
                yield
                TTn = TTm[l % 2]
                rows(lambda r0, nr: S.copy(TTn.ap(r0, nr, 0, [(C, 4), (1, C)]), psT.ap(r0, nr, 0, [(C, 4), (1, C)]), eng="act"))
                psT.done()
                Pc = Pn
                PTc = (lambda PTn: (lambda ob, hp: PTn.ap(ob, C, hp * C, [(1, C)])))(PTn)
                TTc = TTn
            pss = []
            pbk, pv_ = yield from getbanks(2)
            for (src, dstb, pst_, co) in [(Bt, ts.Btok, pbk, 0), (Kt, ts.Ktok, pbk, 256), (Vt, ts.Vtok, pv_, 0)]:
                for hp in range(4):
                    for h in range(2):
                        rb, ob = 64 * h, HS * h
                        S.mm(pst_.ap(ob, C, co + hp * 64, [(1, 64)]), src[rb:rb + 64, hp, cs], idb[rb:rb + 64, C_IDS:C_IDS + 64],
                             signal=(hp == 3 and h == 1))
                pss.append((pst_, dstb, co))
            yield
            for pst_, dstb, co in pss:
                rows(lambda r0, nr: S.copy(dstb.ap(r0, nr, 0, [(64, 4), (1, 64)]), pst_.ap(r0, nr, co, [(64, 4), (1, 64)]), eng="act"))
            pbk.done()
            pv_.done()
            if not sample:
                HB = ts.HB
                full = lambda b_: b_.ap(0, 128, 0, [(C, 4), (1, C)])
                pfull = lambda p_: p_.ap(0, 128, 0, [(C, 4), (1, C)])
                bl = lambda b_: (lambda ob, hp: b_.ap(ob, C, hp * C, [(1, C)]))
                N0, N0T, N1, D0, X, D1, X2, D1T = HB
                D0T = TTc
                (p_,) = yield from getbanks(1)
                for hp in range(4):
                    for h in range(2):
                        rb = 64 * h
                        S.mm(p_.ap(rb, 64, hp * C, [(1, C)]), D0T.ap(rb, 64, hp * C, [(1, C)]), idb[rb:rb + 64, C_IDS:C_IDS + 64],
                             signal=(hp == 3 and h == 1))
                yield
                S.copy(full(D0), pfull(p_), eng="act")
                p_.done()
                p_, p2_ = yield from getbanks(2)
                hmm(p_, bl(N0T), bl(D0))
                hmm(p2_, bl(N0), bl(D0T))
                yield
                S.copy(full(X), pfull(p_), eng="act")
                S.copy(full(X2), pfull(p2_), eng="act")
                p_.done()
                p2_.done()
                p_, p2_ = yield from getbanks(2)
                hmm(p_, bl(D0T), bl(X))
                hmm(p2_, bl(D0), bl(X2))
                yield
                S.tt(full(D1), pfull(p_), full(D0), ALU.add)
                S.tt(full(D1T), pfull(p2_), full(D0T), ALU.add)
                p_.done()
                p2_.done()
                (p_,) = yield from getbanks(1)
                hmm(p_, bl(N1), bl(D1T))
                yield
                S.copy(full(X), pfull(p_), eng="act")
                p_.done()
                (p_,) = yield from getbanks(1)
                hmm(p_, bl(D1), bl(X))
                yield
                TTf = TTm[0] if TTc is TTm[1] else TTm[1]
                S.tt(full(TTf), pfull(p_), full(D1T), ALU.add)
                p_.done()
                TTc = TTf
            ts.TTfin = TTc

        def rwkv_chain_gen(c, ts):
            sidx = c if sample else 0
            cs = slice(c * C, (c + 1) * C)
            AkRk, AbRb, Btok, Ktok, Vtok, TTc = ts.AkRk, ts.AbRb, ts.Btok, ts.Ktok, ts.Vtok, ts.TTfin
            (psW,) = yield from getbanks(1)
            for hp in range(4):
                for h in range(2):
                    rb, ob = 64 * h, HS * h
                    S.mm(psW.ap(ob, C, hp * 64, [(1, 64)]), AR.ap(rb, 64, (hp * NCH + c) * 2 * C, [(1, C)]),
                         Hsb[rb:rb + 64, hp, sidx, :], start=True, stop=False, signal=False)
                    S.mm(psW.ap(ob, C, hp * 64, [(1, 64)]), AkRk.ap(ob, C, hp * 2 * C, [(1, C)]),
                         Vtok.ap(ob, C, hp * 64, [(1, 64)]), start=False, stop=True, signal=(hp == 3 and h == 1))
            yield
            rows(lambda r0, nr: S.copy(Wsb.ap(r0, nr, 0, [(64, 4), (1, 64)]), psW.ap(r0, nr, 0, [(64, 4), (1, 64)]), eng="act"))
            psW.done()
            (psU,) = yield from getbanks(1)
            hmm(psU, lambda ob, hp: TTc.ap(ob, C, hp * C, [(1, C)]), lambda ob, hp: Wsb.ap(ob, C, hp * 64, [(1, 64)]), w=64)
            yield
            rows(lambda r0, nr: S.copy(Usb.ap(r0, nr, 0, [(64, 4), (1, 64)]), psU.ap(r0, nr, 0, [(64, 4), (1, 64)]), eng="act"))
            psU.done()
            psO, psH = yield from getbanks(2)
            for hp in range(4):
                for h in range(2):
                    rb, ob = 64 * h, HS * h
                    last = (hp == 3 and h == 1)
                    S.mm(psO.ap(rb, 64, hp * C, [(1, C)]), Hsb[rb:rb + 64, hp, sidx, :],
                         AR.ap(rb, 64, (hp * NCH + c) * 2 * C + C, [(1, C)]), start=True, stop=False, signal=False)
                    S.mm(psO.ap(rb, 64, hp * C, [(1, C)]), Usb.ap(ob, C, hp * 64, [(1, 64)]),
                         AbRb.ap(ob, C, hp * 2 * C + C, [(1, C)]), start=False, stop=False, signal=False)
                    S.mm(psO.ap(rb, 64, hp * C, [(1, C)]), Vtok.ap(ob, C, hp * 64, [(1, 64)]),
                         AkRk.ap(ob, C, hp * 2 * C + C, [(1, C)]), start=False, stop=True, signal=last)
                    S.mm(psH.ap(rb, 64, hp * 64, [(1, 64)]), Btok.ap(ob, C, hp * 64, [(1, 64)]),
                         Usb.ap(ob, C, hp * 64, [(1, 64)]), start=True, stop=False, signal=False)
                    S.mm(psH.ap(rb, 64, hp * 64, [(1, 64)]), Ktok.ap(ob, C, hp * 64, [(1, 64)]),
                         Vtok.ap(ob, C, hp * 64, [(1, 64)]), start=False, stop=True, signal=last)
            yield
            hview = Hst.ap(0, 128, sidx * 64, [(Hst.pstride // 4, 4), (1, 64)])
            hbview = Hsb.ap(0, 128, sidx * 64, [(Hsb.pstride // 4, 4), (1, 64)])
            S.tt(tH[:, :, :], psH.ap(0, 128, 0, [(64, 4), (1, 64)]), hview, ALU.add)
            gv = gamR.ap(0, 128, c, [(NCH, 4), (0, 64)])
            S.tt(hbview, tH[:, :, :], gv, ALU.mult, eng="pool")
            S.tt(hview, tH[:, :, :], gv, ALU.mult, eng="pool")
            OTc, PT1, PT2 = psets[c % 2]
            S.copy(OTc[:, :, :], psO.ap(0, 128, 0, [(C, 4), (1, C)]), eng="act")
            psO.done()
            psH.done()
            post_gens[c] = rwkv_post_gen(c)

        def rwkv_post_gen(c):
            OTc, PT1, PT2 = psets[c % 2]
            flat = lambda b_: b_.ap(0, 128, 0, [(1, 4 * C)])
            (ps1,) = yield from getbanks(1)
            S.mm(ps1.ap(0, 128, 0, [(1, 4 * C)]), idf[:, C_BMEAN:C_BMEAN + 128], flat(OTc))
            yield
            S.tt(flat(PT1), flat(OTc), ps1.ap(0, 128, 0, [(1, 4 * C)]), ALU.subtract)
            ps1.done()
            S.tt(flat(PT2), flat(PT1), flat(PT1), ALU.mult, eng="pool")
            (ps2,) = yield from getbanks(1)
            S.mm(ps2.ap(0, 128, 0, [(1, 4 * C)]), idf[:, C_BMEAN:C_BMEAN + 128], flat(PT2))
            yield
            S.act(flat(PT2), ps2.ap(0, 128, 0, [(1, 4 * C)]), AF.Ln, bias=self.epsc[:, 0:1], scale=1.0)
            ps2.done()
            S.act(flat(PT2), flat(PT2), AF.Exp, scale=-0.5)
            S.tt(flat(PT1), flat(PT1), flat(PT2), ALU.mult, eng="pool")
            lw = vf.ap(0, 128, VC["lnx_w"], [(1, 4), (0, C)])
            lb_ = vf.ap(0, 128, VC["lnx_b"], [(1, 4), (0, C)])
            S.tt(PT1[:, :, :], PT1[:, :, :], lw, ALU.mult, eng="pool")
            S.tt(PT1[:, :, :], PT1[:, :, :], lb_, ALU.add, eng="pool")
            S.tt(PT1[:, :, :], PT1[:, :, :], bonus.ap(0, 128, c * C, [(N, 4), (1, C)]), ALU.add, eng="pool")
            S.tt(mixT.ap(0, 128, c * C, [(N, 4), (1, C)]), PT1[:, :, :], gbf.ap(0, 128, c * C, [(N, 4), (1, C)]), ALU.mult, eng="pool")

        def hgrn_gen():
            mi = idb.ap(0, C, (C_M1S + 4) if sample else (C_M1 + 64), [(0, 4), (1, C)])
            for c in range(NCH):
                sidx = c if sample else 0
                cs = slice(c * C, (c + 1) * C)
                psK, psV = yield from getbanks(2)
                for h in range(4):
                    S.mm(psK.ap(0, C, h * 128, [(1, 128)]), Kh[:, h, cs], idb[:, C_ID:C_ID + 128], signal=(h == 3))
                    S.mm(psV.ap(0, C, h * 128, [(1, 128)]), Vh[:, h, cs], idb[:, C_ID:C_ID + 128], signal=(h == 3))
                yield
                S.copy(hKtok[0:C, :, :], psK.ap(0, C, 0, [(128, 4), (1, 128)]), eng="act")
                S.copy(hVtok[0:C, :, :], psV.ap(0, C, 0, [(128, 4), (1, 128)]), eng="act")
                psK.done()
                psV.done()
                psA, psG = yield from getbanks(2)
                for h in range(4):
                    S.mm(psA.ap(0, C, h * C, [(1, C)]), Kh[:, h, cs], Qh[:, h, cs], signal=(h == 3))
                for h in range(4):
                    S.mm(psG.ap(0, 128, h * 128, [(1, 128)]), hKtok[0:C, h, :], hVtok[0:C, h, :], signal=(h == 3))
                yield
                S.tt(hAT[0:C, :, :], psA.ap(0, C, 0, [(C, 4), (1, C)]), mi, ALU.mult)
                psA.done()
                sview = Shs.ap(0, 128, sidx * 128, [(Shs.pstride // 4, 4), (1, 128)])
                sbview = Shb.ap(0, 128, sidx * 128, [(Shb.pstride // 4, 4), (1, 128)])
                S.tt(tS[:, :, :], psG.ap(0, 128, 0, [(128, 4), (1, 128)]), sview, ALU.add)
                psG.done()
                (psO,) = yield from getbanks(1)
                for h in range(4):
                    S.mm(psO.ap(0, 128, h * C, [(1, C)]), Shb[:, h, sidx, :], Qh[:, h, cs], start=(h == 0), stop=False, signal=False)
                for h in range(4):
                    S.mm(psO.ap(0, 128, h * C, [(1, C)]), hVtok[0:C, h, :], hAT[0:C, h, :], start=False, stop=True, signal=(h == 3))
                gv = gamH.ap(0, 128, c, [(NCH, 4), (0, 128)])
                S.tt(sbview, tS[:, :, :], gv, ALU.mult, eng="pool")
                S.tt(sview, tS[:, :, :], gv, ALU.mult, eng="pool")
                yield
                flat = lambda b_: b_.ap(0, 128, 0, [(1, 4 * C)])
                S.copy(HO[:, :, :], psO.ap(0, 128, 0, [(C, 4), (1, C)]), eng="act")
                psO.done()
                S.tt(flat(HT1), flat(HO), flat(HO), ALU.mult, eng="pool")
                (ps1,) = yield from getbanks(1)
                S.mm(ps1.ap(0, 128, 0, [(1, 4 * C)]), idf[:, C_AMEAN:C_AMEAN + 128], flat(HT1))
                yield
                S.act(flat(HT1), ps1.ap(0, 128, 0, [(1, 4 * C)]), AF.Ln, bias=self.epsc[:, 1:2], scale=1.0)
                ps1.done()
                S.act(flat(HT1), flat(HT1), AF.Exp, scale=-0.5)
                S.tt(flat(HT1), flat(HT1), flat(HO), ALU.mult, eng="pool")
                S.stt(mixT.ap(0, 128, 4 * N + c * C, [(N, 4), (1, C)]), HT1[:, :, :], vf[:, VC["gnorm"]:VC["gnorm"] + 1],
                      ogs.ap(0, 128, c * C, [(N, 4), (1, C)]), ALU.mult, ALU.mult)

        def run_units():
            free = list(range(NS))
            inv = {}
            inv_done = {}
            chain = None
            next_inv = 0
            next_chain = 0
            hg = hgrn_gen()
            hg_alive = True
            while next_chain < NCH or chain is not None or hg_alive or post_gens:
                while free and next_inv < NCH:
                    si = free.pop(0)
                    inv[next_inv] = (rwkv_inv_gen(next_inv, tsets[si]), si)
                    next_inv += 1
                if chain is None and next_chain in inv_done and (next_chain - 2) not in post_gens:
                    si = inv_done.pop(next_chain)
                    chain = (rwkv_chain_gen(next_chain, tsets[si]), next_chain, si)
                if chain is not None:
                    try:
                        next(chain[0])
                    except StopIteration:
                        free.append(chain[2])
                        next_chain += 1
                        chain = None
                for cc in sorted(list(inv.keys())):
                    g_, si = inv[cc]
                    try:
                        next(g_)
                    except StopIteration:
                        del inv[cc]
                        inv_done[cc] = si
                for cc in sorted(list(post_gens.keys())):
                    try:
                        next(post_gens[cc])
                    except StopIteration:
                        del post_gens[cc]
                if hg_alive:
                    try:
                        next(hg)
                    except StopIteration:
                        hg_alive = False

        if self.stage < 0.3:
            return
        for blk in range(nblk):
            xsrc = I["xs"] if sample else I["xp"][blk * 512:(blk + 1) * 512, :]
            ydst = O["ys"] if sample else O["yp"][blk * 512:(blk + 1) * 512, :]
            for t in range(NT):
                S.dma("sp", xbt[t][0:R, :].ap, xsrc[t * 128:t * 128 + R, :], writes=[xbt[t]])
            S.mark(('s' if sample else 'p') + str(blk) + ':norm1')
            rmsnorm_T(self.G1, 0, hT)
            S.mark(('s' if sample else 'p') + str(blk) + ':inproj')
            if self.stage < 0.35:
                return
            def inproj_stream():
                for gi in range(0, 30, 4):
                    chunks = INPROJ_ORDER[gi:gi + 4]
                    wt = self.get_tile()
                    for j, c in enumerate(chunks):
                        if c < 12 and c // 4 == 2:
                            while active_rw:
                                yield
                        if 14 <= c < 22:
                            while active_hg:
                                yield
                        ps = self.ps()
                        for k in range(8):
                            S.mm(ps.full(128, N), wt.ap(0, 128, k * 512 + j * 128, [(1, 128)]), hT[:, k, :], start=(k == 0), stop=(k == 7))
                        post_inproj(c, ps)
                        yield

            ip_ = inproj_stream()
            ip_alive = True
            while ip_alive or active_rw or active_hg:
                for lst_ in (active_rw, active_hg):
                    for g_ in list(lst_):
                        try:
                            next(g_)
                        except StopIteration:
                            lst_.remove(g_)
                if ip_alive:
                    try:
                        next(ip_)
                    except StopIteration:
                        ip_alive = False
            if self.stage < 0.4:
                return
            S.mark(('s' if sample else 'p') + str(blk) + ':units')
            run_units()
            S.mark(('s' if sample else 'p') + str(blk) + ':outproj')
            if sample and self.debug:
                dtmp = sb('dtmp', [128, 8 * N])
                S.copy(dtmp[:, :], mixT.ap(0, 128, 0, [(1, 8 * N)]))
                self.dbg(dtmp[:, :], 8 * N)
            if self.stage < 0.6:
                return
            for hf in range(2):
                wt = self.get_tile()
                for t in range(NT):
                    ps = self.ps()
                    for k in range(8):
                        S.mm(ps.full(R, 512), mixT[:, k, t * 128:t * 128 + R], wt.ap(0, 128, k * 512, [(1, 512)]),
                             start=(k == 0), stop=(k == 7))
                    tmp = self.tmp512(t)
                    S.tt(tmp[0:R, :], ps.full(R, 512), GT1[0:R, hf * 512:(hf + 1) * 512], ALU.mult)
                    S.tt(xbt[t][0:R, hf * 512:(hf + 1) * 512], xbt[t][0:R, hf * 512:(hf + 1) * 512], tmp[0:R, :], ALU.add, eng="pool")
            if sample and self.debug:
                self.dbg(xbt[0][0:R, :], 1024, R)
            if self.stage < 0.7:
                return
            S.mark(('s' if sample else 'p') + str(blk) + ':norm2')
            rmsnorm_T(self.G2, 16, hT)
            S.mark(('s' if sample else 'p') + str(blk) + ':mlp')
            if sample and self.debug:
                S.copy(dtmp[:, :], hT.ap(0, 128, 0, [(1, 8 * N)]))
                self.dbg(dtmp[:, :], 8 * N)
            for g in range(8):
                wu = self.get_tile()
                wd = self.get_tile()
                u = uT[g % 2]
                for j in range(4):
                    ps = self.ps()
                    for k in range(8):
                        S.mm(ps.full(128, N), wu.ap(0, 128, k * 512 + j * 128, [(1, 128)]), hT[:, k, :], start=(k == 0), stop=(k == 7))
                    tmp = self.tmp512(j)
                    S.act(tmp.ap(0, 128, 0, [(1, N)]), ps.full(128, N), AF.Relu)
                    S.tt(u[:, j, :], tmp.ap(0, 128, 0, [(1, N)]), tmp.ap(0, 128, 0, [(1, N)]), ALU.mult, eng="pool")
                for t in range(NT):
                    for hf in range(2):
                        ps = self.ps()
                        for j in range(4):
                            S.mm(ps.full(R, 512), u[:, j, t * 128:t * 128 + R], wd.ap(0, 128, j * 1024 + hf * 512, [(1, 512)]),
                                 start=(j == 0), stop=(j == 3))
                        dv = dacc[0:R, t, hf * 512:(hf + 1) * 512]
                        if g == 0:
                            S.copy(dv, ps.full(R, 512), eng="act")
                        else:
                            S.tt(dv, ps.full(R, 512), dv, ALU.add)
            if sample and self.debug:
                self.dbg(dacc[0:R, 0, :], 1024, R)
                self.dbg(GT1[0:R, :], 1024, R)
                self.dbg(GT2[0:R, :], 1024, R)
            S.mark(('s' if sample else 'p') + str(blk) + ':final')
            for t in range(NT):
                for hf in range(2):
                    tmp = self.tmp512(hf)
                    S.tt(tmp[0:R, :], dacc[0:R, t, hf * 512:(hf + 1) * 512], GT2[0:R, hf * 512:(hf + 1) * 512], ALU.mult, eng="pool")
                    S.tt(xbt[t][0:R, hf * 512:(hf + 1) * 512], xbt[t][0:R, hf * 512:(hf + 1) * 512], tmp[0:R, :], ALU.add)
            for t in range(NT):
                S.act(xn[0:R, t, :], xbt[t][0:R, :], AF.Square, accum=st8[0:R, t:t + 1])
            S.ts(st8[0:R, 4:4 + NT], st8[0:R, 0:NT], 1.0 / D, ALU.mult, NORM_EPS, ALU.add)
            S.act(st8[0:R, 4:4 + NT], st8[0:R, 4:4 + NT], AF.Sqrt)
            S.recip(st8[0:R, 8:8 + NT], st8[0:R, 4:4 + NT])
            for t in range(NT):
                S.stt(xbt[t][0:R, :], xbt[t][0:R, :], st8[0:R, 8 + t:9 + t], self.normf[0:R, :], ALU.mult, ALU.mult)
                S.dma("sp", ydst[t * 128:t * 128 + R, :], xbt[t][0:R, :].ap, reads=[xbt[t]])

        if self.stage < 0.9:
            return
        if sample:
            shsb = sb("shsb", [SB, 14, 128])
            for c0 in range(0, 14, 4):
                ps = self.ps()
                n = min(4, 14 - c0)
                for c in range(c0, c0 + n):
                    S.mm(ps.ap(0, SB, (c - c0) * 128, [(1, 128)]), sprev[:, c, :], idf[:, C_ID:C_ID + 128], signal=(c == c0 + n - 1))
                S.copy(shsb[0:SB, c0:c0 + n, :], ps.ap(0, SB, 0, [(128, n), (1, 128)]), eng="act")
            S.dma("sp", O["shs"], shsb[:, :, :].ap, reads=[shsb])
            for hp in range(4):
                for b0 in range(0, SB, 8):
                    ps = self.ps()
                    for b in range(b0, b0 + 8):
                        for h in range(2):
                            S.mm(ps.ap(64 * h, 64, (b - b0) * 64, [(1, 64)]), Hst[64 * h:64 * h + 64, hp, b, :],
                                 idf[64 * h:64 * h + 64, C_IDS:C_IDS + 64], signal=(b == b0 + 7 and h == 1))
                    S.copy(Sld[:, hp, b0:b0 + 8, :], ps.ap(0, 128, 0, [(64, 8), (1, 64)]), eng="act")
                S.dma("sp", O["wkvs"][:, 2 * hp:2 * hp + 2, :, :].rearrange("b h v k -> (h v) b k"), Sld[:, hp, :, :].ap, reads=[Sld])
            for h in range(4):
                S.dma("sp", O["hgs"][:, h, :, :].rearrange("b f i -> f b i"), Shs[:, h, :, :].ap, reads=[Shs])
        else:
            shpb = sb("shpb", [14, 128])
            ps = self.ps()
            S.mm(ps.ap(0, 14, 0, [(1, 128)]), sprev[:, :, 0], idf[:, C_ID:C_ID + 128])
            S.copy(shpb[:, :], ps.ap(0, 14, 0, [(1, 128)]), eng="act")
            S.dma("sp", O["shp"], shpb[:, :].ap, reads=[shpb])
            ps = self.ps()
            for hp in range(4):
                for h in range(2):
                    S.mm(ps.ap(64 * h, 64, hp * 64, [(1, 64)]), Hst[64 * h:64 * h + 64, hp, 0, :],
                         idf[64 * h:64 * h + 64, C_IDS:C_IDS + 64], signal=(hp == 3 and h == 1))
            S.copy(Sld[:, :, 0, :], ps.ap(0, 128, 0, [(64, 4), (1, 64)]), eng="act")
            S.dma("sp", O["wkvp"].rearrange("(hp h) v k -> (h v) hp k", h=2), Sld[:, :, 0, :].ap, reads=[Sld])
            S.dma("sp", O["hgp"].rearrange("h f i -> f h i"), Shs[:, :, 0, :].ap, reads=[Shs])

    def tmp512(self, i):
        return self._tmp[i % len(self._tmp)]


_NC_CACHE = {}


def _prep_inputs(inp):
    f = lambda a: np.ascontiguousarray(np.asarray(a, dtype=np.float32))
    chunks = lambda v: f(v).reshape(-1, 128).T
    vec = np.zeros((128, NVC), np.float32)

    def put(name, v):
        c = chunks(v)
        vec[:, VC[name]:VC[name] + c.shape[1]] = c

    put("norm1", inp["norm1"][0]); put("norm2", inp["norm2"][0]); put("b_ada", inp["b_ada"][0])
    put("mu", inp["mu_shift"][0]); put("w0", inp["w0"][0]); put("a0", inp["a0"][0]); put("k_k", inp["k_k"][0])
    put("k_a", inp["k_a"][0]); put("r_k", np.asarray(inp["r_k"][0]).reshape(-1)); put("lnx_w", inp["lnx_w"][0])
    put("lnx_b", inp["lnx_b"][0]); put("lb0", inp["hgrn_lb"][0]); put("lb1", inp["hgrn_lb"][1])
    put("gnorm", inp["hgrn_gnorm"][0])
    b_ada = f(inp["b_ada"][0])
    shared = {
        "w_ada": f(inp["w_ada"][0]),
        "w_in": np.ascontiguousarray(f(inp["w_in"][0])[:, np.concatenate([np.arange(c * 128, (c + 1) * 128) for c in INPROJ_ORDER])]),
        "w_out": f(inp["w_out"][0]),
        "w_up": f(inp["w_up"][0]), "w_down": f(inp["w_down"][0]), "w_dec": f(inp["w_decay_up"][0]),
        "w_aaa": f(inp["w_aaa_up"][0]), "w_gate": f(inp["w_gate_up"][0]), "vecF": vec,
        "normf": np.ascontiguousarray(np.broadcast_to(f(inp["norm_f"])[None, :], (128, D))),
        "bgt": np.ascontiguousarray(np.concatenate([b_ada[2 * D:3 * D], b_ada[5 * D:6 * D]])[None, :]),
        "cst": make_consts(),
    }
    xp, xs = f(inp["x_prompt"]), f(inp["x_sample"])
    cp, cs_ = f(inp["c_prompt"]), f(inp["c_sample"])
    sst, swkv, shg = f(inp["state_shift"][0]), f(inp["state_wkv"][0]), f(inp["state_hgrn"][0])
    maps = []
    for c in range(NCORES):
        bs = slice(c * SB, (c + 1) * SB)
        call = np.concatenate([cp[c:c + 1], cs_[bs]], axis=0)
        cT = np.ascontiguousarray(call.reshape(17, 8, 128).transpose(2, 1, 0))
        sstc = np.ascontiguousarray(sst[bs].reshape(SB, 14, 128).transpose(2, 1, 0))
        m = dict(shared)
        m.update({"xp": xp[c], "xs": np.ascontiguousarray(xs[bs].reshape(SB * ST, D)), "cT": cT, "sst": sstc,
                  "swkv": np.ascontiguousarray(swkv[bs]), "shg": np.ascontiguousarray(shg[bs])})
        maps.append(m)
    return maps


def _run(inp, stage=99, debug=False, lite=False):
    key = (stage, debug, lite)
    if key not in _NC_CACHE:
        _NC_CACHE[key] = Builder(stage=stage, debug=debug, lite=lite).build()
    nc = _NC_CACHE[key]
    maps = _prep_inputs(inp)
    if lite:
        for m in maps:
            for n in ["w_ada", "w_in", "w_out", "w_up", "w_down"]:
                m[n] = np.zeros((8, 8), np.float32)
    res = run_bass_kernel_spmd(nc, maps, core_ids=list(range(NCORES)))
    return res.results


def kernel(**inp):
    rs = _run(inp)
    g = lambda k: [np.asarray(r[k], dtype=np.float32) for r in rs]
    y_prompt = np.stack(g("yp"), 0)
    y_sample = np.concatenate(g("ys"), 0).reshape(NCORES * SB, ST, D)
    shift_p = np.stack([a.reshape(SHW) for a in g("shp")], 0)[None]
    wkv_p = np.stack(g("wkvp"), 0)[None]
    hgrn_p = np.stack(g("hgp"), 0)[None]
    shift_s = np.concatenate([a.reshape(SB, SHW) for a in g("shs")], 0)[None]
    wkv_s = np.concatenate(g("wkvs"), 0)[None]
    hgrn_s = np.concatenate(g("hgs"), 0)[None]
    return (y_prompt, y_sample, shift_p, wkv_p, hgrn_p, shift_s, wkv_s, hgrn_s)
```
